# Optimizing a Trainium2 kernel written in Bass

```python
import math
import jax, jax.numpy as jnp
from jax import lax
import numpy as np

D_MODEL = 2048
BATCH = 4
SEQ = 4096
DEPTH = 1
DEC_BATCH = 32
DEC_SEQ = 16
PAST_LEN = 4096

CHUNK = 64
HEAD_DIM = 128
GDN_HEADS = 8
GDN_CONV = 4
SWA_HEADS = 8
SWA_KV_HEADS = 2
SWA_GROUP = SWA_HEADS // SWA_KV_HEADS
WINDOW = 128
D_FF = 5632
FFN_CONV = 3
N_MOD = 6
EPS = 1e-6

GDN_WIDTH = GDN_HEADS * HEAD_DIM
SWA_WIDTH = SWA_HEADS * HEAD_DIM
KV_WIDTH = SWA_KV_HEADS * HEAD_DIM
QKV_D_WIDTH = 3 * GDN_WIDTH
MIX_WIDTH = GDN_WIDTH + SWA_WIDTH
IN_WIDTH = QKV_D_WIDTH + GDN_WIDTH + 2 * GDN_HEADS + SWA_WIDTH + 2 * KV_WIDTH

kernel_name = 'hybrid_gdn_swa_convffn_stream_step'


def rms_norm(x, w):
    xf = x.astype(jnp.float32)
    y = xf * lax.rsqrt(jnp.mean(xf * xf, axis=-1, keepdims=True) + EPS)
    return (y * w.astype(jnp.float32)).astype(x.dtype)


def l2_norm(x):
    xf = x.astype(jnp.float32)
    return xf * lax.rsqrt(jnp.sum(xf * xf, axis=-1, keepdims=True) + EPS)


def causal_dwconv(x_ext, w):
    width = w.shape[0]
    L = x_ext.shape[1] - width + 1
    out = x_ext[:, 0:L] * w[0]
    for i in range(1, width):
        out = out + x_ext[:, i:i + L] * w[i]
    return out


def alibi_slopes():
    h = jnp.arange(1, SWA_HEADS + 1, dtype=jnp.float32)
    return (2.0 ** (-8.0 * h / SWA_HEADS)).reshape(SWA_KV_HEADS, SWA_GROUP)


def gated_delta_rule(q, k, v, g, beta, s0, chunk):
    B, L, H, DK = q.shape
    DV = v.shape[-1]
    n = L // chunk

    def to_blocks(t):
        t = t.reshape((B, n, chunk, H) + t.shape[3:])
        return jnp.moveaxis(t, 3, 1)

    q, k, v, g, beta = [to_blocks(t) for t in (q, k, v, g, beta)]
    gc = jnp.cumsum(g, axis=-1)
    idx = jnp.arange(chunk)
    incl = idx[:, None] >= idx[None, :]
    strict = idx[:, None] > idx[None, :]
    decay = jnp.exp(jnp.where(incl, gc[..., :, None] - gc[..., None, :], -jnp.inf))
    kb = k * beta[..., None]
    a = jnp.where(strict, jnp.einsum('bhnid,bhnjd->bhnij', kb, k) * decay, 0.0)
    t_mat = a + jnp.eye(chunk, dtype=a.dtype)
    rhs = jnp.concatenate([v * beta[..., None], kb * jnp.exp(gc)[..., None]], axis=-1)
    sol = lax.linalg.triangular_solve(t_mat, rhs, left_side=True, lower=True, unit_diagonal=True)
    u, w = sol[..., :DV], sol[..., DV:]
    qk = jnp.where(incl, jnp.einsum('bhnid,bhnjd->bhnij', q, k) * decay, 0.0)
    qg = q * jnp.exp(gc)[..., None]
    kd = k * jnp.exp(gc[..., -1:] - gc)[..., None]
    gl = jnp.exp(gc[..., -1])

    def step(s, xs):
        u_n, w_n, qk_n, qg_n, kd_n, gl_n = xs
        v_new = u_n - jnp.einsum('bhck,bhkv->bhcv', w_n, s)
        o_n = jnp.einsum('bhck,bhkv->bhcv', qg_n, s) + jnp.einsum('bhij,bhjv->bhiv', qk_n, v_new)
        s = s * gl_n[..., None, None] + jnp.einsum('bhck,bhcv->bhkv', kd_n, v_new)
        return s, o_n

    xs = tuple(jnp.moveaxis(t, 2, 0) for t in (u, w, qk, qg, kd, gl))
    s_fin, o = lax.scan(step, s0, xs)
    o = jnp.transpose(o, (1, 0, 3, 2, 4)).reshape(B, L, H, DV)
    return o, s_fin


def banded_sink_alibi_attention(q, k_ext, v_ext, pos0, chunk, sinks, slopes):
    B, L, HQ, D = q.shape
    n = L // chunk
    span = WINDOW + chunk
    kidx = jnp.arange(n)[:, None] * chunk + jnp.arange(span)[None, :]
    kb = jnp.take(k_ext, kidx, axis=1)
    vb = jnp.take(v_ext, kidx, axis=1)
    qb = q.reshape(B, n, chunk, SWA_KV_HEADS, SWA_GROUP, D)
    qpos = pos0 + jnp.arange(L).reshape(n, chunk)
    kpos = pos0 - WINDOW + kidx
    dist = jnp.abs(qpos[:, :, None] - kpos[:, None, :]).astype(jnp.float32)
    valid = (kpos >= 0)[:, None, :]
    s = jnp.einsum('bnqhgd,bnkhd->bnhgqk', qb, kb).astype(jnp.float32) * (D ** -0.5)
    s = s - slopes[:, :, None, None] * dist[:, None, None]
    s = jnp.where(valid[:, None, None], s, -jnp.inf)
    sink = sinks.astype(jnp.float32)[:, :, None, None]
    m = jnp.maximum(jnp.max(s, axis=-1, keepdims=True), sink)
    p = jnp.exp(s - m)
    probs = p / (jnp.sum(p, axis=-1, keepdims=True) + jnp.exp(sink - m))
    o = jnp.einsum('bnhgqk,bnkhd->bnqhgd', probs.astype(v_ext.dtype), vb)
    return o.reshape(B, L, HQ * D)


def hybrid_layer(x, c, conv_prev, s0, k_prev, v_prev, ffn_prev, pos0,
                 ada_w, ada_b, norm1_w, norm2_w, w_in, conv_qkv_w, a_log, dt_bias, gdn_norm_w,
                 q_norm_w, k_norm_w, sinks, w_o, w_up, ffn_conv_w, ffn_conv_b, w_down):
    B, L, _ = x.shape
    chunk = min(CHUNK, L)
    f32 = jnp.float32
    mod = (jax.nn.silu(c) @ ada_w + ada_b).reshape(B, N_MOD, D_MODEL)[:, :, None, :]
    shift1, scale1, gate1, shift2, scale2, gate2 = [mod[:, i] for i in range(N_MOD)]

    h = rms_norm(x, norm1_w) * (1 + scale1) + shift1
    proj = h @ w_in
    cuts = [QKV_D_WIDTH, QKV_D_WIDTH + GDN_WIDTH, QKV_D_WIDTH + GDN_WIDTH + GDN_HEADS,
            QKV_D_WIDTH + GDN_WIDTH + 2 * GDN_HEADS, QKV_D_WIDTH + GDN_WIDTH + 2 * GDN_HEADS + SWA_WIDTH,
            QKV_D_WIDTH + GDN_WIDTH + 2 * GDN_HEADS + SWA_WIDTH + KV_WIDTH]
    qkv_d, z_d, a_d, b_d, q_a, k_a, v_a = jnp.split(proj, cuts, axis=-1)

    qkv_ext = jnp.concatenate([conv_prev, qkv_d], axis=1)
    qkv_c = jax.nn.silu(causal_dwconv(qkv_ext, conv_qkv_w))
    qd, kd, vd = jnp.split(qkv_c, 3, axis=-1)
    qd = l2_norm(qd.reshape(B, L, GDN_HEADS, HEAD_DIM)) * (HEAD_DIM ** -0.5)
    kd = l2_norm(kd.reshape(B, L, GDN_HEADS, HEAD_DIM))
    vd = vd.reshape(B, L, GDN_HEADS, HEAD_DIM).astype(f32)
    g = -jnp.exp(a_log.astype(f32)) * jax.nn.softplus(a_d.astype(f32) + dt_bias.astype(f32))
    beta = jax.nn.sigmoid(b_d.astype(f32))
    o_d, s_new = gated_delta_rule(qd, kd, vd, g, beta, s0.astype(f32), chunk)
    o_d = rms_norm(o_d, gdn_norm_w) * jax.nn.silu(z_d.reshape(B, L, GDN_HEADS, HEAD_DIM).astype(f32))
    o_d = o_d.reshape(B, L, GDN_WIDTH).astype(x.dtype)

    qa = rms_norm(q_a.reshape(B, L, SWA_HEADS, HEAD_DIM), q_norm_w)
    ka = rms_norm(k_a.reshape(B, L, SWA_KV_HEADS, HEAD_DIM), k_norm_w)
    va = v_a.reshape(B, L, SWA_KV_HEADS, HEAD_DIM)
    k_ext = jnp.concatenate([k_prev, ka], axis=1)
    v_ext = jnp.concatenate([v_prev, va], axis=1)
    o_a = banded_sink_alibi_attention(qa, k_ext, v_ext, pos0, chunk,
                                      sinks.reshape(SWA_KV_HEADS, SWA_GROUP), alibi_slopes())

    x = x + gate1 * (jnp.concatenate([o_d, o_a], axis=-1) @ w_o)

    h2 = rms_norm(x, norm2_w) * (1 + scale2) + shift2
    u = h2 @ w_up
    u_ext = jnp.concatenate([ffn_prev, u], axis=1)
    u_c = causal_dwconv(u_ext, ffn_conv_w) + ffn_conv_b
    gt, up = jnp.split(u_c, 2, axis=-1)
    x = x + gate2 * ((jax.nn.silu(gt) * up) @ w_down)
    return (x, qkv_ext[:, -(GDN_CONV - 1):], s_new.astype(s0.dtype), k_ext[:, -WINDOW:],
            v_ext[:, -WINDOW:], u_ext[:, -(FFN_CONV - 1):])


def setup_inputs(seed: int = 0) -> dict:
    key = jax.random.key(seed)
    kit = iter(jax.random.split(key, 40))
    f32 = jnp.float32

    def nrm(shape, scale=1.0):
        return jax.random.normal(next(kit), shape, f32) * scale

    dt = jnp.exp(jax.random.uniform(next(kit), (DEPTH, GDN_HEADS), f32, math.log(1e-3), math.log(1e-1)))
    return {
        'x_prompt': nrm((BATCH, SEQ, D_MODEL)),
        'x_sample': nrm((DEC_BATCH, DEC_SEQ, D_MODEL)),
        'state_conv_qkv': nrm((DEPTH, DEC_BATCH, GDN_CONV - 1, QKV_D_WIDTH)),
        'state_delta': nrm((DEPTH, DEC_BATCH, GDN_HEADS, HEAD_DIM, HEAD_DIM), 0.1),
        'cache_swa_k': nrm((DEPTH, DEC_BATCH, WINDOW, SWA_KV_HEADS, HEAD_DIM)),
        'cache_swa_v': nrm((DEPTH, DEC_BATCH, WINDOW, SWA_KV_HEADS, HEAD_DIM)),
        'state_ffn_conv': nrm((DEPTH, DEC_BATCH, FFN_CONV - 1, 2 * D_FF)),
        'c_prompt': nrm((BATCH, D_MODEL)),
        'c_sample': nrm((DEC_BATCH, D_MODEL)),
        'ada_w': nrm((DEPTH, D_MODEL, N_MOD * D_MODEL), 0.5 * D_MODEL ** -0.5),
        'ada_b': nrm((DEPTH, N_MOD * D_MODEL), 0.1),
        'norm1_w': 1.0 + nrm((DEPTH, D_MODEL), 0.05),
        'norm2_w': 1.0 + nrm((DEPTH, D_MODEL), 0.05),
        'w_in': nrm((DEPTH, D_MODEL, IN_WIDTH), D_MODEL ** -0.5),
        'conv_qkv_w': nrm((DEPTH, GDN_CONV, QKV_D_WIDTH), GDN_CONV ** -0.5),
        'a_log': jnp.log(jax.random.uniform(next(kit), (DEPTH, GDN_HEADS), f32, 1.0, 16.0)),
        'dt_bias': dt + jnp.log(-jnp.expm1(-dt)),
        'gdn_norm_w': 1.0 + nrm((DEPTH, HEAD_DIM), 0.05),
        'q_norm_w': 1.0 + nrm((DEPTH, HEAD_DIM), 0.05),
        'k_norm_w': 1.0 + nrm((DEPTH, HEAD_DIM), 0.05),
        'sinks': nrm((DEPTH, SWA_HEADS), 0.5),
        'w_o': nrm((DEPTH, MIX_WIDTH, D_MODEL), MIX_WIDTH ** -0.5),
        'w_up': nrm((DEPTH, D_MODEL, 2 * D_FF), D_MODEL ** -0.5),
        'ffn_conv_w': nrm((DEPTH, FFN_CONV, 2 * D_FF), FFN_CONV ** -0.5),
        'ffn_conv_b': nrm((DEPTH, 2 * D_FF), 0.02),
        'w_down': nrm((DEPTH, D_FF, D_MODEL), D_FF ** -0.5),
    }


def reference(x_prompt, x_sample, state_conv_qkv, state_delta, cache_swa_k, cache_swa_v, state_ffn_conv,
              c_prompt, c_sample, ada_w, ada_b, norm1_w, norm2_w, w_in, conv_qkv_w, a_log, dt_bias,
              gdn_norm_w, q_norm_w, k_norm_w, sinks, w_o, w_up, ffn_conv_w, ffn_conv_b, w_down):
    xp, xs = x_prompt, x_sample
    acc_p = [[] for _ in range(5)]
    acc_s = [[] for _ in range(5)]
    for l in range(DEPTH):
        lw = (ada_w[l], ada_b[l], norm1_w[l], norm2_w[l], w_in[l], conv_qkv_w[l], a_log[l], dt_bias[l],
              gdn_norm_w[l], q_norm_w[l], k_norm_w[l], sinks[l], w_o[l], w_up[l], ffn_conv_w[l],
              ffn_conv_b[l], w_down[l])
        b, dt = xp.shape[0], xp.dtype
        zero_state = (jnp.zeros((b, GDN_CONV - 1, QKV_D_WIDTH), dt),
                      jnp.zeros((b, GDN_HEADS, HEAD_DIM, HEAD_DIM), dt),
                      jnp.zeros((b, WINDOW, SWA_KV_HEADS, HEAD_DIM), dt),
                      jnp.zeros((b, WINDOW, SWA_KV_HEADS, HEAD_DIM), dt),
                      jnp.zeros((b, FFN_CONV - 1, 2 * D_FF), dt))
        xp, *new_p = hybrid_layer(xp, c_prompt, *zero_state, 0, *lw)
        xs, *new_s = hybrid_layer(xs, c_sample, state_conv_qkv[l], state_delta[l], cache_swa_k[l],
                                  cache_swa_v[l], state_ffn_conv[l], PAST_LEN, *lw)
        for acc, t in zip(acc_p, new_p):
            acc.append(t)
        for acc, t in zip(acc_s, new_s):
            acc.append(t)
    p_conv_qkv, p_delta, p_swa_k, p_swa_v, p_ffn_conv = [jnp.stack(a) for a in acc_p]
    s_conv_qkv, s_delta, s_swa_k, s_swa_v, s_ffn_conv = [jnp.stack(a) for a in acc_s]
    return (xp, xs, p_conv_qkv, p_delta, p_swa_k, p_swa_v, p_ffn_conv,
            s_conv_qkv, s_delta, s_swa_k, s_swa_v, s_ffn_conv)
```

```python
import numpy as np
from contextlib import ExitStack
import concourse.bass as bass
import concourse.mybir as mybir
from concourse.bass_utils import run_bass_kernel_spmd

F32 = mybir.dt.float32
BF16 = mybir.dt.bfloat16
AF = mybir.ActivationFunctionType
ALU = mybir.AluOpType

D = 2048
NFC = 16
HD = 128
NH = 8
DFF = 5632
NFF = 44
INW = 5648
EPS = 1e-6
NS = 4
LS = 16
ENG_NAMES = ("pe", "act", "dve", "pool", "sp")


class Buf:
    __slots__ = ("name", "w", "r")

    def __init__(self, name=""):
        self.name = name
        self.w = None
        self.r = []


class Sched:
    def __init__(self, nc, n_dma_sems=8):
        self.nc = nc
        self.prog = {e: [] for e in ENG_NAMES}
        self.count = {e: 0 for e in ENG_NAMES}
        self.seen = {e: {} for e in ENG_NAMES}
        self.sems = {}
        self.n_dma_sems = n_dma_sems
        self.dma_rr = {"sp": 0, "pool": 0}
        self.dma_val = {}
        self.all_dma_events = []

    def _waits_for(self, e, reads, writes, is_dma=False):
        deps = []
        for b in reads:
            if b.w is not None:
                deps.append(b.w)
        for b in writes:
            if b.w is not None:
                deps.append(b.w)
            deps.extend(b.r)
        waits = {}
        for (sk, val, src) in deps:
            if src == e and e == "pe" and not is_dma:
                continue
            if self.seen[e].get(sk, 0) >= val:
                continue
            if waits.get(sk, 0) < val:
                waits[sk] = val
        for sk, val in waits.items():
            self.seen[e][sk] = val
        return list(waits.items())

    def op(self, e, fn, reads=(), writes=(), inc=True):
        waits = self._waits_for(e, reads, writes)
        if inc:
            self.count[e] += 1
            ev = (("eng", e), self.count[e], e)
        else:
            ev = (("eng", e), self.count[e] + 1, e)
        self.prog[e].append((waits, fn, inc))
        for b in reads:
            b.r.append(ev)
        for b in writes:
            b.w = ev
            b.r = []
        return ev

    def dma(self, q, fn, reads=(), writes=()):
        i = self.dma_rr[q]
        self.dma_rr[q] = (i + 1) % self.n_dma_sems
        sk = ("dma", q, i)
        prev = self.dma_val.get(sk, 0)
        waits = dict(self._waits_for(q, reads, writes, is_dma=True))
        if prev > 0 and self.seen[q].get(sk, 0) < prev:
            waits[sk] = prev
            self.seen[q][sk] = prev
        val = prev + 16
        self.dma_val[sk] = val
        ev = (sk, val, "dma_" + q)
        self.prog[q].append((list(waits.items()), fn, ("dma", sk)))
        for b in reads:
            b.r.append(ev)
        for b in writes:
            b.w = ev
            b.r = []
        return ev

    def barrier(self):
        evs = [(("eng", e), self.count[e]) for e in ENG_NAMES if self.count[e] > 0]
        evs += [(sk, v) for sk, v in self.dma_val.items()]
        for e in ENG_NAMES:
            waits = []
            for sk, val in evs:
                if sk == ("eng", e) and e == "pe":
                    continue
                if self.seen[e].get(sk, 0) < val:
                    self.seen[e][sk] = val
                    waits.append((sk, val))
            if waits:
                self.prog[e].append((waits, None, False))

    def finish(self, e="sp"):
        waits = [(sk, v) for sk, v in self.dma_val.items()]
        waits += [(("eng", x), self.count[x]) for x in ENG_NAMES if self.count[x] > 0 and x != e]
        self.prog[e].append((waits, None, False))

    def emit(self):
        nc = self.nc
        with ExitStack() as es:
            for e in ENG_NAMES:
                self.sems[("eng", e)] = es.enter_context(nc.semaphore("s_" + e))
            for q in ("sp", "pool"):
                for i in range(self.n_dma_sems):
                    self.sems[("dma", q, i)] = es.enter_context(nc.semaphore("d_%s%d" % (q, i)))
            block = es.enter_context(nc.Block())
            sems = self.sems
            prog = self.prog

            def run(ename):
                def body(engine):
                    for (waits, fn, inc) in prog[ename]:
                        for sk, val in waits:
                            engine.wait_ge(sems[sk], val)
                        if fn is None:
                            continue
                        ins = fn(engine)
                        if inc is True:
                            ins.then_inc(sems[("eng", ename)], 1)
                        elif isinstance(inc, tuple):
                            ins.then_inc(sems[inc[1]], 16)
                return body

            block.tensor(run("pe"))
            block.scalar(run("act"))
            block.vector(run("dve"))
            block.gpsimd(run("pool"))
            block.sync(run("sp"))


class Tn:
    def __init__(self, ap, name=""):
        self.ap = ap
        self.b = Buf(name)

    def __getitem__(self, k):
        return self.ap[k]


class Builder:
    def __init__(self, NP, NM, TS):
        self.NP, self.NM, self.TS = NP, NM, TS
        self.nc = bass.Bass("TRN2", target_bir_lowering=False)
        self.S = Sched(self.nc)
        self.es = ExitStack()
        self.dram = {}
        self.rr = 0

    def din(self, name, shape):
        self.dram[name] = self.nc.dram_tensor(name, list(shape), F32, kind="ExternalInput").ap()
        return self.dram[name]

    def dout(self, name, shape):
        self.dram[name] = self.nc.dram_tensor(name, list(shape), F32, kind="ExternalOutput").ap()
        return self.dram[name]

    def sb(self, name, shape, dt=F32):
        t = self.es.enter_context(self.nc.sbuf_tensor("sb_" + name, list(shape), dt))
        return Tn(t[:], name)

    def ps(self, name, shape, dt=F32):
        t = self.es.enter_context(self.nc.psum_tensor("ps_" + name, list(shape), dt))
        return Tn(t[:], name)

    def op(self, e, fn, r=(), w=(), inc=True):
        if e == "pool":
            e = "dve"
        return self.S.op(e, fn, reads=[t.b for t in r], writes=[t.b for t in w], inc=inc)

    def dma(self, q, out, in_, r=(), w=()):
        return self.S.dma(q, lambda e: e.dma_start(out=out, in_=in_), reads=[t.b for t in r], writes=[t.b for t in w])

    def mm(self, out, lhsT, rhs, start=True, stop=True, r=(), w=(), inc=True):
        return self.op("pe", lambda e: e.matmul(out, lhsT, rhs, start=start, stop=stop), r, w, inc)

    def tr(self, out, in_, ident, r=(), w=(), inc=True):
        return self.op("pe", lambda e: e.transpose(out, in_, ident), r, w, inc)

    def act(self, out, in_, func, r=(), w=(), scale=1.0, bias=0.0, accum_out=None, eng="act"):
        kw = {}
        if accum_out is not None:
            kw["accum_out"] = accum_out
        return self.op("act", lambda e: e.activation(out=out, in_=in_, func=func, scale=scale, bias=bias, **kw), r, w)

    def ew_engine(self):
        self.rr += 1
        return "dve" if self.rr % 3 else "pool"

    def tt(self, out, a, b, op, r=(), w=(), eng="dve"):
        return self.op(eng, lambda e: e.tensor_tensor(out=out, in0=a, in1=b, op=op), r, w)

    def ts(self, out, a, s1, op0, s2=None, op1=None, r=(), w=(), eng="dve"):
        if op1 is None:
            return self.op(eng, lambda e: e.tensor_scalar(out=out, in0=a, scalar1=s1, scalar2=None, op0=op0), r, w)
        return self.op(eng, lambda e: e.tensor_scalar(out=out, in0=a, scalar1=s1, scalar2=s2, op0=op0, op1=op1), r, w)

    def stt(self, out, a, s, b, op0, op1, r=(), w=()):
        return self.op("dve", lambda e: e.scalar_tensor_tensor(out=out, in0=a, scalar=s, in1=b, op0=op0, op1=op1), r, w)

    def cp(self, out, in_, r=(), w=(), eng="dve"):
        if eng == "act":
            return self.op("act", lambda e: e.copy(out=out, in_=in_), r, w)
        return self.op(eng, lambda e: e.tensor_copy(out=out, in_=in_), r, w)

    def recip(self, out, in_, r=(), w=()):
        return self.op("dve", lambda e: e.reciprocal(out=out, in_=in_), r, w)

    def memset(self, t, ap, val, eng="dve"):
        return self.op(eng, lambda e: e.memset(ap, val), (), [t])

    def rsqrt(self, tn, ap, mul):
        self.ts(ap, ap, mul, ALU.mult, EPS, ALU.add, r=[tn], w=[tn])
        self.act(ap, ap, AF.Sqrt, r=[tn], w=[tn])
        self.recip(ap, ap, r=[tn], w=[tn])

    def build(self):
        nc, S = self.nc, self.S
        NP, NM, TS = self.NP, self.NM, self.TS
        T = TS * 128
        TE = T + 128
        din, dout, sb, ps = self.din, self.dout, self.sb, self.ps

        xm = din("xm", [NM * 128, D])
        xp = din("xp", [max(NP, 1) * 128, D])
        xs = din("xs", [NS * LS, D])
        flag_d = din("flag", [128, 1])
        cT_d = din("cT", [128, NFC, 5])
        ada_w = din("ada_w", [D, 6 * D])
        ada_b = din("ada_b", [128, 96])
        n1w_d = din("norm1_w", [128, NFC])
        n2w_d = din("norm2_w", [128, NFC])
        w_in = din("w_in", [D, INW])
        cw_d = din("conv_w", [128, 24, 4])
        alog_d = din("a_log", [8])
        dtb_d = din("dt_bias", [8])
        gnw_d = din("gdn_norm_w", [128, 1])
        qnw_d = din("q_norm_w", [128, 1])
        knw_d = din("k_norm_w", [128, 1])
        sinks_d = din("sinks", [8])
        w_o = din("w_o", [D, D])
        w_up = din("w_up", [D, 2 * DFF])
        fcw_d = din("ffn_conv_w", [128, 2 * NFF, 3])
        fcb_d = din("ffn_conv_b", [128, 2 * NFF])
        w_down = din("w_down", [DFF, D])
        scr = lambda n, sh: nc.dram_tensor(n, list(sh), BF16, kind="Internal").ap()
        w_in_b, w_o_b = scr("w_in_b", [D, INW]), scr("w_o_b", [D, D])
        w_up_b, w_down_b = scr("w_up_b", [D, 2 * DFF]), scr("w_down_b", [DFF, D])
        s_conv_d = din("s_conv", [NS, 128, 24, 3])
        s_delta_d = din("s_delta", [NS, NH, 128, 128])
        s_kT_d = din("s_kT", [NS, 128, 2, 128])
        s_v_d = din("s_v", [NS, 128, 2, 128])
        s_ffn_d = din("s_ffn", [NS, 128, 2 * NFF, 2])
        ident_d = din("ident", [128, 128])
        U_d = din("U", [128, 128])
        Lst_d = din("Lst", [128, 128])
        Linc_d = din("Linc", [128, 128])
        biasT_d = din("biasT", [64, 2, 3, 4, 64])
        maskc_d = din("maskc", [64, 2 * NM * 3])

        ym = dout("ym", [NM * 128, D])
        ys = dout("ys", [NS * LS, D])
        o_conv = dout("o_conv", [128, 24, 3])
        o_delta = dout("o_delta", [NH, 128, 128])
        o_kT = dout("o_kT", [128, 2, 128])
        o_v = dout("o_v", [128, 2, 128])
        o_ffn = dout("o_ffn", [128, 2 * NFF, 2])
        os_conv = dout("os_conv", [NS, 128, 24, 3])
        os_delta = dout("os_delta", [NS, NH, 128, 128])
        os_kT = dout("os_kT", [NS, 128, 2, 128])
        os_v = dout("os_v", [NS, 128, 2, 128])
        os_ffn = dout("os_ffn", [NS, 128, 2 * NFF, 2])

        ident = sb("ident", [128, 128]); identb = sb("identb", [128, 128], BF16)
        U = sb("U", [128, 128]); Lst = sb("Lst", [128, 128]); Linc = sb("Linc", [128, 128])
        onesb = sb("onesb", [128, 128], BF16); onesf = sb("onesf", [128, 128])
        biasT = sb("biasT", [64, 2, 3, 4, 64]); maskc = sb("maskc", [64, 2 * NM * 3])
        flag = sb("flag", [128, 1])
        cT = sb("cT", [128, NFC, 5]); scT = sb("scT", [128, NFC, 5])
        adab = sb("adab", [128, 96]); n1w = sb("n1w", [128, NFC]); n2w = sb("n2w", [128, NFC])
        cw = sb("cw", [128, 24, 4]); fcw = sb("fcw", [128, 2 * NFF, 3]); fcb = sb("fcb", [128, 2 * NFF])
        alog = sb("alog", [128, 8]); dtb = sb("dtb", [128, 8]); sinks = sb("sinks", [128, 8])
        gnw = sb("gnw", [128, 1]); qnw = sb("qnw", [128, 1]); knw = sb("knw", [128, 1])
        modT = sb("modT", [128, 96, 5])
        A1 = sb("A1", [128, NFC, 5]); A2 = sb("A2", [128, NFC, 5])
        A1f = sb("A1f", [128, NFC]); B1f = sb("B1f", [128, NFC]); A2f = sb("A2f", [128, NFC]); B2f = sb("B2f", [128, NFC])
        esink = sb("esink", [128, 8])
        negea = sb("negea", [128, 8])

        ld = [(ident, ident_d), (U, U_d), (Lst, Lst_d), (Linc, Linc_d), (biasT, biasT_d), (maskc, maskc_d),
              (flag, flag_d), (cT, cT_d), (adab, ada_b), (n1w, n1w_d), (n2w, n2w_d), (cw, cw_d), (fcw, fcw_d),
              (fcb, fcb_d), (gnw, gnw_d), (qnw, qnw_d), (knw, knw_d)]
        for t, d_ in ld:
            self.dma("sp", t.ap[:], d_, w=[t])
        for t, d_ in [(alog, alog_d), (dtb, dtb_d), (sinks, sinks_d)]:
            self.dma("sp", t.ap[:], d_.partition_broadcast(128), w=[t])
        self.cp(identb.ap[:], ident.ap[:], r=[ident], w=[identb])
        self.memset(onesb, onesb.ap[:], 1.0)
        self.memset(onesf, onesf.ap[:], 1.0)
        self.act(scT.ap[:], cT.ap[:], AF.Silu, r=[cT], w=[scT])
        self.act(esink.ap[:], sinks.ap[:], AF.Exp, r=[sinks], w=[esink])
        self.act(negea.ap[:], alog.ap[:], AF.Exp, r=[alog], w=[negea])
        self.ts(negea.ap[:], negea.ap[:], -1.0, ALU.mult, r=[negea], w=[negea])
        self.ts(qnw.ap[:], qnw.ap[:], HD ** -0.5, ALU.mult, r=[qnw], w=[qnw])

        psA = [ps("psA%d" % i, [128, 512]) for i in range(2)]
        psT = ps("psT", [128, 512])
        psG = ps("psG", [128, 1024])
        psH = ps("psH", [128, 1024])
        psGb = Tn(psG.ap[:].bitcast(BF16), "psGb"); psGb.b = psG.b
        psG1 = Tn(psG.ap, "psG1")
        psHb = Tn(psH.ap[:].bitcast(BF16), "psHb"); psHb.b = psH.b
        psS = ps("psS", [128, 512])
        psAi = [0]

        def next_psA():
            psAi[0] ^= 1
            return psA[psAi[0]]

        WSLOT = 4096
        NWS = 4
        wslots = [sb("wslot%d" % i, [128, WSLOT], BF16) for i in range(NWS)]
        wsi = [0]

        def wload(src_ap):
            wsi[0] = (wsi[0] + 1) % NWS
            slot = wslots[wsi[0]]
            n = src_ap.shape[1] * src_ap.shape[2]
            assert n <= WSLOT
            view = slot.ap[:, 0:n].rearrange("p (a b) -> p a b", a=src_ap.shape[1])
            q = "sp" if src_ap.dtype == BF16 else "pool"
            self.S.dma(q, lambda e: e.dma_start(out=view, in_=src_ap), reads=[], writes=[slot.b])
            return slot, view

        conv_done = {}

        def convert(name, src, dst):
            R, Cc = src.shape
            for r0 in range(0, R, 128):
                for c0 in range(0, Cc, 2048):
                    c1 = min(Cc, c0 + 2048)
                    self.S.dma("pool", lambda e, r0=r0, c0=c0, c1=c1: e.dma_start(out=dst[r0:r0 + 128, c0:c1], in_=src[r0:r0 + 128, c0:c1]))
            conv_done[name] = {sk: v for sk, v in self.S.dma_val.items() if sk[1] == "pool"}

        def need_conv(name):
            waits = []
            for sk, v in conv_done[name].items():
                if self.S.seen["sp"].get(sk, 0) < v:
                    self.S.seen["sp"][sk] = v
                    waits.append((sk, v))
            if waits:
                self.S.prog["sp"].append((waits, None, False))

        def wload_k(src2d, kps):
            nk = src2d.shape[0] // 128
            parts = []
            for k0 in range(0, nk, kps):
                k1 = min(nk, k0 + kps)
                slot, view = wload(src2d[k0 * 128:k1 * 128, :].rearrange("(kc p) n -> p kc n", p=128))
                parts.append((k0, k1, slot, view))

            def wk(k):
                for (k0, k1, slot, view) in parts:
                    if k0 <= k < k1:
                        return slot, view[:, k - k0, :]
                raise IndexError(k)
            return wk

        hT = sb("hT", [128, NFC, T], BF16)
        mixT = sb("mixT", [128, NFC, T], BF16)
        xres = sb("xres", [128, TS, D])
        stat = sb("stat", [128, 8])
        qn = sb("qn", [128, NH, T], BF16); kn = sb("kn", [128, NH, T], BF16); vT = sb("vT", [128, NH, T], BF16)
        zs = sb("zs", [128, NH, T], BF16)
        qa = sb("qa", [128, NH, T], BF16)
        kaf = sb("kaf", [128, 2, TE]); kab = sb("kab", [128, 2, TE], BF16)
        vaf = sb("vaf", [128, 2, TE])
        vtok = sb("vtok", [64, TE // 64, 2, 128], BF16)
        vtokf = sb("vtokf", [128, 2, 128])
        cacc = sb("cacc", [128, T])
        sqb = sb("sqb", [128, T], BF16)
        rsd = sb("rsd", [128, T])
        gtok = sb("gtok", [128, 2 * TS + NS, 16])
        convprev = sb("convprev", [128, 24, 3])
        uprev = sb("uprev", [128, 2 * NFF, 2])
        Sst = sb("Sst", [128, NH, 128]); Sbf = sb("Sbf", [128, NH, 128], BF16)
        f1 = sb("f1", [128, 512]); f2 = sb("f2", [128, 512]); f3 = sb("f3", [128, 512])
        Dm = sb("Dm", [128, 512])
        xsb = sb("xsb", [128, D], BF16)
        psTb = Tn(psT.ap.bitcast(BF16), "psTb"); psTb.b = psT.b
        b_ = {n: sb(n, [128, 1024], BF16) for n in ["N0", "N1", "M0", "M1", "Pm", "vb", "kbg", "kd", "vnew"]}
        for n in ["QKD", "QKDT", "qg"]:
            b_[n] = sb(n, [128, 512], BF16)
        b_["Xb"] = sb("Xb", [128, 512], BF16)
        nf = {n: Tn(b_[n].ap.bitcast(F32), n + "_f") for n in ["N0", "N1", "M0", "M1", "Pm"]}
        for n in nf:
            nf[n].b = b_[n].b
        b_["ktok"] = b_["vnew"]
        b_["nwT"] = b_["QKD"]
        sm = sb("sm", [128, 64])
        def actT_c(c):
            tn = (qn, kn, vT)[c // 8]
            return tn, tn.ap[:, c % 8, :]
        ug, uu = f3, Dm
        fa, fb = cacc, rsd
        evac = f1
        sc1 = sb("sc1", [64, 512]); sc2 = sb("sc2", [64, 256])
        pT1 = sb("pT1", [64, 512], BF16); pT2 = sb("pT2", [64, 256], BF16)
        den = f2
        dens = sb("dens", [128, 256])
        mixTg = Tn(mixT.ap, "mixTg"); mixTs = Tn(mixT.ap, "mixTs")

        self.memset(convprev, convprev.ap[:], 0.0)
        self.memset(uprev, uprev.ap[:], 0.0)
        self.memset(Sst, Sst.ap[:], 0.0)
        self.memset(Sbf, Sbf.ap[:], 0.0)
        self.memset(kaf, kaf.ap[:], 0.0)
        self.memset(vaf, vaf.ap[:], 0.0)

        convert("w_in", w_in, w_in_b)
        convert("w_o", w_o, w_o_b)
        convert("w_up", w_up, w_up_b)
        convert("w_down", w_down, w_down_b)

        def wload_f32(src_ap):
            wsi[0] = (wsi[0] + 1) % NWS
            slot = wslots[wsi[0]]
            n = src_ap.shape[1] * src_ap.shape[2]
            assert 2 * n <= WSLOT
            view = slot.ap.bitcast(F32)[:, 0:n].rearrange("p (a b) -> p a b", a=src_ap.shape[1])
            self.S.dma("sp", lambda e: e.dma_start(out=view, in_=src_ap), reads=[], writes=[slot.b])
            return slot, view
        for nb in range(48):
            parts = [wload_f32(ada_w[kh * 1024:(kh + 1) * 1024, nb * 256:(nb + 1) * 256].rearrange("(kc p) n -> p kc n", p=128))
                     for kh in range(2)]
            pt = psS
            for m in range(2):
                for kc in range(NFC):
                    slot, view = parts[kc // 8]
                    self.mm(pt.ap[:, m * 8:m * 8 + 5], view[:, kc % 8, m * 128:(m + 1) * 128], scT.ap[:, kc, :],
                            start=(kc == 0), stop=(kc == NFC - 1), r=[slot, scT], w=[pt], inc=(kc == NFC - 1))
            for m in range(2):
                ch = nb * 2 + m
                self.ts(modT.ap[:, ch, :], pt.ap[:, m * 8:m * 8 + 5], adab.ap[:, ch:ch + 1], ALU.add, r=[pt, adab], w=[modT])
        self.stt(A1.ap[:], modT.ap[:, 16:32, :], 1.0, n1w.ap[:].unsqueeze(2).to_broadcast([128, NFC, 5]), ALU.add, ALU.mult,
                 r=[modT, n1w], w=[A1])
        self.stt(A2.ap[:], modT.ap[:, 64:80, :], 1.0, n2w.ap[:].unsqueeze(2).to_broadcast([128, NFC, 5]), ALU.add, ALU.mult,
                 r=[modT, n2w], w=[A2])
        B1 = lambda fc, s: modT.ap[:, fc, s:s + 1]
        B2 = lambda fc, s: modT.ap[:, 48 + fc, s:s + 1]
        G1 = lambda fc, s: modT.ap[:, 32 + fc, s:s + 1]
        G2 = lambda fc, s: modT.ap[:, 80 + fc, s:s + 1]
        self.ts(A1f.ap[:], A1.ap[:, :, 0], flag.ap[:, 0:1], ALU.mult, r=[A1, flag], w=[A1f])
        self.ts(B1f.ap[:], modT.ap[:, 0:16, 0], flag.ap[:, 0:1], ALU.mult, r=[modT, flag], w=[B1f])
        self.ts(A2f.ap[:], A2.ap[:, :, 0], flag.ap[:, 0:1], ALU.mult, r=[A2, flag], w=[A2f])
        self.ts(B2f.ap[:], modT.ap[:, 48:64, 0], flag.ap[:, 0:1], ALU.mult, r=[modT, flag], w=[B2f])

        def norm_transpose(xt_tn, xt_ap, nrows, col0, groups, which):
            self.act(xsb.ap[0:nrows, :], xt_ap, AF.Square, r=[xt_tn], w=[xsb, stat], accum_out=stat.ap[0:nrows, 0:1])
            self.rsqrt(stat, stat.ap[0:nrows, 0:1], 1.0 / D)
            self.ts(xsb.ap[0:nrows, :], xt_ap, stat.ap[0:nrows, 0:1], ALU.mult, r=[xt_tn, stat], w=[xsb])
            for g4 in range(4):
                for q in range(4):
                    fc = g4 * 4 + q
                    self.tr(psTb.ap[:, q * 128:q * 128 + nrows], xsb.ap[0:nrows, fc * 128:(fc + 1) * 128], identb.ap[0:nrows, 0:nrows],
                            r=[xsb, identb], w=[psTb])
                for q in range(4):
                    fc = g4 * 4 + q
                    for (c0, n, s, fl) in groups:
                        if which == 1:
                            sc = A1f.ap[:, fc:fc + 1] if fl else A1.ap[:, fc, s:s + 1]
                            bi = B1f.ap[:, fc:fc + 1] if fl else B1(fc, s)
                            rd = [psTb, A1f, B1f, A1, modT]
                        else:
                            sc = A2f.ap[:, fc:fc + 1] if fl else A2.ap[:, fc, s:s + 1]
                            bi = B2f.ap[:, fc:fc + 1] if fl else B2(fc, s)
                            rd = [psTb, A2f, B2f, A2, modT]
                        self.act(hT.ap[:, fc, col0 + c0:col0 + c0 + n], psTb.ap[:, q * 128 + c0:q * 128 + c0 + n], AF.Identity,
                                 r=rd, w=[hT], scale=sc, bias=bi)

        def proj_chunk(wk, mcol, ncols, Tn_, nk, rhs_of, rd=None):
            pt = next_psA()
            rd = [hT] if rd is None else rd
            for k in range(nk):
                slot, wap = wk(k)
                self.mm(pt.ap[0:ncols, 0:Tn_], wap[:, mcol:mcol + ncols], rhs_of(k), start=(k == 0), stop=(k == nk - 1),
                        r=[slot] + rd, w=[pt], inc=(k == nk - 1))
            return pt

        par = [0]
        PB = lambda: [(f3, cacc, rsd, sqb, psT), (Dm, f1, f2, b_["qg"], psS)][par[0] % 2]

        def head_rstd(src_ap, src_tn, Tn_, mul, bufs):
            _, _, rs, sq, pb = bufs
            self.act(sq.ap[:, 0:Tn_], src_ap, AF.Square, r=[src_tn], w=[sq])
            self.mm(pb.ap[:, 0:Tn_], onesb.ap[:], sq.ap[:, 0:Tn_], r=[onesb, sq], w=[pb])
            self.ts(rs.ap[:, 0:Tn_], pb.ap[:, 0:Tn_], mul, ALU.mult, EPS, ALU.add, r=[pb], w=[rs])
            self.act(rs.ap[:, 0:Tn_], rs.ap[:, 0:Tn_], AF.Sqrt, r=[rs], w=[rs])
            self.recip(rs.ap[:, 0:Tn_], rs.ap[:, 0:Tn_], r=[rs], w=[rs])

        def conv_silu(pt, ch, Tn_, segs, dst_ap, dst_tn, norm):
            par[0] += 1
            bufs = PB()
            ext, acc, rs = bufs[0], bufs[1], bufs[2]
            for (c0, n, prv, sav) in segs:
                self.cp(ext.ap[:, 0:3], prv[0], r=[prv[1]], w=[ext])
                self.cp(ext.ap[:, 3:3 + n], pt.ap[:, c0:c0 + n], r=[pt], w=[ext], eng="act")
                self.cp(sav[0], ext.ap[:, n:n + 3], r=[ext], w=[sav[1]])
                self.ts(acc.ap[:, c0:c0 + n], ext.ap[:, 0:n], cw.ap[:, ch, 0:1], ALU.mult, r=[ext, cw], w=[acc])
                for tp in range(1, 4):
                    self.stt(acc.ap[:, c0:c0 + n], ext.ap[:, tp:tp + n], cw.ap[:, ch, tp:tp + 1], acc.ap[:, c0:c0 + n],
                             ALU.mult, ALU.add, r=[ext, cw, acc], w=[acc])
            self.act(acc.ap[:, 0:Tn_], acc.ap[:, 0:Tn_], AF.Silu, r=[acc], w=[acc])
            if norm is None:
                self.cp(dst_ap, acc.ap[:, 0:Tn_], r=[acc], w=[dst_tn], eng="act")
            else:
                head_rstd(acc.ap[:, 0:Tn_], acc, Tn_, 1.0, bufs)
                self.stt(dst_ap, acc.ap[:, 0:Tn_], norm, rs.ap[:, 0:Tn_], ALU.mult, ALU.mult, r=[acc, rs], w=[dst_tn])

        def rms_head(pt, Tn_, wcol, dst_ap, dst_tn):
            par[0] += 1
            bufs = PB()
            acc, rs = bufs[1], bufs[2]
            self.cp(acc.ap[:, 0:Tn_], pt.ap[:, 0:Tn_], r=[pt], w=[acc], eng="act")
            head_rstd(acc.ap[:, 0:Tn_], acc, Tn_, 1.0 / HD, bufs)
            self.stt(dst_ap, acc.ap[:, 0:Tn_], wcol, rs.ap[:, 0:Tn_], ALU.mult, ALU.mult, r=[acc, rs, qnw, knw], w=[dst_tn])

        def gdn_chunk(C, cols, gcol, want_out):
            c0 = cols
            g = gtok.ap[0:C, gcol, 0:8]
            beta = gtok.ap[0:C, gcol, 8:16]
            HC = NH * C
            v3 = lambda t, n=C: t.ap[0:n, 0:NH * 128].rearrange("p (h d) -> p h d", h=NH)
            c3 = lambda t, n=C: t.ap[0:n, 0:HC].rearrange("p (h c) -> p h c", h=NH)
            d3 = lambda t: t.ap[:, 0:HC].rearrange("p (h c) -> p h c", h=NH)
            banks = [(h0, min(h0 + max(1, 512 // C), NH)) for h0 in range(0, NH, max(1, 512 // C))] if C * NH > 512 else [(0, NH)]
            self.mm(psS.ap[0:C, 0:8], U.ap[0:C, 0:C], g, r=[U, gtok], w=[psS])
            self.mm(psS.ap[:, 8:16], onesf.ap[0:C, :], g, r=[onesf, gtok], w=[psS])
            self.cp(sm.ap[0:C, 0:8], psS.ap[0:C, 0:8], r=[psS], w=[sm])
            self.act(sm.ap[0:C, 8:16], psS.ap[0:C, 0:8], AF.Exp, r=[psS], w=[sm])
            self.tt(sm.ap[0:C, 16:24], psS.ap[0:C, 8:16], sm.ap[0:C, 0:8], ALU.subtract, r=[psS, sm], w=[sm])
            self.act(sm.ap[0:C, 16:24], sm.ap[0:C, 16:24], AF.Exp, r=[sm], w=[sm])
            self.tt(sm.ap[0:C, 24:32], beta, sm.ap[0:C, 8:16], ALU.mult, r=[gtok, sm], w=[sm])
            self.act(sm.ap[:, 32:40], psS.ap[:, 8:16], AF.Exp, r=[psS], w=[sm])
            self.ts(sm.ap[0:C, 40:48], beta, -1.0, ALU.mult, r=[gtok], w=[sm])
            yield
            self.tt(c3(f1), g.unsqueeze(2).to_broadcast([C, NH, C]), Lst.ap[0:C, 0:C].unsqueeze(1).to_broadcast([C, NH, C]),
                    ALU.mult, r=[gtok, Lst], w=[f1])
            for (h0, h1) in banks:
                self.mm(psG.ap[0:C, h0 * C:h1 * C], U.ap[0:C, 0:C], f1.ap[0:C, h0 * C:h1 * C], r=[U, f1], w=[psG])
            self.act(Dm.ap[0:C, 0:HC], psG.ap[0:C, 0:HC], AF.Exp, r=[psG], w=[Dm])
            yield
            for h in range(NH):
                self.tr(psHb.ap[0:C, h * 128:(h + 1) * 128], kn.ap[:, h, c0:c0 + C], identb.ap[:], r=[kn, identb], w=[psHb])
            self.cp(v3(b_["ktok"]), psHb.ap[0:C, 0:1024].rearrange("p (h d) -> p h d", h=NH), r=[psHb], w=[b_["ktok"]])
            for h in range(NH):
                self.tr(psHb.ap[0:C, h * 128:(h + 1) * 128], vT.ap[:, h, c0:c0 + C], identb.ap[:], r=[vT, identb], w=[psHb])
            self.tt(v3(b_["vb"]), psHb.ap[0:C, 0:1024].rearrange("p (h d) -> p h d", h=NH),
                    beta.unsqueeze(2).to_broadcast([C, NH, 128]), ALU.mult, r=[psHb, gtok], w=[b_["vb"]])
            self.tt(v3(b_["kbg"]), v3(b_["ktok"]), sm.ap[0:C, 24:32].unsqueeze(2).to_broadcast([C, NH, 128]), ALU.mult,
                    r=[b_["ktok"], sm], w=[b_["kbg"]])
            self.tt(v3(b_["kd"]), v3(b_["ktok"]), sm.ap[0:C, 16:24].unsqueeze(2).to_broadcast([C, NH, 128]), ALU.mult,
                    r=[b_["ktok"], sm], w=[b_["kd"]], eng="pool")
            yield
            for h in range(NH):
                self.mm(psG.ap[0:C, h * C:(h + 1) * C], kn.ap[:, h, c0:c0 + C], kn.ap[:, h, c0:c0 + C], r=[kn], w=[psG])
            self.tt(f2.ap[0:C, 0:HC], psG.ap[0:C, 0:HC], Dm.ap[0:C, 0:HC], ALU.mult, r=[psG, Dm], w=[f2])
            self.tt(c3(f3), sm.ap[0:C, 40:48].unsqueeze(2).to_broadcast([C, NH, C]),
                    Lst.ap[0:C, 0:C].unsqueeze(1).to_broadcast([C, NH, C]), ALU.mult, r=[sm, Lst], w=[f3], eng="pool")
            self.tt(nf["N0"].ap[0:C, 0:HC], f2.ap[0:C, 0:HC], f3.ap[0:C, 0:HC], ALU.mult, r=[f2, f3], w=[nf["N0"]])
            if want_out:
                for h in range(NH):
                    self.mm(psG.ap[0:C, h * C:(h + 1) * C], qn.ap[:, h, c0:c0 + C], kn.ap[:, h, c0:c0 + C], r=[qn, kn], w=[psG])
                self.tt(f2.ap[0:C, 0:HC], psG.ap[0:C, 0:HC], Dm.ap[0:C, 0:HC], ALU.mult, r=[psG, Dm], w=[f2])
                self.tt(c3(b_["QKD"]), c3(f2), Linc.ap[0:C, 0:C].unsqueeze(1).to_broadcast([C, NH, C]), ALU.mult,
                        r=[f2, Linc], w=[b_["QKD"]])
                for h in range(NH):
                    self.tr(psHb.ap[0:C, h * C:(h + 1) * C], b_["QKD"].ap[0:C, h * C:(h + 1) * C], identb.ap[0:C, 0:C],
                            r=[b_["QKD"], identb], w=[psHb])
                self.cp(b_["QKDT"].ap[0:C, 0:HC], psHb.ap[0:C, 0:HC], r=[psHb], w=[b_["QKDT"]], eng="act")
                for h in range(NH):
                    self.mm(psH.ap[:, h * C:(h + 1) * C] if C * NH <= 1024 else None, g[:, h:h + 1].to_broadcast([C, 128]),
                            U.ap[0:C, 0:C], r=[gtok, U], w=[psH])
                self.act(f2.ap[:, 0:HC], psH.ap[:, 0:HC], AF.Exp, r=[psH], w=[f2])
                self.tt(d3(b_["qg"]), qn.ap[:, :, c0:c0 + C], d3(f2), ALU.mult, r=[qn, f2], w=[b_["qg"]])
            yield
            for h in range(NH):
                self.tr(psH.ap[0:C, h * C:(h + 1) * C], nf["N0"].ap[0:C, h * C:(h + 1) * C], ident.ap[0:C, 0:C],
                        r=[nf["N0"], ident], w=[psH])
            self.cp(nf["M0"].ap[0:C, 0:HC], psH.ap[0:C, 0:HC], r=[psH], w=[nf["M0"]], eng="act")
            yield
            c3f = lambda t: t.ap[0:C, 0:HC].rearrange("p (h c) -> p h c", h=NH)
            self.tt(c3f(nf["Pm"]), c3f(nf["M0"]), ident.ap[0:C, 0:C].unsqueeze(1).to_broadcast([C, NH, C]), ALU.add,
                    r=[nf["M0"], ident], w=[nf["Pm"]])
            nlev = int(np.log2(C)) - 1
            Nc, Mc, Nn, Mn = nf["N0"], nf["M0"], nf["N1"], nf["M1"]
            for lev in range(nlev):
                for h in range(NH):
                    sl = slice(h * C, (h + 1) * C)
                    self.mm(psG.ap[0:C, sl], Mc.ap[0:C, sl], Nc.ap[0:C, sl], r=[Mc, Nc], w=[psG])
                self.cp(Nn.ap[0:C, 0:HC], psG.ap[0:C, 0:HC], r=[psG], w=[Nn], eng="act")
                yield
                if lev < nlev - 1:
                    for h in range(NH):
                        sl = slice(h * C, (h + 1) * C)
                        self.mm(psH.ap[0:C, sl], Nc.ap[0:C, sl], Mc.ap[0:C, sl], r=[Mc, Nc], w=[psH])
                    self.cp(Mn.ap[0:C, 0:HC], psH.ap[0:C, 0:HC], r=[psH], w=[Mn])
                    yield
                for h in range(NH):
                    sl = slice(h * C, (h + 1) * C)
                    self.mm(psG.ap[0:C, 512 + h * C:512 + (h + 1) * C], Nn.ap[0:C, sl], nf["Pm"].ap[0:C, sl],
                            r=[Nn, nf["Pm"]], w=[psG1])
                self.tt(nf["Pm"].ap[0:C, 0:HC], nf["Pm"].ap[0:C, 0:HC], psG.ap[0:C, 512:512 + HC], ALU.add, r=[psG1, nf["Pm"]], w=[nf["Pm"]])
                Nc, Nn = Nn, Nc
                Mc, Mn = Mn, Mc
                yield
            X = b_["Xb"]
            self.cp(X.ap[0:C, 0:HC], nf["Pm"].ap[0:C, 0:HC], r=[nf["Pm"]], w=[X], eng="pool")
            yield
            for h in range(NH):
                self.mm(psH.ap[:, h * C:(h + 1) * C], b_["kbg"].ap[0:C, h * 128:(h + 1) * 128], X.ap[0:C, h * C:(h + 1) * C],
                        r=[b_["kbg"], X], w=[psH])
            self.act(b_["nwT"].ap[:, 0:HC], psH.ap[:, 0:HC], AF.Copy, r=[psH], w=[b_["nwT"]], scale=-1.0)
            yield
            for h in range(NH):
                sl = slice(h * 128, (h + 1) * 128)
                self.mm(psG.ap[0:C, sl], X.ap[0:C, h * C:(h + 1) * C], b_["vb"].ap[0:C, sl], start=True, stop=False,
                        r=[X, b_["vb"]], w=[psG, psG1], inc=False)
                self.mm(psG.ap[0:C, sl], b_["nwT"].ap[:, h * C:(h + 1) * C], Sbf.ap[:, h, :], start=False, stop=True,
                        r=[b_["nwT"], Sbf], w=[psG, psG1])
            self.cp(b_["vnew"].ap[0:C, :], psG.ap[0:C, :], r=[psG, psG1], w=[b_["vnew"]], eng="act")
            yield
            if want_out:
                for h in range(NH):
                    self.mm(psH.ap[:, h * C:(h + 1) * C], Sbf.ap[:, h, :], b_["qg"].ap[:, h * C:(h + 1) * C], start=True, stop=False,
                            r=[Sbf, b_["qg"]], w=[psH], inc=False)
                    self.mm(psH.ap[:, h * C:(h + 1) * C], b_["vnew"].ap[0:C, h * 128:(h + 1) * 128],
                            b_["QKDT"].ap[0:C, h * C:(h + 1) * C], start=False, stop=True, r=[b_["vnew"], b_["QKDT"]], w=[psH])
                self.cp(f1.ap[:, 0:HC], psH.ap[:, 0:HC], r=[psH], w=[f1], eng="act")
            yield
            for h in range(NH):
                sl = slice(h * 128, (h + 1) * 128)
                self.mm(psG.ap[:, sl], b_["kd"].ap[0:C, sl], b_["vnew"].ap[0:C, sl], r=[b_["kd"], b_["vnew"]], w=[psG, psG1])
            self.tt(Sst.ap[:], Sst.ap[:], sm.ap[:, 32:40].unsqueeze(2).to_broadcast([128, NH, 128]), ALU.mult, r=[Sst, sm], w=[Sst])
            self.tt(Sst.ap[:], Sst.ap[:], psG.ap[:, :].rearrange("p (h d) -> p h d", h=NH), ALU.add, r=[Sst, psG, psG1], w=[Sst])
            self.cp(Sbf.ap[:], Sst.ap[:], r=[Sst], w=[Sbf], eng="act")
            yield
            if want_out:
                self.act(b_["N1"].ap[:, 0:HC], f1.ap[:, 0:HC], AF.Square, r=[f1], w=[b_["N1"]])
                for (h0, h1) in banks:
                    self.mm(psH.ap[:, h0 * C:h1 * C], onesb.ap[:], b_["N1"].ap[:, h0 * C:h1 * C], r=[onesb, b_["N1"]], w=[psH])
                self.ts(f2.ap[:, 0:HC], psH.ap[:, 0:HC], 1.0 / HD, ALU.mult, EPS, ALU.add, r=[psH], w=[f2])
                self.act(f2.ap[:, 0:HC], f2.ap[:, 0:HC], AF.Sqrt, r=[f2], w=[f2])
                self.recip(f2.ap[:, 0:HC], f2.ap[:, 0:HC], r=[f2], w=[f2])
                self.tt(f1.ap[:, 0:HC], f1.ap[:, 0:HC], f2.ap[:, 0:HC], ALU.mult, r=[f1, f2], w=[f1])
                self.stt(mixT.ap[:, 0:NH, c0:c0 + C], d3(f1), gnw.ap[:, 0:1], zs.ap[:, :, c0:c0 + C], ALU.mult, ALU.mult,
                         r=[f1, gnw, zs], w=[mixTg])

        def swa_chunk(qcol, Cq, kblocks, mcols):
            for g in range(2):
                NQ = 4 * Cq
                rhs_q = qa.ap[:, g * 4:(g + 1) * 4, qcol:qcol + Cq]
                dsts = [(psA[0], sc1, pT1, 0), (psA[0], sc1, pT1, 256), (psA[1], sc2, pT2, 0)]
                for bi, (ec, nk, vfn) in enumerate(kblocks):
                    pt, sc, pTt, off = dsts[bi]
                    self.mm(pt.ap[0:nk, off:off + NQ], kab.ap[:, g, ec:ec + nk], rhs_q, r=[kab, qa], w=[pt])
                yield
                for bi, (ec, nk, vfn) in enumerate(kblocks):
                    pt, sc, pTt, off = dsts[bi]
                    bias_ap = biasT.ap[0:nk, g, bi, :, 0:Cq]
                    o3 = sc.ap[0:nk, off:off + NQ].rearrange("p (h c) -> p h c", h=4)
                    i3 = pt.ap[0:nk, off:off + NQ].rearrange("p (h c) -> p h c", h=4)
                    if mcols is None:
                        self.tt(o3, i3, bias_ap, ALU.add, r=[pt, biasT], w=[sc])
                    else:
                        mc = mcols[bi]
                        self.stt(o3, i3, maskc.ap[0:nk, mc:mc + 1], bias_ap, ALU.add, ALU.add, r=[pt, biasT, maskc], w=[sc])
                    self.act(pTt.ap[0:nk, off:off + NQ], sc.ap[0:nk, off:off + NQ], AF.Exp, r=[sc], w=[pTt])
                    yield
                for bi, (ec, nk, vfn) in enumerate(kblocks):
                    pt, sc, pTt, off = dsts[bi]
                    self.mm(psT.ap[:, 0:NQ], vfn(g), pTt.ap[0:nk, off:off + NQ], start=(bi == 0), stop=(bi == 2),
                            r=[vtok, pTt], w=[psT], inc=(bi == 2))
                for bi, (ec, nk, vfn) in enumerate(kblocks):
                    pt, sc, pTt, off = dsts[bi]
                    self.mm(psT.ap[:, 256:256 + NQ], onesb.ap[0:nk, :], pTt.ap[0:nk, off:off + NQ], start=(bi == 0), stop=(bi == 2),
                            r=[onesb, pTt], w=[psT], inc=(bi == 2))
                yield
                self.tt(dens.ap[:, 0:NQ].rearrange("p (h c) -> p h c", h=4), psT.ap[:, 256:256 + NQ].rearrange("p (h c) -> p h c", h=4),
                        esink.ap[:, g * 4:(g + 1) * 4].unsqueeze(2).to_broadcast([128, 4, Cq]), ALU.add, r=[psT, esink], w=[dens])
                self.recip(dens.ap[:, 0:NQ], dens.ap[:, 0:NQ], r=[dens], w=[dens])
                yield
                self.tt(mixT.ap[:, 8 + g * 4:8 + (g + 1) * 4, qcol:qcol + Cq], psT.ap[:, 0:NQ].rearrange("p (h c) -> p h c", h=4),
                        dens.ap[:, 0:NQ].rearrange("p (h c) -> p h c", h=4), ALU.mult, r=[psT, dens], w=[mixTs])
                yield

        def interleave(ga, gb, ratio):
            da = db = False
            while not (da and db):
                for _ in range(ratio):
                    if not da:
                        try:
                            next(ga)
                        except StopIteration:
                            da = True
                if not db:
                    try:
                        next(gb)
                    except StopIteration:
                        db = True

        def chain(gens):
            for g_ in gens:
                yield from g_

        def run_pipeline(tasks):
            starts = [i for i, t in enumerate(tasks) if t[0] is not None]
            done = set()
            pend = None
            for i, (ld, pj, post) in enumerate(tasks):
                if ld is not None:
                    if i not in done:
                        ld(); done.add(i)
                    nxt = [j for j in starts if j > i]
                    if nxt and nxt[0] not in done:
                        tasks[nxt[0]][0](); done.add(nxt[0])
                pt = pj()
                if pend is not None:
                    pend[1](pend[0])
                pend = (pt, post)
            if pend is not None:
                pend[1](pend[0])

        def gates_mm(wk_ab, col0, n, go):
            pt = psS
            for k in range(NFC):
                wslot, wap = wk_ab(k)
                self.mm(pt.ap[0:n, go:go + 16], hT.ap[:, k, col0:col0 + n], wap[:, 0:16], start=(k == 0), stop=(k == NFC - 1),
                        r=[hT, wslot], w=[pt], inc=(k == NFC - 1))
            return pt

        def gates_post(pt, n, gcol, go):
            x_ = sm.ap[0:n, 48:56]
            t_ = sm.ap[0:n, 56:64]
            self.tt(x_, pt.ap[0:n, go:go + 8], dtb.ap[0:n, :], ALU.add, r=[pt, dtb], w=[sm])
            self.ts(t_, x_, -1.0, ALU.mult, r=[sm], w=[sm])
            self.tt(t_, t_, x_, ALU.min, r=[sm], w=[sm])
            self.act(t_, t_, AF.Exp, r=[sm], w=[sm])
            self.act(t_, t_, AF.Ln, r=[sm], w=[sm], bias=1.0)
            self.stt(x_, x_, 0.0, t_, ALU.max, ALU.add, r=[sm], w=[sm])
            self.tt(gtok.ap[0:n, gcol, 0:8], x_, negea.ap[0:n, :], ALU.mult, r=[sm, negea], w=[gtok])
            self.act(gtok.ap[0:n, gcol, 8:16], pt.ap[0:n, go + 8:go + 16], AF.Sigmoid, r=[pt], w=[gtok])

        QKV0, Z0, AB0, QA0, KA0, VA0 = 0, 3072, 4096, 4112, 5136, 5392

        def w_in_block(c0, ncols):
            need_conv("w_in")
            return wload_k(w_in_b[:, c0:c0 + ncols], 8 if ncols > 256 else 16)

        def inproj(Tn_, segs_conv, need_q, need_z, need_swa, kvcol0, gate_list):
            hrhs = lambda k: hT.ap[:, k, 0:Tn_]
            tasks = []

            def add_block(c0, ncols, chunks):
                holder = {}

                def load():
                    holder["wk"] = w_in_block(c0, ncols)
                for i, (mcol, post) in enumerate(chunks):
                    tasks.append((load if i == 0 else None,
                                  (lambda mcol=mcol: proj_chunk(holder["wk"], mcol, 128, Tn_, NFC, hrhs)), post))
                return holder

            def qkv_post(ch):
                which, h = ch // 8, ch % 8
                if which == 0:
                    return lambda pt: conv_silu(pt, ch, Tn_, segs_conv(ch), qn.ap[:, h, 0:Tn_], qn, HD ** -0.5)
                if which == 1:
                    return lambda pt: conv_silu(pt, ch, Tn_, segs_conv(ch), kn.ap[:, h, 0:Tn_], kn, 1.0)
                return lambda pt: conv_silu(pt, ch, Tn_, segs_conv(ch), vT.ap[:, h, 0:Tn_], vT, None)
            for blk in range(6):
                if blk < 2 and not need_q:
                    continue
                add_block(QKV0 + blk * 512, 512, [(m * 128, qkv_post(blk * 4 + m)) for m in range(4)])
            if need_z:
                for blk in range(2):
                    add_block(Z0 + blk * 512, 512,
                              [(m * 128, (lambda pt, c=blk * 4 + m: self.act(zs.ap[:, c, 0:Tn_], pt.ap[:, 0:Tn_], AF.Silu, r=[pt], w=[zs])))
                               for m in range(4)])
            hab = {}

            def load_ab():
                hab["wk"] = w_in_block(AB0, 16)
            for gi, (col0, n, gcol) in enumerate(gate_list):
                go = 16 + 16 * (gi % 2)
                tasks.append((load_ab if gi == 0 else None, (lambda col0=col0, n=n, go=go: gates_mm(hab["wk"], col0, n, go)),
                              (lambda pt, n=n, gcol=gcol, go=go: gates_post(pt, n, gcol, go))))
            if need_swa:
                for blk in range(2):
                    add_block(QA0 + blk * 512, 512,
                              [(m * 128, (lambda pt, c=blk * 4 + m: rms_head(pt, Tn_, qnw.ap[:, 0:1], qa.ap[:, c, 0:Tn_], qa)))
                               for m in range(4)])
            if kvcol0 is not None:
                def k_post(m):
                    def f(pt):
                        rms_head(pt, Tn_, knw.ap[:, 0:1], kaf.ap[:, m, kvcol0:kvcol0 + Tn_], kaf)
                        self.cp(kab.ap[:, m, kvcol0:kvcol0 + Tn_], kaf.ap[:, m, kvcol0:kvcol0 + Tn_], r=[kaf], w=[kab], eng="act")
                    return f
                add_block(KA0, 512, [(m * 128, k_post(m)) for m in range(2)] +
                          [(256 + m * 128, (lambda pt, m=m: self.cp(vaf.ap[:, m, kvcol0:kvcol0 + Tn_], pt.ap[:, 0:Tn_], r=[pt], w=[vaf], eng="act")))
                           for m in range(2)])
            run_pipeline(tasks)

        def make_vtok(ext_c0, n, blk_idx, rows0=0):
            for g in range(2):
                self.tr(psT.ap[0:n, g * 128:(g + 1) * 128], vaf.ap[:, g, ext_c0:ext_c0 + n], ident.ap[:], r=[vaf, ident], w=[psT])
            self.cp(vtok.ap[rows0:rows0 + n, blk_idx, :, :], psT.ap[0:n, 0:256].rearrange("p (g d) -> p g d", g=2), r=[psT], w=[vtok])

        def proj_residual(wdram, nk, rhs_of, rhs_tn, Tn_, gate_groups, xtiles, bcols, kps):
            tasks = []
            for mb in range(D // bcols):
                holder = {}

                def load(mb=mb, holder=holder):
                    holder["wk"] = wload_k(wdram[:, mb * bcols:(mb + 1) * bcols], kps)
                for m in range(bcols // 128):
                    fc = mb * (bcols // 128) + m

                    def post(pt, fc=fc):
                        for (c0, n, s, gfn) in gate_groups:
                            self.act(evac.ap[:, c0:c0 + n], pt.ap[:, c0:c0 + n], AF.Copy, r=[pt, modT], w=[evac], scale=gfn(fc, s))
                        for (col0, nrows, x_ap, x_tn) in xtiles:
                            self.tr(psT.ap[0:nrows, 0:128], evac.ap[:, col0:col0 + nrows], ident.ap[:], r=[evac, ident], w=[psT])
                            self.tt(x_ap[:, fc * 128:(fc + 1) * 128], x_ap[:, fc * 128:(fc + 1) * 128], psT.ap[0:nrows, 0:128], ALU.add,
                                    r=[psT, x_tn], w=[x_tn])
                    tasks.append((load if m == 0 else None,
                                  (lambda m=m, holder=holder: proj_chunk(holder["wk"], m * 128, 128, Tn_, nk, rhs_of, rd=rhs_tn)), post))
            run_pipeline(tasks)

        def ffn_conv(pt, ch, ext, acc, segs_ffn):
            for (c0, n, prev_fn, save_fn) in segs_ffn:
                pa, ptn = prev_fn(ch)
                self.cp(ext.ap[:, 0:2], pa, r=[ptn], w=[ext])
                self.cp(ext.ap[:, 2:2 + n], pt.ap[:, c0:c0 + n], r=[pt], w=[ext], eng="act")
                if save_fn is not None:
                    sa, stn = save_fn(ch)
                    self.cp(sa, ext.ap[:, n:n + 2], r=[ext], w=[stn])
                self.act(acc.ap[:, c0:c0 + n], ext.ap[:, 0:n], AF.Identity, r=[ext, fcw, fcb], w=[acc],
                         scale=fcw.ap[:, ch, 0:1], bias=fcb.ap[:, ch:ch + 1])
                for tp in (1, 2):
                    self.stt(acc.ap[:, c0:c0 + n], ext.ap[:, tp:tp + n], fcw.ap[:, ch, tp:tp + 1], acc.ap[:, c0:c0 + n],
                             ALU.mult, ALU.add, r=[ext, fcw, acc], w=[acc])

        def ffn(Tn_, segs_ffn, gate_groups, xtiles):
            hrhs = lambda k: hT.ap[:, k, 0:Tn_]
            cc = 0
            for (c_lo, c_hi) in ((0, 24), (24, NFF)):
                for blk in range(c_lo, c_hi, 4):
                    need_conv("w_up")
                    wk = wload_k(w_up_b[:, blk * 128:blk * 128 + 512], 8)
                    for m in range(4):
                        ch = blk + m
                        cc += 1
                        ext, acc = [(ug, fa), (uu, fb)][cc % 2]
                        pt = proj_chunk(wk, m * 128, 128, Tn_, NFC, hrhs)
                        ffn_conv(pt, ch, ext, acc, segs_ffn)
                        atn, aap = actT_c(ch - c_lo)
                        self.act(aap[:, 0:Tn_], acc.ap[:, 0:Tn_], AF.Silu, r=[acc], w=[atn])
                    wk = wload_k(w_up_b[:, DFF + blk * 128:DFF + blk * 128 + 512], 8)
                    for m in range(4):
                        ch = blk + m
                        cc += 1
                        ext, acc = [(ug, fa), (uu, fb)][cc % 2]
                        pt = proj_chunk(wk, m * 128, 128, Tn_, NFC, hrhs)
                        ffn_conv(pt, NFF + ch, ext, acc, segs_ffn)
                        atn, aap = actT_c(ch - c_lo)
                        self.tt(aap[:, 0:Tn_], aap[:, 0:Tn_], acc.ap[:, 0:Tn_], ALU.mult, r=[atn, acc], w=[atn])
                nk = c_hi - c_lo
                need_conv("w_down")
                proj_residual(w_down_b[c_lo * 128:c_hi * 128, :], nk, lambda k: actT_c(k)[1][:, 0:Tn_], [qn, kn, vT], Tn_,
                              gate_groups, xtiles, 256, nk // 2)

        xstage = xres
        n_pst = (NP + TS - 1) // TS
        for st in range(n_pst):
            t0 = st * TS
            nt = min(TS, NP - t0)
            Tn_ = nt * 128
            last = (st == n_pst - 1)
            for j in range(nt):
                self.dma("sp", xstage.ap[:, j, :], xp[(t0 + j) * 128:(t0 + j + 1) * 128, :], w=[xstage])
                norm_transpose(xstage, xstage.ap[:, j, :], 128, j * 128, [(0, 128, 0, True)], 1)
            cp_t = (convprev,)
            segs_conv = lambda ch: [(0, Tn_, (convprev.ap[:, ch, :], convprev), (convprev.ap[:, ch, :], convprev))]
            inproj(Tn_, segs_conv, need_q=last, need_z=False, need_swa=False,
                   kvcol0=(128 - Tn_ + 0 if last else None) if False else (0 if last else None),
                   gate_list=[(j * 64, 64, j) for j in range(2 * nt)])
            if last:
                if Tn_ > 128:
                    for g in range(2):
                        self.cp(f1.ap[:, 0:128], kaf.ap[:, g, Tn_ - 128:Tn_], r=[kaf], w=[f1])
                        self.cp(kaf.ap[:, g, 0:128], f1.ap[:, 0:128], r=[f1], w=[kaf])
                        self.cp(f1.ap[:, 0:128], vaf.ap[:, g, Tn_ - 128:Tn_], r=[vaf], w=[f1])
                        self.cp(vaf.ap[:, g, 0:128], f1.ap[:, 0:128], r=[f1], w=[vaf])
                    self.cp(kab.ap[:, :, 0:128], kaf.ap[:, :, 0:128], r=[kaf], w=[kab])
            for j in range(2 * nt):
                for _ in gdn_chunk(64, j * 64, j, want_out=False):
                    pass

        n_mst = (NM + TS - 1) // TS
        assert NM - (n_mst - 1) * TS < TS, "last main super-tile needs a free tile slot for the sample tokens"
        out_events = []
        TSM = NS * LS
        sconv = sb("sconv", [128, NS, 24, 3]); sffn = sb("sffn", [128, NS, 2 * NFF, 2])
        sconv_o = sb("sconv_o", [128, NS, 24, 3]); sffn_o = sb("sffn_o", [128, NS, 2 * NFF, 2])
        scv = sb("scv", [64, 2, 2, 128])
        scvb = sb("scvb", [64, 2, 2, 128], BF16)
        skT = sb("skT", [128, 2, 128])
        kout = sb("kout", [128, 2, 128])
        for st in range(n_mst):
            t0 = st * TS
            nt = min(TS, NM - t0)
            Tm = nt * 128
            smp = (st == n_mst - 1)
            Tn_ = Tm + (TSM if smp else 0)
            grp = [(s * LS, LS, 1 + s, False) for s in range(NS)]
            for j in range(nt):
                self.dma("sp", xres.ap[:, j, :], xm[(t0 + j) * 128:(t0 + j + 1) * 128, :], w=[xres])
                norm_transpose(xres, xres.ap[:, j, :], 128, j * 128, [(0, 128, 0, (t0 + j) == 0)], 1)
            if smp:
                for s in range(NS):
                    self.dma("sp", sconv.ap[:, s], s_conv_d[s], w=[sconv])
                    self.dma("sp", sffn.ap[:, s], s_ffn_d[s], w=[sffn])
                self.dma("sp", xres.ap[0:TSM, nt, :], xs, w=[xres])
                norm_transpose(xres, xres.ap[0:TSM, nt, :], TSM, Tm, grp, 1)

            def segs_conv(ch, Tm=Tm, smp=smp):
                sg = [(0, Tm, (convprev.ap[:, ch, :], convprev), (convprev.ap[:, ch, :], convprev))]
                if smp:
                    sg += [(Tm + s * LS, LS, (sconv.ap[:, s, ch, :], sconv), (sconv_o.ap[:, s, ch, :], sconv_o)) for s in range(NS)]
                return sg
            gl_ = [(j * 64, 64, j) for j in range(2 * nt)]
            if smp:
                gl_ += [(Tm + s * LS, LS, 2 * nt + s) for s in range(NS)]
            inproj(Tn_, segs_conv, need_q=True, need_z=True, need_swa=True, kvcol0=128, gate_list=gl_)
            for bk in range((128 + Tm) // 64):
                make_vtok(bk * 64, 64, bk)
            swa_gens = []
            for cq in range(Tm // 64):
                gch = t0 * 2 + cq
                kbl = [((cq + r_) * 64, 64, (lambda g, b=cq + r_: vtok.ap[0:64, b, g, :])) for r_ in range(3)]
                mcols = [gch * 3 + r_ for r_ in range(3)] if gch < 4 else None
                swa_gens.append(swa_chunk(cq * 64, 64, kbl, mcols))
            interleave(chain([gdn_chunk(64, j * 64, j, want_out=True) for j in range(2 * nt)]), chain(swa_gens), 3)
            if smp:
                out_events.append(self.dma("sp", o_kT, kaf.ap[:, :, Tm:Tm + 128], r=[kaf]))
                for g in range(2):
                    self.tr(psT.ap[:, g * 128:(g + 1) * 128], vaf.ap[:, g, Tm:Tm + 128], ident.ap[:], r=[vaf, ident], w=[psT])
                self.cp(vtokf.ap[:], psT.ap[:, 0:256].rearrange("p (g d) -> p g d", g=2), r=[psT], w=[vtokf])
                out_events.append(self.dma("sp", o_v, vtokf.ap[:], r=[vtokf]))
                out_events.append(self.dma("sp", o_conv, convprev.ap[:], r=[convprev]))
                out_events.append(self.dma("sp", o_delta.rearrange("h k v -> k h v"), Sst.ap[:], r=[Sst]))
                for s in range(NS):
                    sc0 = Tm + s * LS
                    self.dma("sp", scv.ap[:], s_v_d[s].rearrange("(b p) g d -> p b g d", p=64), w=[scv])
                    self.dma("sp", skT.ap[:], s_kT_d[s], w=[skT])
                    self.cp(scvb.ap[:], scv.ap[:], r=[scv], w=[scvb], eng="pool")
                    self.dma("sp", Sst.ap[:], s_delta_d[s].rearrange("h k v -> k h v"), w=[Sst])
                    self.cp(Sbf.ap[:], Sst.ap[:], r=[Sst], w=[Sbf], eng="act")
                    for _ in gdn_chunk(LS, sc0, 2 * nt + s, want_out=True):
                        pass
                    out_events.append(self.dma("sp", os_delta[s].rearrange("h k v -> k h v"), Sst.ap[:], r=[Sst]))
                    self.cp(kab.ap[:, :, 0:128], skT.ap[:], r=[skT], w=[kab])
                    make_vtok(128 + sc0, LS, TE // 64 - 1)
                    self._swa_sample(sc0, 128 + sc0, TE // 64 - 1, scvb, vtok, qa, kab, psS, psT, psG, psH, sc1, sc2, pT1, pT2,
                                     biasT, onesb, esink, den, mixT)
                    self.cp(kout.ap[:, :, 0:112], skT.ap[:, :, 16:128], r=[skT], w=[kout])
                    self.cp(kout.ap[:, :, 112:128], kaf.ap[:, :, 128 + sc0:128 + sc0 + LS], r=[kaf], w=[kout])
                    out_events.append(self.dma("sp", os_kT[s], kout.ap[:], r=[kout]))
                    out_events.append(self.dma("sp", os_v[s, 0:112], s_v_d[s, 16:128]))
                    for g in range(2):
                        self.tr(psT.ap[0:LS, g * 128:(g + 1) * 128], vaf.ap[:, g, 128 + sc0:128 + sc0 + LS], ident.ap[:],
                                r=[vaf, ident], w=[psT])
                    self.cp(vtokf.ap[0:LS], psT.ap[0:LS, 0:256].rearrange("p (g d) -> p g d", g=2), r=[psT], w=[vtokf])
                    out_events.append(self.dma("sp", os_v[s, 112:128], vtokf.ap[0:LS], r=[vtokf]))
            else:
                for g in range(2):
                    self.cp(f1.ap[:, 0:128], kaf.ap[:, g, Tm:Tm + 128], r=[kaf], w=[f1])
                    self.cp(kaf.ap[:, g, 0:128], f1.ap[:, 0:128], r=[f1], w=[kaf])
                    self.cp(f1.ap[:, 0:128], vaf.ap[:, g, Tm:Tm + 128], r=[vaf], w=[f1])
                    self.cp(vaf.ap[:, g, 0:128], f1.ap[:, 0:128], r=[f1], w=[vaf])
                self.cp(kab.ap[:, :, 0:128], kaf.ap[:, :, 0:128], r=[kaf], w=[kab])
            xt = [(j * 128, 128, xres.ap[:, j, :], xres) for j in range(nt)]
            gg1 = [(0, Tm, 0, G1)]
            gg2 = [(0, Tm, 0, G2)]
            if smp:
                xt.append((Tm, TSM, xres.ap[0:TSM, nt, :], xres))
                gg1 += [(Tm + s * LS, LS, 1 + s, G1) for s in range(NS)]
                gg2 += [(Tm + s * LS, LS, 1 + s, G2) for s in range(NS)]
            need_conv("w_o")
            proj_residual(w_o_b, NFC, lambda k: mixT.ap[:, k, 0:Tn_], [mixT, mixTg, mixTs], Tn_, gg1, xt, 512, 8)
            for j in range(nt):
                norm_transpose(xres, xres.ap[:, j, :], 128, j * 128, [(0, 128, 0, (t0 + j) == 0)], 2)
            if smp:
                norm_transpose(xres, xres.ap[0:TSM, nt, :], TSM, Tm, grp, 2)
            segs_ffn = [(0, Tm, lambda ch: (uprev.ap[:, ch, :], uprev), lambda ch: (uprev.ap[:, ch, :], uprev))]
            if smp:
                segs_ffn += [(Tm + s * LS, LS, (lambda ch, s=s: (sffn.ap[:, s, ch, :], sffn)),
                              (lambda ch, s=s: (sffn_o.ap[:, s, ch, :], sffn_o))) for s in range(NS)]
            ffn(Tn_, segs_ffn, gg2, xt)
            for j in range(nt):
                out_events.append(self.dma("sp", ym[(t0 + j) * 128:(t0 + j + 1) * 128, :], xres.ap[:, j, :], r=[xres]))
            if smp:
                out_events.append(self.dma("sp", ys, xres.ap[0:TSM, nt, :], r=[xres]))
                for s in range(NS):
                    out_events.append(self.dma("sp", os_conv[s], sconv_o.ap[:, s], r=[sconv_o]))
                    out_events.append(self.dma("sp", os_ffn[s], sffn_o.ap[:, s], r=[sffn_o]))
        out_events.append(self.dma("sp", o_ffn, uprev.ap[:], r=[uprev]))

        S.finish("sp")
        S.emit()
        self.es.close()
        return nc

    def _swa_sample(self, qcol, kcol, vblk, scvb, vtok, qa, kab, psS, psT, psG, psH, sc1, sc2, pT1, pT2, biasT, onesb, esink, den, mixT):
        Cq = LS
        for g in range(2):
            NQ = 4 * Cq
            rhs_q = qa.ap[:, g * 4:(g + 1) * 4, qcol:qcol + Cq]
            blocks = [(kab.ap[:, g, 0:64], 64, scvb.ap[0:64, 0, g, :], kab),
                      (kab.ap[:, g, 64:128], 64, scvb.ap[0:64, 1, g, :], kab),
                      (kab.ap[:, g, kcol:kcol + LS], LS, vtok.ap[0:LS, vblk, g, :], kab)]
            dsts = [(psS, sc1, pT1, 0), (psS, sc1, pT1, 256), (psT, sc2, pT2, 0)]
            for bi, (kap, nk, vap, ktn) in enumerate(blocks):
                pt, sc, pTt, off = dsts[bi]
                self.mm(pt.ap[0:nk, off:off + NQ], kap, rhs_q, r=[ktn, qa], w=[pt])
            for bi, (kap, nk, vap, ktn) in enumerate(blocks):
                pt, sc, pTt, off = dsts[bi]
                bias_ap = biasT.ap[0:nk, g, bi, :, 0:Cq]
                o3 = sc.ap[0:nk, off:off + NQ].rearrange("p (h c) -> p h c", h=4)
                i3 = pt.ap[0:nk, off:off + NQ].rearrange("p (h c) -> p h c", h=4)
                self.tt(o3, i3, bias_ap, ALU.add, r=[pt, biasT], w=[sc])
                self.act(pTt.ap[0:nk, off:off + NQ], sc.ap[0:nk, off:off + NQ], AF.Exp, r=[sc], w=[pTt])
            for bi, (kap, nk, vap, ktn) in enumerate(blocks):
                pt, sc, pTt, off = dsts[bi]
                self.mm(psG.ap[:, 0:NQ], vap, pTt.ap[0:nk, off:off + NQ], start=(bi == 0), stop=(bi == 2),
                        r=[vtok, scvb, pTt], w=[psG], inc=(bi == 2))
            for bi, (kap, nk, vap, ktn) in enumerate(blocks):
                pt, sc, pTt, off = dsts[bi]
                self.mm(psH.ap[:, 0:NQ], onesb.ap[0:nk, :], pTt.ap[0:nk, off:off + NQ], start=(bi == 0), stop=(bi == 2),
                        r=[onesb, pTt], w=[psH], inc=(bi == 2))
            self.tt(den.ap[:, 0:NQ].rearrange("p (h c) -> p h c", h=4), psH.ap[:, 0:NQ].rearrange("p (h c) -> p h c", h=4),
                    esink.ap[:, g * 4:(g + 1) * 4].unsqueeze(2).to_broadcast([128, 4, Cq]), ALU.add, r=[psH, esink], w=[den])
            self.recip(den.ap[:, 0:NQ], den.ap[:, 0:NQ], r=[den], w=[den])
            self.tt(mixT.ap[:, 8 + g * 4:8 + (g + 1) * 4, qcol:qcol + Cq], psG.ap[:, 0:NQ].rearrange("p (h c) -> p h c", h=4),
                    den.ap[:, 0:NQ].rearrange("p (h c) -> p h c", h=4), ALU.mult, r=[psG, den], w=[mixT])


def _consts(NM, half):
    idx = np.arange(128)
    c = {}
    c["ident"] = np.eye(128, dtype=np.float32)
    c["U"] = (idx[:, None] <= idx[None, :]).astype(np.float32)
    c["Lst"] = (idx[:, None] > idx[None, :]).astype(np.float32)
    c["Linc"] = (idx[:, None] >= idx[None, :]).astype(np.float32)
    slopes = 2.0 ** (-8.0 * np.arange(1, 9, dtype=np.float64) / 8.0)
    p = np.arange(64)[:, None, None, None, None]
    g = np.arange(2)[None, :, None, None, None]
    r = np.arange(3)[None, None, :, None, None]
    hh = np.arange(4)[None, None, None, :, None]
    i = np.arange(64)[None, None, None, None, :]
    dist = np.abs(128 + i - (64 * r + p)).astype(np.float64)
    c["biasT"] = (-(slopes[(g * 4 + hh)]) * dist).astype(np.float32)
    mk = np.zeros((64, 2 * NM * 3), np.float32)
    if half == 0:
        for m in range(2 * NM):
            for r_ in range(3):
                if 64 * (m + r_) - 128 < 128:
                    mk[:, m * 3 + r_] = -30000.0
    c["maskc"] = mk
    return c


def _fm(v, n):
    return np.ascontiguousarray(np.asarray(v, np.float32).reshape(n, 128).T)


_CACHE = {}


def run_cores(inputs, SEQ, TS=3, n_cores=8):
    f32 = np.float32
    half_len = SEQ // 2
    NM = (half_len + 128) // 128
    NP = (half_len - 128) // 128
    key = (NP, NM, TS)
    if key not in _CACHE:
        _CACHE[key] = Builder(NP, NM, TS).build()
    nc = _CACHE[key]
    x_prompt = np.asarray(inputs["x_prompt"], f32)
    x_sample = np.asarray(inputs["x_sample"], f32)
    shared = {
        "ada_w": np.ascontiguousarray(np.asarray(inputs["ada_w"], f32)[0]),
        "ada_b": _fm(inputs["ada_b"][0], 96),
        "norm1_w": _fm(inputs["norm1_w"][0], 16), "norm2_w": _fm(inputs["norm2_w"][0], 16),
        "w_in": np.ascontiguousarray(np.asarray(inputs["w_in"], f32)[0]),
        "conv_w": np.ascontiguousarray(np.asarray(inputs["conv_qkv_w"], f32)[0].reshape(4, 24, 128).transpose(2, 1, 0)),
        "a_log": np.asarray(inputs["a_log"], f32)[0], "dt_bias": np.asarray(inputs["dt_bias"], f32)[0],
        "gdn_norm_w": np.asarray(inputs["gdn_norm_w"], f32)[0].reshape(128, 1),
        "q_norm_w": np.asarray(inputs["q_norm_w"], f32)[0].reshape(128, 1),
        "k_norm_w": np.asarray(inputs["k_norm_w"], f32)[0].reshape(128, 1),
        "sinks": np.asarray(inputs["sinks"], f32)[0],
        "w_o": np.ascontiguousarray(np.asarray(inputs["w_o"], f32)[0]),
        "w_up": np.ascontiguousarray(np.asarray(inputs["w_up"], f32)[0]),
        "ffn_conv_w": np.ascontiguousarray(np.asarray(inputs["ffn_conv_w"], f32)[0].reshape(3, 88, 128).transpose(2, 1, 0)),
        "ffn_conv_b": _fm(inputs["ffn_conv_b"][0], 88),
        "w_down": np.ascontiguousarray(np.asarray(inputs["w_down"], f32)[0]),
    }
    sc = np.asarray(inputs["state_conv_qkv"], f32)[0]
    sd = np.asarray(inputs["state_delta"], f32)[0]
    sk = np.asarray(inputs["cache_swa_k"], f32)[0]
    sv = np.asarray(inputs["cache_swa_v"], f32)[0]
    sf = np.asarray(inputs["state_ffn_conv"], f32)[0]
    cp = np.asarray(inputs["c_prompt"], f32)
    cs = np.asarray(inputs["c_sample"], f32)
    in_maps = []
    for c in range(n_cores):
        b, half = c // 2, c % 2
        start = half * half_len
        m = dict(shared)
        m.update(_consts(NM, half))
        xm = np.zeros((NM * 128, D), f32)
        xp = np.zeros((max(NP, 1) * 128, D), f32)
        if half == 0:
            xm[128:] = x_prompt[b, 0:half_len]
        else:
            xm[:] = x_prompt[b, start - 128:start + half_len]
            xp[:NP * 128] = x_prompt[b, 0:start - 128]
        m["xm"], m["xp"] = xm, xp
        ss = slice(c * NS, (c + 1) * NS)
        m["xs"] = np.ascontiguousarray(x_sample[ss].reshape(NS * LS, D))
        m["flag"] = np.full((128, 1), float(half), f32)
        cc = np.concatenate([cp[b:b + 1], cs[ss]], axis=0)
        m["cT"] = np.ascontiguousarray(cc.reshape(5, 16, 128).transpose(2, 1, 0))
        m["s_conv"] = np.ascontiguousarray(sc[ss].reshape(NS, 3, 24, 128).transpose(0, 3, 2, 1))
        m["s_delta"] = np.ascontiguousarray(sd[ss])
        m["s_kT"] = np.ascontiguousarray(sk[ss].transpose(0, 3, 2, 1))
        m["s_v"] = np.ascontiguousarray(sv[ss])
        m["s_ffn"] = np.ascontiguousarray(sf[ss].reshape(NS, 2, 88, 128).transpose(0, 3, 2, 1))
        in_maps.append(m)
    res = run_bass_kernel_spmd(nc, in_maps, core_ids=list(range(n_cores)))
    R = res.results
    B = n_cores // 2
    y_p = np.zeros((B, SEQ, D), f32)
    for c in range(n_cores):
        b, half = c // 2, c % 2
        y_p[b, half * half_len:(half + 1) * half_len] = R[c]["ym"][128:]
    y_s = np.concatenate([R[c]["ys"].reshape(NS, LS, D) for c in range(n_cores)], axis=0)
    odd = [R[c] for c in range(1, n_cores, 2)]
    p_conv = np.stack([r["o_conv"].transpose(2, 1, 0).reshape(3, 3072) for r in odd])[None]
    p_delta = np.stack([r["o_delta"] for r in odd])[None]
    p_k = np.stack([r["o_kT"].transpose(2, 1, 0) for r in odd])[None]
    p_v = np.stack([r["o_v"] for r in odd])[None]
    p_ffn = np.stack([r["o_ffn"].transpose(2, 1, 0).reshape(2, 2 * DFF) for r in odd])[None]
    s_conv = np.concatenate([r["os_conv"].transpose(0, 3, 2, 1).reshape(NS, 3, 3072) for r in R])[None]
    s_delta = np.concatenate([r["os_delta"] for r in R])[None]
    s_k = np.concatenate([r["os_kT"].transpose(0, 3, 2, 1) for r in R])[None]
    s_v = np.concatenate([r["os_v"] for r in R])[None]
    s_ffn = np.concatenate([r["os_ffn"].transpose(0, 3, 2, 1).reshape(NS, 2, 2 * DFF) for r in R])[None]
    outs = (y_p, y_s, p_conv, p_delta, p_k, p_v, p_ffn, s_conv, s_delta, s_k, s_v, s_ffn)
    return tuple(np.ascontiguousarray(o, dtype=f32) for o in outs)


def kernel(**inputs):
    return run_cores(inputs, SEQ=4096, TS=3, n_cores=8)
```

```python
import numpy as np
from contextlib import ExitStack
import concourse.bass as bass
import concourse.mybir as mybir
from concourse.bass_utils import run_bass_kernel_spmd

F32 = mybir.dt.float32
BF16 = mybir.dt.bfloat16
AF = mybir.ActivationFunctionType
ALU = mybir.AluOpType

D = 2048
NFC = 16
HD = 128
NH = 8
DFF = 5632
NFF = 44
INW = 5648
EPS = 1e-6
NS = 4
LS = 16
ENG_NAMES = ("pe", "act", "dve", "pool", "sp")


class Buf:
    __slots__ = ("name", "w", "r")

    def __init__(self, name=""):
        self.name = name
        self.w = None
        self.r = []


class Sched:
    def __init__(self, nc, n_dma_sems=8):
        self.nc = nc
        self.prog = {e: [] for e in ENG_NAMES}
        self.count = {e: 0 for e in ENG_NAMES}
        self.seen = {e: {} for e in ENG_NAMES}
        self.sems = {}
        self.n_dma_sems = n_dma_sems
        self.dma_rr = {"sp": 0, "pool": 0}
        self.dma_val = {}
        self.all_dma_events = []

    def _waits_for(self, e, reads, writes, is_dma=False):
        deps = []
        for b in reads:
            if b.w is not None:
                deps.append(b.w)
        for b in writes:
            if b.w is not None:
                deps.append(b.w)
            deps.extend(b.r)
        waits = {}
        for (sk, val, src) in deps:
            if src == e and e == "pe" and not is_dma:
                continue
            if self.seen[e].get(sk, 0) >= val:
                continue
            if waits.get(sk, 0) < val:
                waits[sk] = val
        for sk, val in waits.items():
            self.seen[e][sk] = val
        return list(waits.items())

    def op(self, e, fn, reads=(), writes=(), inc=True):
        waits = self._waits_for(e, reads, writes)
        if inc:
            self.count[e] += 1
            ev = (("eng", e), self.count[e], e)
        else:
            ev = (("eng", e), self.count[e] + 1, e)
        self.prog[e].append((waits, fn, inc))
        for b in reads:
            b.r.append(ev)
        for b in writes:
            b.w = ev
            b.r = []
        return ev

    def dma(self, q, fn, reads=(), writes=()):
        i = self.dma_rr[q]
        self.dma_rr[q] = (i + 1) % self.n_dma_sems
        sk = ("dma", q, i)
        prev = self.dma_val.get(sk, 0)
        waits = dict(self._waits_for(q, reads, writes, is_dma=True))
        if prev > 0 and self.seen[q].get(sk, 0) < prev:
            waits[sk] = prev
            self.seen[q][sk] = prev
        val = prev + 16
        self.dma_val[sk] = val
        ev = (sk, val, "dma_" + q)
        self.prog[q].append((list(waits.items()), fn, ("dma", sk)))
        for b in reads:
            b.r.append(ev)
        for b in writes:
            b.w = ev
            b.r = []
        return ev

    def barrier(self):
        evs = [(("eng", e), self.count[e]) for e in ENG_NAMES if self.count[e] > 0]
        evs += [(sk, v) for sk, v in self.dma_val.items()]
        for e in ENG_NAMES:
            waits = []
            for sk, val in evs:
                if sk == ("eng", e) and e == "pe":
                    continue
                if self.seen[e].get(sk, 0) < val:
                    self.seen[e][sk] = val
                    waits.append((sk, val))
            if waits:
                self.prog[e].append((waits, None, False))

    def finish(self, e="sp"):
        waits = [(sk, v) for sk, v in self.dma_val.items()]
        waits += [(("eng", x), self.count[x]) for x in ENG_NAMES if self.count[x] > 0 and x != e]
        self.prog[e].append((waits, None, False))

    def emit(self):
        nc = self.nc
        with ExitStack() as es:
            for e in ENG_NAMES:
                self.sems[("eng", e)] = es.enter_context(nc.semaphore("s_" + e))
            for q in ("sp", "pool"):
                for i in range(self.n_dma_sems):
                    self.sems[("dma", q, i)] = es.enter_context(nc.semaphore("d_%s%d" % (q, i)))
            block = es.enter_context(nc.Block())
            sems = self.sems
            prog = self.prog

            def run(ename):
                def body(engine):
                    for (waits, fn, inc) in prog[ename]:
                        for sk, val in waits:
                            engine.wait_ge(sems[sk], val)
                        if fn is None:
                            continue
                        ins = fn(engine)
                        if inc is True:
                            ins.then_inc(sems[("eng", ename)], 1)
                        elif isinstance(inc, tuple):
                            ins.then_inc(sems[inc[1]], 16)
                return body

            block.tensor(run("pe"))
            block.scalar(run("act"))
            block.vector(run("dve"))
            block.gpsimd(run("pool"))
            block.sync(run("sp"))


class Tn:
    def __init__(self, ap, name=""):
        self.ap = ap
        self.b = Buf(name)

    def __getitem__(self, k):
        return self.ap[k]


class Builder:
    def __init__(self, NP, NM, TS):
        self.NP, self.NM, self.TS = NP, NM, TS
        self.nc = bass.Bass("TRN2", target_bir_lowering=False)
        self.S = Sched(self.nc)
        self.es = ExitStack()
        self.dram = {}
        self.rr = 0

    def din(self, name, shape):
        self.dram[name] = self.nc.dram_tensor(name, list(shape), F32, kind="ExternalInput").ap()
        return self.dram[name]

    def dout(self, name, shape):
        self.dram[name] = self.nc.dram_tensor(name, list(shape), F32, kind="ExternalOutput").ap()
        return self.dram[name]

    def sb(self, name, shape, dt=F32):
        t = self.es.enter_context(self.nc.sbuf_tensor("sb_" + name, list(shape), dt))
        return Tn(t[:], name)

    def ps(self, name, shape, dt=F32):
        t = self.es.enter_context(self.nc.psum_tensor("ps_" + name, list(shape), dt))
        return Tn(t[:], name)

    def op(self, e, fn, r=(), w=(), inc=True):
        if e == "pool":
            e = "dve"
        return self.S.op(e, fn, reads=[t.b for t in r], writes=[t.b for t in w], inc=inc)

    def dma(self, q, out, in_, r=(), w=()):
        return self.S.dma(q, lambda e: e.dma_start(out=out, in_=in_), reads=[t.b for t in r], writes=[t.b for t in w])

    def mm(self, out, lhsT, rhs, start=True, stop=True, r=(), w=(), inc=True):
        return self.op("pe", lambda e: e.matmul(out, lhsT, rhs, start=start, stop=stop), r, w, inc)

    def tr(self, out, in_, ident, r=(), w=(), inc=True):
        return self.op("pe", lambda e: e.transpose(out, in_, ident), r, w, inc)

    def act(self, out, in_, func, r=(), w=(), scale=1.0, bias=0.0, accum_out=None, eng="act"):
        kw = {}
        if accum_out is not None:
            kw["accum_out"] = accum_out
        return self.op("act", lambda e: e.activation(out=out, in_=in_, func=func, scale=scale, bias=bias, **kw), r, w)

    def ew_engine(self):
        self.rr += 1
        return "dve" if self.rr % 3 else "pool"

    def tt(self, out, a, b, op, r=(), w=(), eng="dve"):
        return self.op(eng, lambda e: e.tensor_tensor(out=out, in0=a, in1=b, op=op), r, w)

    def ts(self, out, a, s1, op0, s2=None, op1=None, r=(), w=(), eng="dve"):
        if op1 is None:
            return self.op(eng, lambda e: e.tensor_scalar(out=out, in0=a, scalar1=s1, scalar2=None, op0=op0), r, w)
        return self.op(eng, lambda e: e.tensor_scalar(out=out, in0=a, scalar1=s1, scalar2=s2, op0=op0, op1=op1), r, w)

    def stt(self, out, a, s, b, op0, op1, r=(), w=()):
        return self.op("dve", lambda e: e.scalar_tensor_tensor(out=out, in0=a, scalar=s, in1=b, op0=op0, op1=op1), r, w)

    def cp(self, out, in_, r=(), w=(), eng="dve"):
        if eng == "act":
            return self.op("act", lambda e: e.copy(out=out, in_=in_), r, w)
        return self.op(eng, lambda e: e.tensor_copy(out=out, in_=in_), r, w)

    def recip(self, out, in_, r=(), w=()):
        return self.op("dve", lambda e: e.reciprocal(out=out, in_=in_), r, w)

    def memset(self, t, ap, val, eng="dve"):
        return self.op(eng, lambda e: e.memset(ap, val), (), [t])

    def rsqrt(self, tn, ap, mul):
        self.ts(ap, ap, mul, ALU.mult, EPS, ALU.add, r=[tn], w=[tn])
        self.act(ap, ap, AF.Sqrt, r=[tn], w=[tn])
        self.recip(ap, ap, r=[tn], w=[tn])

    def build(self):
        nc, S = self.nc, self.S
        NP, NM, TS = self.NP, self.NM, self.TS
        T = TS * 128
        TE = T + 128
        din, dout, sb, ps = self.din, self.dout, self.sb, self.ps

        xm = din("xm", [NM * 128, D])
        xp = din("xp", [max(NP, 1) * 128, D])
        xs = din("xs", [NS * LS, D])
        flag_d = din("flag", [128, 1])
        cT_d = din("cT", [128, NFC, 5])
        ada_w = din("ada_w", [D, 6 * D])
        ada_b = din("ada_b", [128, 96])
        n1w_d = din("norm1_w", [128, NFC])
        n2w_d = din("norm2_w", [128, NFC])
        w_in = din("w_in", [D, INW])
        cw_d = din("conv_w", [128, 24, 4])
        alog_d = din("a_log", [8])
        dtb_d = din("dt_bias", [8])
        gnw_d = din("gdn_norm_w", [128, 1])
        qnw_d = din("q_norm_w", [128, 1])
        knw_d = din("k_norm_w", [128, 1])
        sinks_d = din("sinks", [8])
        w_o = din("w_o", [D, D])
        w_up = din("w_up", [D, 2 * DFF])
        fcw_d = din("ffn_conv_w", [128, 2 * NFF, 3])
        fcb_d = din("ffn_conv_b", [128, 2 * NFF])
        w_down = din("w_down", [DFF, D])
        scr = lambda n, sh: nc.dram_tensor(n, list(sh), BF16, kind="Internal").ap()
        w_in_b, w_o_b = scr("w_in_b", [D, INW]), scr("w_o_b", [D, D])
        w_up_b, w_down_b = scr("w_up_b", [D, 2 * DFF]), scr("w_down_b", [DFF, D])
        s_conv_d = din("s_conv", [NS, 128, 24, 3])
        s_delta_d = din("s_delta", [NS, NH, 128, 128])
        s_kT_d = din("s_kT", [NS, 128, 2, 128])
        s_v_d = din("s_v", [NS, 128, 2, 128])
        s_ffn_d = din("s_ffn", [NS, 128, 2 * NFF, 2])
        ident_d = din("ident", [128, 128])
        U_d = din("U", [128, 128])
        Lst_d = din("Lst", [128, 128])
        Linc_d = din("Linc", [128, 128])
        biasT_d = din("biasT", [64, 2, 3, 4, 64])
        maskc_d = din("maskc", [64, 2 * NM * 3])

        ym = dout("ym", [NM * 128, D])
        ys = dout("ys", [NS * LS, D])
        o_conv = dout("o_conv", [128, 24, 3])
        o_delta = dout("o_delta", [NH, 128, 128])
        o_kT = dout("o_kT", [128, 2, 128])
        o_v = dout("o_v", [128, 2, 128])
        o_ffn = dout("o_ffn", [128, 2 * NFF, 2])
        os_conv = dout("os_conv", [NS, 128, 24, 3])
        os_delta = dout("os_delta", [NS, NH, 128, 128])
        os_kT = dout("os_kT", [NS, 128, 2, 128])
        os_v = dout("os_v", [NS, 128, 2, 128])
        os_ffn = dout("os_ffn", [NS, 128, 2 * NFF, 2])

        ident = sb("ident", [128, 128]); identb = sb("identb", [128, 128], BF16)
        U = sb("U", [128, 128]); Lst = sb("Lst", [128, 128]); Linc = sb("Linc", [128, 128])
        onesb = sb("onesb", [128, 128], BF16); onesf = sb("onesf", [128, 128])
        biasT = sb("biasT", [64, 2, 3, 4, 64]); maskc = sb("maskc", [64, 2 * NM * 3])
        flag = sb("flag", [128, 1])
        cT = sb("cT", [128, NFC, 5]); scT = sb("scT", [128, NFC, 5], BF16)
        adab = sb("adab", [128, 96]); n1w = sb("n1w", [128, NFC]); n2w = sb("n2w", [128, NFC])
        cw = sb("cw", [128, 24, 4]); fcw = sb("fcw", [128, 2 * NFF, 3]); fcb = sb("fcb", [128, 2 * NFF])
        alog = sb("alog", [128, 8]); dtb = sb("dtb", [128, 8]); sinks = sb("sinks", [128, 8])
        gnw = sb("gnw", [128, 1]); qnw = sb("qnw", [128, 1]); knw = sb("knw", [128, 1])
        modT = sb("modT", [128, 96, 5])
        A1 = sb("A1", [128, NFC, 5]); A2 = sb("A2", [128, NFC, 5])
        A1f = sb("A1f", [128, NFC]); B1f = sb("B1f", [128, NFC]); A2f = sb("A2f", [128, NFC]); B2f = sb("B2f", [128, NFC])
        esink = sb("esink", [128, 8])
        negea = sb("negea", [128, 8])

        ld = [(ident, ident_d), (U, U_d), (Lst, Lst_d), (Linc, Linc_d), (biasT, biasT_d), (maskc, maskc_d),
              (flag, flag_d), (cT, cT_d), (adab, ada_b), (n1w, n1w_d), (n2w, n2w_d), (cw, cw_d), (fcw, fcw_d),
              (fcb, fcb_d), (gnw, gnw_d), (qnw, qnw_d), (knw, knw_d)]
        for t, d_ in ld:
            self.dma("sp", t.ap[:], d_, w=[t])
        for t, d_ in [(alog, alog_d), (dtb, dtb_d), (sinks, sinks_d)]:
            self.dma("sp", t.ap[:], d_.partition_broadcast(128), w=[t])
        self.cp(identb.ap[:], ident.ap[:], r=[ident], w=[identb])
        self.memset(onesb, onesb.ap[:], 1.0)
        self.memset(onesf, onesf.ap[:], 1.0)
        self.act(scT.ap[:], cT.ap[:], AF.Silu, r=[cT], w=[scT])
        self.act(esink.ap[:], sinks.ap[:], AF.Exp, r=[sinks], w=[esink])
        self.act(negea.ap[:], alog.ap[:], AF.Exp, r=[alog], w=[negea])
        self.ts(negea.ap[:], negea.ap[:], -1.0, ALU.mult, r=[negea], w=[negea])
        self.ts(qnw.ap[:], qnw.ap[:], HD ** -0.5, ALU.mult, r=[qnw], w=[qnw])

        psA = [ps("psA%d" % i, [128, 512]) for i in range(2)]
        psT = ps("psT", [128, 512])
        psG = ps("psG", [128, 1024])
        psH = ps("psH", [128, 1024])
        psGb = Tn(psG.ap[:].bitcast(BF16), "psGb"); psGb.b = psG.b
        psG1 = Tn(psG.ap, "psG1")
        psHb = Tn(psH.ap[:].bitcast(BF16), "psHb"); psHb.b = psH.b
        psS = ps("psS", [128, 512])
        psAi = [0]

        def next_psA():
            psAi[0] ^= 1
            return psA[psAi[0]]

        WSLOT = 4096
        NWS = 4
        wslots = [sb("wslot%d" % i, [128, WSLOT], BF16) for i in range(NWS)]
        wsi = [0]

        def wload(src_ap):
            wsi[0] = (wsi[0] + 1) % NWS
            slot = wslots[wsi[0]]
            n = src_ap.shape[1] * src_ap.shape[2]
            assert n <= WSLOT
            view = slot.ap[:, 0:n].rearrange("p (a b) -> p a b", a=src_ap.shape[1])
            q = "sp" if src_ap.dtype == BF16 else "pool"
            self.S.dma(q, lambda e: e.dma_start(out=view, in_=src_ap), reads=[], writes=[slot.b])
            return slot, view

        conv_done = {}

        def convert(name, src, dst, cols=None):
            R, Cc = src.shape
            cols = cols or [(c0, min(Cc, c0 + 2048)) for c0 in range(0, Cc, 2048)]
            for r0 in range(0, R, 128):
                for (c0, c1) in cols:
                    self.S.dma("pool", lambda e, r0=r0, c0=c0, c1=c1: e.dma_start(out=dst[r0:r0 + 128, c0:c1], in_=src[r0:r0 + 128, c0:c1]))
            conv_done[name] = {sk: v for sk, v in self.S.dma_val.items() if sk[1] == "pool"}

        def need_conv(name):
            waits = []
            for sk, v in conv_done[name].items():
                if self.S.seen["sp"].get(sk, 0) < v:
                    self.S.seen["sp"][sk] = v
                    waits.append((sk, v))
            if waits:
                self.S.prog["sp"].append((waits, None, False))

        def wload_k(src2d, kps):
            nk = src2d.shape[0] // 128
            parts = []
            for k0 in range(0, nk, kps):
                k1 = min(nk, k0 + kps)
                slot, view = wload(src2d[k0 * 128:k1 * 128, :].rearrange("(kc p) n -> p kc n", p=128))
                parts.append((k0, k1, slot, view))

            def wk(k):
                for (k0, k1, slot, view) in parts:
                    if k0 <= k < k1:
                        return slot, view[:, k - k0, :]
                raise IndexError(k)
            return wk

        hT = sb("hT", [128, NFC, T], BF16)
        mixT = sb("mixT", [128, NFC, T], BF16)
        xres = sb("xres", [128, TS, D])
        stat = sb("stat", [128, 8])
        qn = sb("qn", [128, NH, T], BF16); kn = sb("kn", [128, NH, T], BF16); vT = sb("vT", [128, NH, T], BF16)
        zs = sb("zs", [128, NH, T], BF16)
        qa = sb("qa", [128, NH, T], BF16)
        kaf = sb("kaf", [128, 2, TE]); kab = sb("kab", [128, 2, TE], BF16)
        vaf = sb("vaf", [128, 2, TE])
        vtok = sb("vtok", [64, TE // 64, 2, 128], BF16)
        vtokf = sb("vtokf", [128, 2, 128])
        cacc = sb("cacc", [128, T])
        sqb = sb("sqb", [128, T], BF16)
        rsd = sb("rsd", [128, T])
        gtok = sb("gtok", [128, 2 * TS + NS, 16])
        convprev = sb("convprev", [128, 24, 3])
        uprev = sb("uprev", [128, 2 * NFF, 2])
        Sst = sb("Sst", [128, NH, 128]); Sbf = sb("Sbf", [128, NH, 128], BF16)
        f1 = sb("f1", [128, 512]); f2 = sb("f2", [128, 512]); f3 = sb("f3", [128, 512])
        Dm = sb("Dm", [128, 512])
        xsb = sb("xsb", [128, D], BF16)
        psTb = Tn(psT.ap.bitcast(BF16), "psTb"); psTb.b = psT.b
        b_ = {n: sb(n, [128, 1024], BF16) for n in ["N0", "N1", "M0", "M1", "Pm", "vb", "kbg", "kd", "vnew"]}
        for n in ["QKD", "QKDT", "qg"]:
            b_[n] = sb(n, [128, 512], BF16)
        b_["Xb"] = sb("Xb", [128, 512], BF16)
        nf = {n: Tn(b_[n].ap.bitcast(F32), n + "_f") for n in ["N0", "N1", "M0", "M1", "Pm"]}
        for n in nf:
            nf[n].b = b_[n].b
        b_["ktok"] = b_["vnew"]
        b_["nwT"] = b_["QKD"]
        sm = sb("sm", [128, 64])
        def actT_c(c):
            tn = (qn, kn, vT)[c // 8]
            return tn, tn.ap[:, c % 8, :]
        ug, uu = f3, Dm
        fa, fb = cacc, rsd
        evac = f1
        sc1 = sb("sc1", [64, 512]); sc2 = sb("sc2", [64, 256])
        pT1 = sb("pT1", [64, 512], BF16); pT2 = sb("pT2", [64, 256], BF16)
        den = f2
        dens = sb("dens", [128, 256])
        mixTg = Tn(mixT.ap, "mixTg"); mixTs = Tn(mixT.ap, "mixTs")

        self.memset(convprev, convprev.ap[:], 0.0)
        self.memset(uprev, uprev.ap[:], 0.0)
        self.memset(Sst, Sst.ap[:], 0.0)
        self.memset(Sbf, Sbf.ap[:], 0.0)
        self.memset(kaf, kaf.ap[:], 0.0)
        self.memset(vaf, vaf.ap[:], 0.0)

        convert("w_in_kv", w_in, w_in_b, [(1024, 3072), (4096, 4112)])
        for nb in range(24):
            wk = wload_k(ada_w[:, nb * 512:(nb + 1) * 512], 8)
            pt = psS
            for m in range(4):
                for kc in range(NFC):
                    slot, wap = wk(kc)
                    self.mm(pt.ap[:, m * 8:m * 8 + 5], wap[:, m * 128:(m + 1) * 128], scT.ap[:, kc, :],
                            start=(kc == 0), stop=(kc == NFC - 1), r=[slot, scT], w=[pt], inc=(kc == NFC - 1))
            for m in range(4):
                ch = nb * 4 + m
                self.ts(modT.ap[:, ch, :], pt.ap[:, m * 8:m * 8 + 5], adab.ap[:, ch:ch + 1], ALU.add, r=[pt, adab], w=[modT])
        convert("w_in", w_in, w_in_b, [(0, 1024), (3072, 4096), (4112, INW)])
        convert("w_o", w_o, w_o_b)
        convert("w_up", w_up, w_up_b)
        convert("w_down", w_down, w_down_b)
        self.stt(A1.ap[:], modT.ap[:, 16:32, :], 1.0, n1w.ap[:].unsqueeze(2).to_broadcast([128, NFC, 5]), ALU.add, ALU.mult,
                 r=[modT, n1w], w=[A1])
        self.stt(A2.ap[:], modT.ap[:, 64:80, :], 1.0, n2w.ap[:].unsqueeze(2).to_broadcast([128, NFC, 5]), ALU.add, ALU.mult,
                 r=[modT, n2w], w=[A2])
        B1 = lambda fc, s: modT.ap[:, fc, s:s + 1]
        B2 = lambda fc, s: modT.ap[:, 48 + fc, s:s + 1]
        G1 = lambda fc, s: modT.ap[:, 32 + fc, s:s + 1]
        G2 = lambda fc, s: modT.ap[:, 80 + fc, s:s + 1]
        self.ts(A1f.ap[:], A1.ap[:, :, 0], flag.ap[:, 0:1], ALU.mult, r=[A1, flag], w=[A1f])
        self.ts(B1f.ap[:], modT.ap[:, 0:16, 0], flag.ap[:, 0:1], ALU.mult, r=[modT, flag], w=[B1f])
        self.ts(A2f.ap[:], A2.ap[:, :, 0], flag.ap[:, 0:1], ALU.mult, r=[A2, flag], w=[A2f])
        self.ts(B2f.ap[:], modT.ap[:, 48:64, 0], flag.ap[:, 0:1], ALU.mult, r=[modT, flag], w=[B2f])

        def norm_transpose(xt_tn, xt_ap, nrows, col0, groups, which):
            self.act(xsb.ap[0:nrows, :], xt_ap, AF.Square, r=[xt_tn], w=[xsb, stat], accum_out=stat.ap[0:nrows, 0:1])
            self.rsqrt(stat, stat.ap[0:nrows, 0:1], 1.0 / D)
            self.ts(xsb.ap[0:nrows, :], xt_ap, stat.ap[0:nrows, 0:1], ALU.mult, r=[xt_tn, stat], w=[xsb])
            for g4 in range(4):
                for q in range(4):
                    fc = g4 * 4 + q
                    self.tr(psTb.ap[:, q * 128:q * 128 + nrows], xsb.ap[0:nrows, fc * 128:(fc + 1) * 128], identb.ap[0:nrows, 0:nrows],
                            r=[xsb, identb], w=[psTb])
                for q in range(4):
                    fc = g4 * 4 + q
                    for (c0, n, s, fl) in groups:
                        if which == 1:
                            sc = A1f.ap[:, fc:fc + 1] if fl else A1.ap[:, fc, s:s + 1]
                            bi = B1f.ap[:, fc:fc + 1] if fl else B1(fc, s)
                            rd = [psTb, A1f, B1f, A1, modT]
                        else:
                            sc = A2f.ap[:, fc:fc + 1] if fl else A2.ap[:, fc, s:s + 1]
                            bi = B2f.ap[:, fc:fc + 1] if fl else B2(fc, s)
                            rd = [psTb, A2f, B2f, A2, modT]
                        self.act(hT.ap[:, fc, col0 + c0:col0 + c0 + n], psTb.ap[:, q * 128 + c0:q * 128 + c0 + n], AF.Identity,
                                 r=rd, w=[hT], scale=sc, bias=bi)

        def proj_chunk(wk, mcol, ncols, Tn_, nk, rhs_of, rd=None):
            pt = next_psA()
            rd = [hT] if rd is None else rd
            for k in range(nk):
                slot, wap = wk(k)
                self.mm(pt.ap[0:ncols, 0:Tn_], wap[:, mcol:mcol + ncols], rhs_of(k), start=(k == 0), stop=(k == nk - 1),
                        r=[slot] + rd, w=[pt], inc=(k == nk - 1))
            return pt

        par = [0]
        PB = lambda: [(f3, cacc, rsd, sqb, psT), (Dm, f1, f2, b_["qg"], psS)][par[0] % 2]

        def head_rstd(src_ap, src_tn, Tn_, mul, bufs):
            _, _, rs, sq, pb = bufs
            self.act(sq.ap[:, 0:Tn_], src_ap, AF.Square, r=[src_tn], w=[sq])
            self.mm(pb.ap[:, 0:Tn_], onesb.ap[:], sq.ap[:, 0:Tn_], r=[onesb, sq], w=[pb])
            self.ts(rs.ap[:, 0:Tn_], pb.ap[:, 0:Tn_], mul, ALU.mult, EPS, ALU.add, r=[pb], w=[rs])
            self.act(rs.ap[:, 0:Tn_], rs.ap[:, 0:Tn_], AF.Sqrt, r=[rs], w=[rs])
            self.recip(rs.ap[:, 0:Tn_], rs.ap[:, 0:Tn_], r=[rs], w=[rs])

        def conv_silu(pt, ch, Tn_, segs, dst_ap, dst_tn, norm):
            par[0] += 1
            bufs = PB()
            ext, acc, rs = bufs[0], bufs[1], bufs[2]
            for (c0, n, prv, sav) in segs:
                self.cp(ext.ap[:, 0:3], prv[0], r=[prv[1]], w=[ext])
                self.cp(ext.ap[:, 3:3 + n], pt.ap[:, c0:c0 + n], r=[pt], w=[ext], eng="act")
                self.cp(sav[0], ext.ap[:, n:n + 3], r=[ext], w=[sav[1]])
                self.ts(acc.ap[:, c0:c0 + n], ext.ap[:, 0:n], cw.ap[:, ch, 0:1], ALU.mult, r=[ext, cw], w=[acc])
                for tp in range(1, 4):
                    self.stt(acc.ap[:, c0:c0 + n], ext.ap[:, tp:tp + n], cw.ap[:, ch, tp:tp + 1], acc.ap[:, c0:c0 + n],
                             ALU.mult, ALU.add, r=[ext, cw, acc], w=[acc])
            self.act(acc.ap[:, 0:Tn_], acc.ap[:, 0:Tn_], AF.Silu, r=[acc], w=[acc])
            if norm is None:
                self.cp(dst_ap, acc.ap[:, 0:Tn_], r=[acc], w=[dst_tn], eng="act")
            else:
                head_rstd(acc.ap[:, 0:Tn_], acc, Tn_, 1.0, bufs)
                self.stt(dst_ap, acc.ap[:, 0:Tn_], norm, rs.ap[:, 0:Tn_], ALU.mult, ALU.mult, r=[acc, rs], w=[dst_tn])

        def rms_head(pt, Tn_, wcol, dst_ap, dst_tn):
            par[0] += 1
            bufs = PB()
            acc, rs = bufs[1], bufs[2]
            self.cp(acc.ap[:, 0:Tn_], pt.ap[:, 0:Tn_], r=[pt], w=[acc], eng="act")
            head_rstd(acc.ap[:, 0:Tn_], acc, Tn_, 1.0 / HD, bufs)
            self.stt(dst_ap, acc.ap[:, 0:Tn_], wcol, rs.ap[:, 0:Tn_], ALU.mult, ALU.mult, r=[acc, rs, qnw, knw], w=[dst_tn])

        def gdn_chunk(C, cols, gcol, want_out):
            c0 = cols
            g = gtok.ap[0:C, gcol, 0:8]
            beta = gtok.ap[0:C, gcol, 8:16]
            HC = NH * C
            v3 = lambda t, n=C: t.ap[0:n, 0:NH * 128].rearrange("p (h d) -> p h d", h=NH)
            c3 = lambda t, n=C: t.ap[0:n, 0:HC].rearrange("p (h c) -> p h c", h=NH)
            d3 = lambda t: t.ap[:, 0:HC].rearrange("p (h c) -> p h c", h=NH)
            banks = [(h0, min(h0 + max(1, 512 // C), NH)) for h0 in range(0, NH, max(1, 512 // C))] if C * NH > 512 else [(0, NH)]
            self.mm(psS.ap[0:C, 0:8], U.ap[0:C, 0:C], g, r=[U, gtok], w=[psS])
            self.mm(psS.ap[:, 8:16], onesf.ap[0:C, :], g, r=[onesf, gtok], w=[psS])
            self.cp(sm.ap[0:C, 0:8], psS.ap[0:C, 0:8], r=[psS], w=[sm])
            self.act(sm.ap[0:C, 8:16], psS.ap[0:C, 0:8], AF.Exp, r=[psS], w=[sm])
            self.tt(sm.ap[0:C, 16:24], psS.ap[0:C, 8:16], sm.ap[0:C, 0:8], ALU.subtract, r=[psS, sm], w=[sm])
            self.act(sm.ap[0:C, 16:24], sm.ap[0:C, 16:24], AF.Exp, r=[sm], w=[sm])
            self.tt(sm.ap[0:C, 24:32], beta, sm.ap[0:C, 8:16], ALU.mult, r=[gtok, sm], w=[sm])
            self.act(sm.ap[:, 32:40], psS.ap[:, 8:16], AF.Exp, r=[psS], w=[sm])
            self.ts(sm.ap[0:C, 40:48], beta, -1.0, ALU.mult, r=[gtok], w=[sm])
            yield
            self.tt(c3(f1), g.unsqueeze(2).to_broadcast([C, NH, C]), Lst.ap[0:C, 0:C].unsqueeze(1).to_broadcast([C, NH, C]),
                    ALU.mult, r=[gtok, Lst], w=[f1])
            for (h0, h1) in banks:
                self.mm(psG.ap[0:C, h0 * C:h1 * C], U.ap[0:C, 0:C], f1.ap[0:C, h0 * C:h1 * C], r=[U, f1], w=[psG])
            self.act(Dm.ap[0:C, 0:HC], psG.ap[0:C, 0:HC], AF.Exp, r=[psG], w=[Dm])
            yield
            for h in range(NH):
                self.tr(psHb.ap[0:C, h * 128:(h + 1) * 128], kn.ap[:, h, c0:c0 + C], identb.ap[:], r=[kn, identb], w=[psHb])
            self.cp(v3(b_["ktok"]), psHb.ap[0:C, 0:1024].rearrange("p (h d) -> p h d", h=NH), r=[psHb], w=[b_["ktok"]])
            for h in range(NH):
                self.tr(psHb.ap[0:C, h * 128:(h + 1) * 128], vT.ap[:, h, c0:c0 + C], identb.ap[:], r=[vT, identb], w=[psHb])
            self.tt(v3(b_["vb"]), psHb.ap[0:C, 0:1024].rearrange("p (h d) -> p h d", h=NH),
                    beta.unsqueeze(2).to_broadcast([C, NH, 128]), ALU.mult, r=[psHb, gtok], w=[b_["vb"]])
            self.tt(v3(b_["kbg"]), v3(b_["ktok"]), sm.ap[0:C, 24:32].unsqueeze(2).to_broadcast([C, NH, 128]), ALU.mult,
                    r=[b_["ktok"], sm], w=[b_["kbg"]])
            self.tt(v3(b_["kd"]), v3(b_["ktok"]), sm.ap[0:C, 16:24].unsqueeze(2).to_broadcast([C, NH, 128]), ALU.mult,
                    r=[b_["ktok"], sm], w=[b_["kd"]], eng="pool")
            yield
            for h in range(NH):
                self.mm(psG.ap[0:C, h * C:(h + 1) * C], kn.ap[:, h, c0:c0 + C], kn.ap[:, h, c0:c0 + C], r=[kn], w=[psG])
            self.tt(f2.ap[0:C, 0:HC], psG.ap[0:C, 0:HC], Dm.ap[0:C, 0:HC], ALU.mult, r=[psG, Dm], w=[f2])
            self.tt(c3(f3), sm.ap[0:C, 40:48].unsqueeze(2).to_broadcast([C, NH, C]),
                    Lst.ap[0:C, 0:C].unsqueeze(1).to_broadcast([C, NH, C]), ALU.mult, r=[sm, Lst], w=[f3], eng="pool")
            self.tt(nf["N0"].ap[0:C, 0:HC], f2.ap[0:C, 0:HC], f3.ap[0:C, 0:HC], ALU.mult, r=[f2, f3], w=[nf["N0"]])
            if want_out:
                for h in range(NH):
                    self.mm(psG.ap[0:C, h * C:(h + 1) * C], qn.ap[:, h, c0:c0 + C], kn.ap[:, h, c0:c0 + C], r=[qn, kn], w=[psG])
                self.tt(f2.ap[0:C, 0:HC], psG.ap[0:C, 0:HC], Dm.ap[0:C, 0:HC], ALU.mult, r=[psG, Dm], w=[f2])
                self.tt(c3(b_["QKD"]), c3(f2), Linc.ap[0:C, 0:C].unsqueeze(1).to_broadcast([C, NH, C]), ALU.mult,
                        r=[f2, Linc], w=[b_["QKD"]])
                for h in range(NH):
                    self.tr(psHb.ap[0:C, h * C:(h + 1) * C], b_["QKD"].ap[0:C, h * C:(h + 1) * C], identb.ap[0:C, 0:C],
                            r=[b_["QKD"], identb], w=[psHb])
                self.cp(b_["QKDT"].ap[0:C, 0:HC], psHb.ap[0:C, 0:HC], r=[psHb], w=[b_["QKDT"]], eng="act")
                for h in range(NH):
                    self.mm(psH.ap[:, h * C:(h + 1) * C] if C * NH <= 1024 else None, g[:, h:h + 1].to_broadcast([C, 128]),
                            U.ap[0:C, 0:C], r=[gtok, U], w=[psH])
                self.act(f2.ap[:, 0:HC], psH.ap[:, 0:HC], AF.Exp, r=[psH], w=[f2])
                self.tt(d3(b_["qg"]), qn.ap[:, :, c0:c0 + C], d3(f2), ALU.mult, r=[qn, f2], w=[b_["qg"]])
            yield
            for h in range(NH):
                self.tr(psH.ap[0:C, h * C:(h + 1) * C], nf["N0"].ap[0:C, h * C:(h + 1) * C], ident.ap[0:C, 0:C],
                        r=[nf["N0"], ident], w=[psH])
            self.cp(nf["M0"].ap[0:C, 0:HC], psH.ap[0:C, 0:HC], r=[psH], w=[nf["M0"]], eng="act")
            yield
            c3f = lambda t: t.ap[0:C, 0:HC].rearrange("p (h c) -> p h c", h=NH)
            self.tt(c3f(nf["Pm"]), c3f(nf["M0"]), ident.ap[0:C, 0:C].unsqueeze(1).to_broadcast([C, NH, C]), ALU.add,
                    r=[nf["M0"], ident], w=[nf["Pm"]])
            nlev = int(np.log2(C)) - 1
            Nc, Mc, Nn, Mn = nf["N0"], nf["M0"], nf["N1"], nf["M1"]
            for lev in range(nlev):
                for h in range(NH):
                    sl = slice(h * C, (h + 1) * C)
                    self.mm(psG.ap[0:C, sl], Mc.ap[0:C, sl], Nc.ap[0:C, sl], r=[Mc, Nc], w=[psG])
                self.cp(Nn.ap[0:C, 0:HC], psG.ap[0:C, 0:HC], r=[psG], w=[Nn], eng="act")
                yield
                if lev < nlev - 1:
                    for h in range(NH):
                        sl = slice(h * C, (h + 1) * C)
                        self.mm(psH.ap[0:C, sl], Nc.ap[0:C, sl], Mc.ap[0:C, sl], r=[Mc, Nc], w=[psH])
                    self.cp(Mn.ap[0:C, 0:HC], psH.ap[0:C, 0:HC], r=[psH], w=[Mn])
                    yield
                for h in range(NH):
                    sl = slice(h * C, (h + 1) * C)
                    self.mm(psG.ap[0:C, 512 + h * C:512 + (h + 1) * C], Nn.ap[0:C, sl], nf["Pm"].ap[0:C, sl],
                            r=[Nn, nf["Pm"]], w=[psG1])
                self.tt(nf["Pm"].ap[0:C, 0:HC], nf["Pm"].ap[0:C, 0:HC], psG.ap[0:C, 512:512 + HC], ALU.add, r=[psG1, nf["Pm"]], w=[nf["Pm"]])
                Nc, Nn = Nn, Nc
                Mc, Mn = Mn, Mc
                yield
            X = b_["Xb"]
            self.cp(X.ap[0:C, 0:HC], nf["Pm"].ap[0:C, 0:HC], r=[nf["Pm"]], w=[X], eng="pool")
            yield
            for h in range(NH):
                self.mm(psH.ap[:, h * C:(h + 1) * C], b_["kbg"].ap[0:C, h * 128:(h + 1) * 128], X.ap[0:C, h * C:(h + 1) * C],
                        r=[b_["kbg"], X], w=[psH])
            self.act(b_["nwT"].ap[:, 0:HC], psH.ap[:, 0:HC], AF.Copy, r=[psH], w=[b_["nwT"]], scale=-1.0)
            yield
            for h in range(NH):
                sl = slice(h * 128, (h + 1) * 128)
                self.mm(psG.ap[0:C, sl], X.ap[0:C, h * C:(h + 1) * C], b_["vb"].ap[0:C, sl], start=True, stop=False,
                        r=[X, b_["vb"]], w=[psG, psG1], inc=False)
                self.mm(psG.ap[0:C, sl], b_["nwT"].ap[:, h * C:(h + 1) * C], Sbf.ap[:, h, :], start=False, stop=True,
                        r=[b_["nwT"], Sbf], w=[psG, psG1])
            self.cp(b_["vnew"].ap[0:C, :], psG.ap[0:C, :], r=[psG, psG1], w=[b_["vnew"]], eng="act")
            yield
            if want_out:
                for h in range(NH):
                    self.mm(psH.ap[:, h * C:(h + 1) * C], Sbf.ap[:, h, :], b_["qg"].ap[:, h * C:(h + 1) * C], start=True, stop=False,
                            r=[Sbf, b_["qg"]], w=[psH], inc=False)
                    self.mm(psH.ap[:, h * C:(h + 1) * C], b_["vnew"].ap[0:C, h * 128:(h + 1) * 128],
                            b_["QKDT"].ap[0:C, h * C:(h + 1) * C], start=False, stop=True, r=[b_["vnew"], b_["QKDT"]], w=[psH])
                self.cp(f1.ap[:, 0:HC], psH.ap[:, 0:HC], r=[psH], w=[f1], eng="act")
            yield
            for h in range(NH):
                sl = slice(h * 128, (h + 1) * 128)
                self.mm(psG.ap[:, sl], b_["kd"].ap[0:C, sl], b_["vnew"].ap[0:C, sl], r=[b_["kd"], b_["vnew"]], w=[psG, psG1])
            self.tt(Sst.ap[:], Sst.ap[:], sm.ap[:, 32:40].unsqueeze(2).to_broadcast([128, NH, 128]), ALU.mult, r=[Sst, sm], w=[Sst])
            self.tt(Sst.ap[:], Sst.ap[:], psG.ap[:, :].rearrange("p (h d) -> p h d", h=NH), ALU.add, r=[Sst, psG, psG1], w=[Sst])
            self.cp(Sbf.ap[:], Sst.ap[:], r=[Sst], w=[Sbf], eng="act")
            yield
            if want_out:
                self.act(b_["N1"].ap[:, 0:HC], f1.ap[:, 0:HC], AF.Square, r=[f1], w=[b_["N1"]])
                for (h0, h1) in banks:
                    self.mm(psH.ap[:, h0 * C:h1 * C], onesb.ap[:], b_["N1"].ap[:, h0 * C:h1 * C], r=[onesb, b_["N1"]], w=[psH])
                self.ts(f2.ap[:, 0:HC], psH.ap[:, 0:HC], 1.0 / HD, ALU.mult, EPS, ALU.add, r=[psH], w=[f2])
                self.act(f2.ap[:, 0:HC], f2.ap[:, 0:HC], AF.Sqrt, r=[f2], w=[f2])
                self.recip(f2.ap[:, 0:HC], f2.ap[:, 0:HC], r=[f2], w=[f2])
                self.tt(f1.ap[:, 0:HC], f1.ap[:, 0:HC], f2.ap[:, 0:HC], ALU.mult, r=[f1, f2], w=[f1])
                self.stt(mixT.ap[:, 0:NH, c0:c0 + C], d3(f1), gnw.ap[:, 0:1], zs.ap[:, :, c0:c0 + C], ALU.mult, ALU.mult,
                         r=[f1, gnw, zs], w=[mixTg])

        def swa_chunk(qcol, Cq, kblocks, mcols):
            for g in range(2):
                NQ = 4 * Cq
                rhs_q = qa.ap[:, g * 4:(g + 1) * 4, qcol:qcol + Cq]
                dsts = [(psA[0], sc1, pT1, 0), (psA[0], sc1, pT1, 256), (psA[1], sc2, pT2, 0)]
                for bi, (ec, nk, vfn) in enumerate(kblocks):
                    pt, sc, pTt, off = dsts[bi]
                    self.mm(pt.ap[0:nk, off:off + NQ], kab.ap[:, g, ec:ec + nk], rhs_q, r=[kab, qa], w=[pt])
                yield
                for bi, (ec, nk, vfn) in enumerate(kblocks):
                    pt, sc, pTt, off = dsts[bi]
                    bias_ap = biasT.ap[0:nk, g, bi, :, 0:Cq]
                    o3 = sc.ap[0:nk, off:off + NQ].rearrange("p (h c) -> p h c", h=4)
                    i3 = pt.ap[0:nk, off:off + NQ].rearrange("p (h c) -> p h c", h=4)
                    if mcols is None:
                        self.tt(o3, i3, bias_ap, ALU.add, r=[pt, biasT], w=[sc])
                    else:
                        mc = mcols[bi]
                        self.stt(o3, i3, maskc.ap[0:nk, mc:mc + 1], bias_ap, ALU.add, ALU.add, r=[pt, biasT, maskc], w=[sc])
                    self.act(pTt.ap[0:nk, off:off + NQ], sc.ap[0:nk, off:off + NQ], AF.Exp, r=[sc], w=[pTt])
                    yield
                for bi, (ec, nk, vfn) in enumerate(kblocks):
                    pt, sc, pTt, off = dsts[bi]
                    self.mm(psT.ap[:, 0:NQ], vfn(g), pTt.ap[0:nk, off:off + NQ], start=(bi == 0), stop=(bi == 2),
                            r=[vtok, pTt], w=[psT], inc=(bi == 2))
                for bi, (ec, nk, vfn) in enumerate(kblocks):
                    pt, sc, pTt, off = dsts[bi]
                    self.mm(psT.ap[:, 256:256 + NQ], onesb.ap[0:nk, :], pTt.ap[0:nk, off:off + NQ], start=(bi == 0), stop=(bi == 2),
                            r=[onesb, pTt], w=[psT], inc=(bi == 2))
                yield
                self.tt(dens.ap[:, 0:NQ].rearrange("p (h c) -> p h c", h=4), psT.ap[:, 256:256 + NQ].rearrange("p (h c) -> p h c", h=4),
                        esink.ap[:, g * 4:(g + 1) * 4].unsqueeze(2).to_broadcast([128, 4, Cq]), ALU.add, r=[psT, esink], w=[dens])
                self.recip(dens.ap[:, 0:NQ], dens.ap[:, 0:NQ], r=[dens], w=[dens])
                yield
                self.tt(mixT.ap[:, 8 + g * 4:8 + (g + 1) * 4, qcol:qcol + Cq], psT.ap[:, 0:NQ].rearrange("p (h c) -> p h c", h=4),
                        dens.ap[:, 0:NQ].rearrange("p (h c) -> p h c", h=4), ALU.mult, r=[psT, dens], w=[mixTs])
                yield

        def interleave(ga, gb, ratio):
            da = db = False
            while not (da and db):
                for _ in range(ratio):
                    if not da:
                        try:
                            next(ga)
                        except StopIteration:
                            da = True
                if not db:
                    try:
                        next(gb)
                    except StopIteration:
                        db = True

        def chain(gens):
            for g_ in gens:
                yield from g_

        def run_pipeline(tasks):
            starts = [i for i, t in enumerate(tasks) if t[0] is not None]
            done = set()
            pend = None
            for i, (ld, pj, post) in enumerate(tasks):
                if ld is not None:
                    if i not in done:
                        ld(); done.add(i)
                    nxt = [j for j in starts if j > i]
                    if nxt and nxt[0] not in done:
                        tasks[nxt[0]][0](); done.add(nxt[0])
                pt = pj()
                if pend is not None:
                    pend[1](pend[0])
                pend = (pt, post)
            if pend is not None:
                pend[1](pend[0])

        def gates_mm(wk_ab, col0, n, go):
            pt = psS
            for k in range(NFC):
                wslot, wap = wk_ab(k)
                self.mm(pt.ap[0:n, go:go + 16], hT.ap[:, k, col0:col0 + n], wap[:, 0:16], start=(k == 0), stop=(k == NFC - 1),
                        r=[hT, wslot], w=[pt], inc=(k == NFC - 1))
            return pt

        def gates_post(pt, n, gcol, go):
            x_ = sm.ap[0:n, 48:56]
            t_ = sm.ap[0:n, 56:64]
            self.tt(x_, pt.ap[0:n, go:go + 8], dtb.ap[0:n, :], ALU.add, r=[pt, dtb], w=[sm])
            self.ts(t_, x_, -1.0, ALU.mult, r=[sm], w=[sm])
            self.tt(t_, t_, x_, ALU.min, r=[sm], w=[sm])
            self.act(t_, t_, AF.Exp, r=[sm], w=[sm])
            self.act(t_, t_, AF.Ln, r=[sm], w=[sm], bias=1.0)
            self.stt(x_, x_, 0.0, t_, ALU.max, ALU.add, r=[sm], w=[sm])
            self.tt(gtok.ap[0:n, gcol, 0:8], x_, negea.ap[0:n, :], ALU.mult, r=[sm, negea], w=[gtok])
            self.act(gtok.ap[0:n, gcol, 8:16], pt.ap[0:n, go + 8:go + 16], AF.Sigmoid, r=[pt], w=[gtok])

        QKV0, Z0, AB0, QA0, KA0, VA0 = 0, 3072, 4096, 4112, 5136, 5392

        def w_in_block(c0, ncols):
            need_conv("w_in_kv" if (1024 <= c0 < 3072 or c0 == 4096) else "w_in")
            return wload_k(w_in_b[:, c0:c0 + ncols], 8 if ncols > 256 else 16)

        def inproj(Tn_, segs_conv, need_q, need_z, need_swa, kvcol0, gate_list):
            hrhs = lambda k: hT.ap[:, k, 0:Tn_]
            tasks = []

            def add_block(c0, ncols, chunks):
                holder = {}

                def load():
                    holder["wk"] = w_in_block(c0, ncols)
                for i, (mcol, post) in enumerate(chunks):
                    tasks.append((load if i == 0 else None,
                                  (lambda mcol=mcol: proj_chunk(holder["wk"], mcol, 128, Tn_, NFC, hrhs)), post))
                return holder

            def qkv_post(ch):
                which, h = ch // 8, ch % 8
                if which == 0:
                    return lambda pt: conv_silu(pt, ch, Tn_, segs_conv(ch), qn.ap[:, h, 0:Tn_], qn, HD ** -0.5)
                if which == 1:
                    return lambda pt: conv_silu(pt, ch, Tn_, segs_conv(ch), kn.ap[:, h, 0:Tn_], kn, 1.0)
                return lambda pt: conv_silu(pt, ch, Tn_, segs_conv(ch), vT.ap[:, h, 0:Tn_], vT, None)
            for blk in range(6):
                if blk < 2 and not need_q:
                    continue
                add_block(QKV0 + blk * 512, 512, [(m * 128, qkv_post(blk * 4 + m)) for m in range(4)])
            if need_z:
                for blk in range(2):
                    add_block(Z0 + blk * 512, 512,
                              [(m * 128, (lambda pt, c=blk * 4 + m: self.act(zs.ap[:, c, 0:Tn_], pt.ap[:, 0:Tn_], AF.Silu, r=[pt], w=[zs])))
                               for m in range(4)])
            hab = {}

            def load_ab():
                hab["wk"] = w_in_block(AB0, 16)
            for gi, (col0, n, gcol) in enumerate(gate_list):
                go = 16 + 16 * (gi % 2)
                tasks.append((load_ab if gi == 0 else None, (lambda col0=col0, n=n, go=go: gates_mm(hab["wk"], col0, n, go)),
                              (lambda pt, n=n, gcol=gcol, go=go: gates_post(pt, n, gcol, go))))
            if need_swa:
                for blk in range(2):
                    add_block(QA0 + blk * 512, 512,
                              [(m * 128, (lambda pt, c=blk * 4 + m: rms_head(pt, Tn_, qnw.ap[:, 0:1], qa.ap[:, c, 0:Tn_], qa)))
                               for m in range(4)])
            if kvcol0 is not None:
                def k_post(m):
                    def f(pt):
                        rms_head(pt, Tn_, knw.ap[:, 0:1], kaf.ap[:, m, kvcol0:kvcol0 + Tn_], kaf)
                        self.cp(kab.ap[:, m, kvcol0:kvcol0 + Tn_], kaf.ap[:, m, kvcol0:kvcol0 + Tn_], r=[kaf], w=[kab], eng="act")
                    return f
                add_block(KA0, 512, [(m * 128, k_post(m)) for m in range(2)] +
                          [(256 + m * 128, (lambda pt, m=m: self.cp(vaf.ap[:, m, kvcol0:kvcol0 + Tn_], pt.ap[:, 0:Tn_], r=[pt], w=[vaf], eng="act")))
                           for m in range(2)])
            run_pipeline(tasks)

        def make_vtok(ext_c0, n, blk_idx, rows0=0):
            for g in range(2):
                self.tr(psT.ap[0:n, g * 128:(g + 1) * 128], vaf.ap[:, g, ext_c0:ext_c0 + n], ident.ap[:], r=[vaf, ident], w=[psT])
            self.cp(vtok.ap[rows0:rows0 + n, blk_idx, :, :], psT.ap[0:n, 0:256].rearrange("p (g d) -> p g d", g=2), r=[psT], w=[vtok])

        def proj_residual(wdram, nk, rhs_of, rhs_tn, Tn_, gate_groups, xtiles, bcols, kps):
            tasks = []
            for mb in range(D // bcols):
                holder = {}

                def load(mb=mb, holder=holder):
                    holder["wk"] = wload_k(wdram[:, mb * bcols:(mb + 1) * bcols], kps)
                for m in range(bcols // 128):
                    fc = mb * (bcols // 128) + m

                    def post(pt, fc=fc):
                        for (c0, n, s, gfn) in gate_groups:
                            self.act(evac.ap[:, c0:c0 + n], pt.ap[:, c0:c0 + n], AF.Copy, r=[pt, modT], w=[evac], scale=gfn(fc, s))
                        for (col0, nrows, x_ap, x_tn) in xtiles:
                            self.tr(psT.ap[0:nrows, 0:128], evac.ap[:, col0:col0 + nrows], ident.ap[:], r=[evac, ident], w=[psT])
                            self.tt(x_ap[:, fc * 128:(fc + 1) * 128], x_ap[:, fc * 128:(fc + 1) * 128], psT.ap[0:nrows, 0:128], ALU.add,
                                    r=[psT, x_tn], w=[x_tn])
                    tasks.append((load if m == 0 else None,
                                  (lambda m=m, holder=holder: proj_chunk(holder["wk"], m * 128, 128, Tn_, nk, rhs_of, rd=rhs_tn)), post))
            run_pipeline(tasks)

        def ffn_conv(pt, ch, ext, acc, segs_ffn):
            for (c0, n, prev_fn, save_fn) in segs_ffn:
                pa, ptn = prev_fn(ch)
                self.cp(ext.ap[:, 0:2], pa, r=[ptn], w=[ext])
                self.cp(ext.ap[:, 2:2 + n], pt.ap[:, c0:c0 + n], r=[pt], w=[ext], eng="act")
                if save_fn is not None:
                    sa, stn = save_fn(ch)
                    self.cp(sa, ext.ap[:, n:n + 2], r=[ext], w=[stn])
                self.act(acc.ap[:, c0:c0 + n], ext.ap[:, 0:n], AF.Identity, r=[ext, fcw, fcb], w=[acc],
                         scale=fcw.ap[:, ch, 0:1], bias=fcb.ap[:, ch:ch + 1])
                for tp in (1, 2):
                    self.stt(acc.ap[:, c0:c0 + n], ext.ap[:, tp:tp + n], fcw.ap[:, ch, tp:tp + 1], acc.ap[:, c0:c0 + n],
                             ALU.mult, ALU.add, r=[ext, fcw, acc], w=[acc])

        def ffn(Tn_, segs_ffn, gate_groups, xtiles):
            hrhs = lambda k: hT.ap[:, k, 0:Tn_]
            cc = 0
            for (c_lo, c_hi) in ((0, 24), (24, NFF)):
                for blk in range(c_lo, c_hi, 4):
                    need_conv("w_up")
                    wk = wload_k(w_up_b[:, blk * 128:blk * 128 + 512], 8)
                    for m in range(4):
                        ch = blk + m
                        cc += 1
                        ext, acc = [(ug, fa), (uu, fb)][cc % 2]
                        pt = proj_chunk(wk, m * 128, 128, Tn_, NFC, hrhs)
                        ffn_conv(pt, ch, ext, acc, segs_ffn)
                        atn, aap = actT_c(ch - c_lo)
                        self.act(aap[:, 0:Tn_], acc.ap[:, 0:Tn_], AF.Silu, r=[acc], w=[atn])
                    wk = wload_k(w_up_b[:, DFF + blk * 128:DFF + blk * 128 + 512], 8)
                    for m in range(4):
                        ch = blk + m
                        cc += 1
                        ext, acc = [(ug, fa), (uu, fb)][cc % 2]
                        pt = proj_chunk(wk, m * 128, 128, Tn_, NFC, hrhs)
                        ffn_conv(pt, NFF + ch, ext, acc, segs_ffn)
                        atn, aap = actT_c(ch - c_lo)
                        self.tt(aap[:, 0:Tn_], aap[:, 0:Tn_], acc.ap[:, 0:Tn_], ALU.mult, r=[atn, acc], w=[atn])
                nk = c_hi - c_lo
                need_conv("w_down")
                proj_residual(w_down_b[c_lo * 128:c_hi * 128, :], nk, lambda k: actT_c(k)[1][:, 0:Tn_], [qn, kn, vT], Tn_,
                              gate_groups, xtiles, 256, nk // 2)

        xstage = xres
        n_pst = (NP + TS - 1) // TS
        for st in range(n_pst):
            t0 = st * TS
            nt = min(TS, NP - t0)
            Tn_ = nt * 128
            last = (st == n_pst - 1)
            for j in range(nt):
                self.dma("sp", xstage.ap[:, j, :], xp[(t0 + j) * 128:(t0 + j + 1) * 128, :], w=[xstage])
                norm_transpose(xstage, xstage.ap[:, j, :], 128, j * 128, [(0, 128, 0, True)], 1)
            cp_t = (convprev,)
            segs_conv = lambda ch: [(0, Tn_, (convprev.ap[:, ch, :], convprev), (convprev.ap[:, ch, :], convprev))]
            inproj(Tn_, segs_conv, need_q=last, need_z=False, need_swa=False,
                   kvcol0=(128 - Tn_ + 0 if last else None) if False else (0 if last else None),
                   gate_list=[(j * 64, 64, j) for j in range(2 * nt)])
            if last:
                if Tn_ > 128:
                    for g in range(2):
                        self.cp(f1.ap[:, 0:128], kaf.ap[:, g, Tn_ - 128:Tn_], r=[kaf], w=[f1])
                        self.cp(kaf.ap[:, g, 0:128], f1.ap[:, 0:128], r=[f1], w=[kaf])
                        self.cp(f1.ap[:, 0:128], vaf.ap[:, g, Tn_ - 128:Tn_], r=[vaf], w=[f1])
                        self.cp(vaf.ap[:, g, 0:128], f1.ap[:, 0:128], r=[f1], w=[vaf])
                    self.cp(kab.ap[:, :, 0:128], kaf.ap[:, :, 0:128], r=[kaf], w=[kab])
            for j in range(2 * nt):
                for _ in gdn_chunk(64, j * 64, j, want_out=False):
                    pass

        n_mst = (NM + TS - 1) // TS
        assert NM - (n_mst - 1) * TS < TS, "last main super-tile needs a free tile slot for the sample tokens"
        out_events = []
        TSM = NS * LS
        sconv = sb("sconv", [128, NS, 24, 3]); sffn = sb("sffn", [128, NS, 2 * NFF, 2])
        sconv_o = sb("sconv_o", [128, NS, 24, 3]); sffn_o = sb("sffn_o", [128, NS, 2 * NFF, 2])
        scv = sb("scv", [64, 2, 2, 128])
        scvb = sb("scvb", [64, 2, 2, 128], BF16)
        skT = sb("skT", [128, 2, 128])
        kout = sb("kout", [128, 2, 128])
        for st in range(n_mst):
            t0 = st * TS
            nt = min(TS, NM - t0)
            Tm = nt * 128
            smp = (st == n_mst - 1)
            Tn_ = Tm + (TSM if smp else 0)
            grp = [(s * LS, LS, 1 + s, False) for s in range(NS)]
            for j in range(nt):
                self.dma("sp", xres.ap[:, j, :], xm[(t0 + j) * 128:(t0 + j + 1) * 128, :], w=[xres])
                norm_transpose(xres, xres.ap[:, j, :], 128, j * 128, [(0, 128, 0, (t0 + j) == 0)], 1)
            if smp:
                for s in range(NS):
                    self.dma("sp", sconv.ap[:, s], s_conv_d[s], w=[sconv])
                    self.dma("sp", sffn.ap[:, s], s_ffn_d[s], w=[sffn])
                self.dma("sp", xres.ap[0:TSM, nt, :], xs, w=[xres])
                norm_transpose(xres, xres.ap[0:TSM, nt, :], TSM, Tm, grp, 1)

            def segs_conv(ch, Tm=Tm, smp=smp):
                sg = [(0, Tm, (convprev.ap[:, ch, :], convprev), (convprev.ap[:, ch, :], convprev))]
                if smp:
                    sg += [(Tm + s * LS, LS, (sconv.ap[:, s, ch, :], sconv), (sconv_o.ap[:, s, ch, :], sconv_o)) for s in range(NS)]
                return sg
            gl_ = [(j * 64, 64, j) for j in range(2 * nt)]
            if smp:
                gl_ += [(Tm + s * LS, LS, 2 * nt + s) for s in range(NS)]
            inproj(Tn_, segs_conv, need_q=True, need_z=True, need_swa=True, kvcol0=128, gate_list=gl_)
            for bk in range((128 + Tm) // 64):
                make_vtok(bk * 64, 64, bk)
            swa_gens = []
            for cq in range(Tm // 64):
                gch = t0 * 2 + cq
                kbl = [((cq + r_) * 64, 64, (lambda g, b=cq + r_: vtok.ap[0:64, b, g, :])) for r_ in range(3)]
                mcols = [gch * 3 + r_ for r_ in range(3)] if gch < 4 else None
                swa_gens.append(swa_chunk(cq * 64, 64, kbl, mcols))
            interleave(chain([gdn_chunk(64, j * 64, j, want_out=True) for j in range(2 * nt)]), chain(swa_gens), 3)
            if smp:
                out_events.append(self.dma("sp", o_kT, kaf.ap[:, :, Tm:Tm + 128], r=[kaf]))
                for g in range(2):
                    self.tr(psT.ap[:, g * 128:(g + 1) * 128], vaf.ap[:, g, Tm:Tm + 128], ident.ap[:], r=[vaf, ident], w=[psT])
                self.cp(vtokf.ap[:], psT.ap[:, 0:256].rearrange("p (g d) -> p g d", g=2), r=[psT], w=[vtokf])
                out_events.append(self.dma("sp", o_v, vtokf.ap[:], r=[vtokf]))
                out_events.append(self.dma("sp", o_conv, convprev.ap[:], r=[convprev]))
                out_events.append(self.dma("sp", o_delta.rearrange("h k v -> k h v"), Sst.ap[:], r=[Sst]))
                for s in range(NS):
                    sc0 = Tm + s * LS
                    self.dma("sp", scv.ap[:], s_v_d[s].rearrange("(b p) g d -> p b g d", p=64), w=[scv])
                    self.dma("sp", skT.ap[:], s_kT_d[s], w=[skT])
                    self.cp(scvb.ap[:], scv.ap[:], r=[scv], w=[scvb], eng="pool")
                    self.dma("sp", Sst.ap[:], s_delta_d[s].rearrange("h k v -> k h v"), w=[Sst])
                    self.cp(Sbf.ap[:], Sst.ap[:], r=[Sst], w=[Sbf], eng="act")
                    for _ in gdn_chunk(LS, sc0, 2 * nt + s, want_out=True):
                        pass
                    out_events.append(self.dma("sp", os_delta[s].rearrange("h k v -> k h v"), Sst.ap[:], r=[Sst]))
                    self.cp(kab.ap[:, :, 0:128], skT.ap[:], r=[skT], w=[kab])
                    make_vtok(128 + sc0, LS, TE // 64 - 1)
                    self._swa_sample(sc0, 128 + sc0, TE // 64 - 1, scvb, vtok, qa, kab, psS, psT, psG, psH, sc1, sc2, pT1, pT2,
                                     biasT, onesb, esink, den, mixT)
                    self.cp(kout.ap[:, :, 0:112], skT.ap[:, :, 16:128], r=[skT], w=[kout])
                    self.cp(kout.ap[:, :, 112:128], kaf.ap[:, :, 128 + sc0:128 + sc0 + LS], r=[kaf], w=[kout])
                    out_events.append(self.dma("sp", os_kT[s], kout.ap[:], r=[kout]))
                    out_events.append(self.dma("sp", os_v[s, 0:112], s_v_d[s, 16:128]))
                    for g in range(2):
                        self.tr(psT.ap[0:LS, g * 128:(g + 1) * 128], vaf.ap[:, g, 128 + sc0:128 + sc0 + LS], ident.ap[:],
                                r=[vaf, ident], w=[psT])
                    self.cp(vtokf.ap[0:LS], psT.ap[0:LS, 0:256].rearrange("p (g d) -> p g d", g=2), r=[psT], w=[vtokf])
                    out_events.append(self.dma("sp", os_v[s, 112:128], vtokf.ap[0:LS], r=[vtokf]))
            else:
                for g in range(2):
                    self.cp(f1.ap[:, 0:128], kaf.ap[:, g, Tm:Tm + 128], r=[kaf], w=[f1])
                    self.cp(kaf.ap[:, g, 0:128], f1.ap[:, 0:128], r=[f1], w=[kaf])
                    self.cp(f1.ap[:, 0:128], vaf.ap[:, g, Tm:Tm + 128], r=[vaf], w=[f1])
                    self.cp(vaf.ap[:, g, 0:128], f1.ap[:, 0:128], r=[f1], w=[vaf])
                self.cp(kab.ap[:, :, 0:128], kaf.ap[:, :, 0:128], r=[kaf], w=[kab])
            xt = [(j * 128, 128, xres.ap[:, j, :], xres) for j in range(nt)]
            gg1 = [(0, Tm, 0, G1)]
            gg2 = [(0, Tm, 0, G2)]
            if smp:
                xt.append((Tm, TSM, xres.ap[0:TSM, nt, :], xres))
                gg1 += [(Tm + s * LS, LS, 1 + s, G1) for s in range(NS)]
                gg2 += [(Tm + s * LS, LS, 1 + s, G2) for s in range(NS)]
            need_conv("w_o")
            proj_residual(w_o_b, NFC, lambda k: mixT.ap[:, k, 0:Tn_], [mixT, mixTg, mixTs], Tn_, gg1, xt, 512, 8)
            for j in range(nt):
                norm_transpose(xres, xres.ap[:, j, :], 128, j * 128, [(0, 128, 0, (t0 + j) == 0)], 2)
            if smp:
                norm_transpose(xres, xres.ap[0:TSM, nt, :], TSM, Tm, grp, 2)
            segs_ffn = [(0, Tm, lambda ch: (uprev.ap[:, ch, :], uprev), lambda ch: (uprev.ap[:, ch, :], uprev))]
            if smp:
                segs_ffn += [(Tm + s * LS, LS, (lambda ch, s=s: (sffn.ap[:, s, ch, :], sffn)),
                              (lambda ch, s=s: (sffn_o.ap[:, s, ch, :], sffn_o))) for s in range(NS)]
            ffn(Tn_, segs_ffn, gg2, xt)
            for j in range(nt):
                out_events.append(self.dma("sp", ym[(t0 + j) * 128:(t0 + j + 1) * 128, :], xres.ap[:, j, :], r=[xres]))
            if smp:
                out_events.append(self.dma("sp", ys, xres.ap[0:TSM, nt, :], r=[xres]))
                for s in range(NS):
                    out_events.append(self.dma("sp", os_conv[s], sconv_o.ap[:, s], r=[sconv_o]))
                    out_events.append(self.dma("sp", os_ffn[s], sffn_o.ap[:, s], r=[sffn_o]))
        out_events.append(self.dma("sp", o_ffn, uprev.ap[:], r=[uprev]))

        S.finish("sp")
        S.emit()
        self.es.close()
        return nc

    def _swa_sample(self, qcol, kcol, vblk, scvb, vtok, qa, kab, psS, psT, psG, psH, sc1, sc2, pT1, pT2, biasT, onesb, esink, den, mixT):
        Cq = LS
        for g in range(2):
            NQ = 4 * Cq
            rhs_q = qa.ap[:, g * 4:(g + 1) * 4, qcol:qcol + Cq]
            blocks = [(kab.ap[:, g, 0:64], 64, scvb.ap[0:64, 0, g, :], kab),
                      (kab.ap[:, g, 64:128], 64, scvb.ap[0:64, 1, g, :], kab),
                      (kab.ap[:, g, kcol:kcol + LS], LS, vtok.ap[0:LS, vblk, g, :], kab)]
            dsts = [(psS, sc1, pT1, 0), (psS, sc1, pT1, 256), (psT, sc2, pT2, 0)]
            for bi, (kap, nk, vap, ktn) in enumerate(blocks):
                pt, sc, pTt, off = dsts[bi]
                self.mm(pt.ap[0:nk, off:off + NQ], kap, rhs_q, r=[ktn, qa], w=[pt])
            for bi, (kap, nk, vap, ktn) in enumerate(blocks):
                pt, sc, pTt, off = dsts[bi]
                bias_ap = biasT.ap[0:nk, g, bi, :, 0:Cq]
                o3 = sc.ap[0:nk, off:off + NQ].rearrange("p (h c) -> p h c", h=4)
                i3 = pt.ap[0:nk, off:off + NQ].rearrange("p (h c) -> p h c", h=4)
                self.tt(o3, i3, bias_ap, ALU.add, r=[pt, biasT], w=[sc])
                self.act(pTt.ap[0:nk, off:off + NQ], sc.ap[0:nk, off:off + NQ], AF.Exp, r=[sc], w=[pTt])
            for bi, (kap, nk, vap, ktn) in enumerate(blocks):
                pt, sc, pTt, off = dsts[bi]
                self.mm(psG.ap[:, 0:NQ], vap, pTt.ap[0:nk, off:off + NQ], start=(bi == 0), stop=(bi == 2),
                        r=[vtok, scvb, pTt], w=[psG], inc=(bi == 2))
            for bi, (kap, nk, vap, ktn) in enumerate(blocks):
                pt, sc, pTt, off = dsts[bi]
                self.mm(psH.ap[:, 0:NQ], onesb.ap[0:nk, :], pTt.ap[0:nk, off:off + NQ], start=(bi == 0), stop=(bi == 2),
                        r=[onesb, pTt], w=[psH], inc=(bi == 2))
            self.tt(den.ap[:, 0:NQ].rearrange("p (h c) -> p h c", h=4), psH.ap[:, 0:NQ].rearrange("p (h c) -> p h c", h=4),
                    esink.ap[:, g * 4:(g + 1) * 4].unsqueeze(2).to_broadcast([128, 4, Cq]), ALU.add, r=[psH, esink], w=[den])
            self.recip(den.ap[:, 0:NQ], den.ap[:, 0:NQ], r=[den], w=[den])
            self.tt(mixT.ap[:, 8 + g * 4:8 + (g + 1) * 4, qcol:qcol + Cq], psG.ap[:, 0:NQ].rearrange("p (h c) -> p h c", h=4),
                    den.ap[:, 0:NQ].rearrange("p (h c) -> p h c", h=4), ALU.mult, r=[psG, den], w=[mixT])


def _consts(NM, half):
    idx = np.arange(128)
    c = {}
    c["ident"] = np.eye(128, dtype=np.float32)
    c["U"] = (idx[:, None] <= idx[None, :]).astype(np.float32)
    c["Lst"] = (idx[:, None] > idx[None, :]).astype(np.float32)
    c["Linc"] = (idx[:, None] >= idx[None, :]).astype(np.float32)
    slopes = 2.0 ** (-8.0 * np.arange(1, 9, dtype=np.float64) / 8.0)
    p = np.arange(64)[:, None, None, None, None]
    g = np.arange(2)[None, :, None, None, None]
    r = np.arange(3)[None, None, :, None, None]
    hh = np.arange(4)[None, None, None, :, None]
    i = np.arange(64)[None, None, None, None, :]
    dist = np.abs(128 + i - (64 * r + p)).astype(np.float64)
    c["biasT"] = (-(slopes[(g * 4 + hh)]) * dist).astype(np.float32)
    mk = np.zeros((64, 2 * NM * 3), np.float32)
    if half == 0:
        for m in range(2 * NM):
            for r_ in range(3):
                if 64 * (m + r_) - 128 < 128:
                    mk[:, m * 3 + r_] = -30000.0
    c["maskc"] = mk
    return c


def _fm(v, n):
    return np.ascontiguousarray(np.asarray(v, np.float32).reshape(n, 128).T)


_CACHE = {}


def run_cores(inputs, SEQ, TS=3, n_cores=8):
    f32 = np.float32
    half_len = SEQ // 2
    NM = (half_len + 128) // 128
    NP = (half_len - 128) // 128
    key = (NP, NM, TS)
    if key not in _CACHE:
        _CACHE[key] = Builder(NP, NM, TS).build()
    nc = _CACHE[key]
    x_prompt = np.asarray(inputs["x_prompt"], f32)
    x_sample = np.asarray(inputs["x_sample"], f32)
    shared = {
        "ada_w": np.ascontiguousarray(np.asarray(inputs["ada_w"], f32)[0]),
        "ada_b": _fm(inputs["ada_b"][0], 96),
        "norm1_w": _fm(inputs["norm1_w"][0], 16), "norm2_w": _fm(inputs["norm2_w"][0], 16),
        "w_in": np.ascontiguousarray(np.asarray(inputs["w_in"], f32)[0]),
        "conv_w": np.ascontiguousarray(np.asarray(inputs["conv_qkv_w"], f32)[0].reshape(4, 24, 128).transpose(2, 1, 0)),
        "a_log": np.asarray(inputs["a_log"], f32)[0], "dt_bias": np.asarray(inputs["dt_bias"], f32)[0],
        "gdn_norm_w": np.asarray(inputs["gdn_norm_w"], f32)[0].reshape(128, 1),
        "q_norm_w": np.asarray(inputs["q_norm_w"], f32)[0].reshape(128, 1),
        "k_norm_w": np.asarray(inputs["k_norm_w"], f32)[0].reshape(128, 1),
        "sinks": np.asarray(inputs["sinks"], f32)[0],
        "w_o": np.ascontiguousarray(np.asarray(inputs["w_o"], f32)[0]),
        "w_up": np.ascontiguousarray(np.asarray(inputs["w_up"], f32)[0]),
        "ffn_conv_w": np.ascontiguousarray(np.asarray(inputs["ffn_conv_w"], f32)[0].reshape(3, 88, 128).transpose(2, 1, 0)),
        "ffn_conv_b": _fm(inputs["ffn_conv_b"][0], 88),
        "w_down": np.ascontiguousarray(np.asarray(inputs["w_down"], f32)[0]),
    }
    sc = np.asarray(inputs["state_conv_qkv"], f32)[0]
    sd = np.asarray(inputs["state_delta"], f32)[0]
    sk = np.asarray(inputs["cache_swa_k"], f32)[0]
    sv = np.asarray(inputs["cache_swa_v"], f32)[0]
    sf = np.asarray(inputs["state_ffn_conv"], f32)[0]
    cp = np.asarray(inputs["c_prompt"], f32)
    cs = np.asarray(inputs["c_sample"], f32)
    in_maps = []
    for c in range(n_cores):
        b, half = c // 2, c % 2
        start = half * half_len
        m = dict(shared)
        m.update(_consts(NM, half))
        xm = np.zeros((NM * 128, D), f32)
        xp = np.zeros((max(NP, 1) * 128, D), f32)
        if half == 0:
            xm[128:] = x_prompt[b, 0:half_len]
        else:
            xm[:] = x_prompt[b, start - 128:start + half_len]
            xp[:NP * 128] = x_prompt[b, 0:start - 128]
        m["xm"], m["xp"] = xm, xp
        ss = slice(c * NS, (c + 1) * NS)
        m["xs"] = np.ascontiguousarray(x_sample[ss].reshape(NS * LS, D))
        m["flag"] = np.full((128, 1), float(half), f32)
        cc = np.concatenate([cp[b:b + 1], cs[ss]], axis=0)
        m["cT"] = np.ascontiguousarray(cc.reshape(5, 16, 128).transpose(2, 1, 0))
        m["s_conv"] = np.ascontiguousarray(sc[ss].reshape(NS, 3, 24, 128).transpose(0, 3, 2, 1))
        m["s_delta"] = np.ascontiguousarray(sd[ss])
        m["s_kT"] = np.ascontiguousarray(sk[ss].transpose(0, 3, 2, 1))
        m["s_v"] = np.ascontiguousarray(sv[ss])
        m["s_ffn"] = np.ascontiguousarray(sf[ss].reshape(NS, 2, 88, 128).transpose(0, 3, 2, 1))
        in_maps.append(m)
    res = run_bass_kernel_spmd(nc, in_maps, core_ids=list(range(n_cores)))
    R = res.results
    B = n_cores // 2
    y_p = np.zeros((B, SEQ, D), f32)
    for c in range(n_cores):
        b, half = c // 2, c % 2
        y_p[b, half * half_len:(half + 1) * half_len] = R[c]["ym"][128:]
    y_s = np.concatenate([R[c]["ys"].reshape(NS, LS, D) for c in range(n_cores)], axis=0)
    odd = [R[c] for c in range(1, n_cores, 2)]
    p_conv = np.stack([r["o_conv"].transpose(2, 1, 0).reshape(3, 3072) for r in odd])[None]
    p_delta = np.stack([r["o_delta"] for r in odd])[None]
    p_k = np.stack([r["o_kT"].transpose(2, 1, 0) for r in odd])[None]
    p_v = np.stack([r["o_v"] for r in odd])[None]
    p_ffn = np.stack([r["o_ffn"].transpose(2, 1, 0).reshape(2, 2 * DFF) for r in odd])[None]
    s_conv = np.concatenate([r["os_conv"].transpose(0, 3, 2, 1).reshape(NS, 3, 3072) for r in R])[None]
    s_delta = np.concatenate([r["os_delta"] for r in R])[None]
    s_k = np.concatenate([r["os_kT"].transpose(0, 3, 2, 1) for r in R])[None]
    s_v = np.concatenate([r["os_v"] for r in R])[None]
    s_ffn = np.concatenate([r["os_ffn"].transpose(0, 3, 2, 1).reshape(NS, 2, 2 * DFF) for r in R])[None]
    outs = (y_p, y_s, p_conv, p_delta, p_k, p_v, p_ffn, s_conv, s_delta, s_k, s_v, s_ffn)
    return tuple(np.ascontiguousarray(o, dtype=f32) for o in outs)


def kernel(**inputs):
    return run_cores(inputs, SEQ=4096, TS=3, n_cores=8)
```

```python
import numpy as np
from contextlib import ExitStack
import concourse.bass as bass
import concourse.mybir as mybir
from concourse.bass_utils import run_bass_kernel_spmd

F32 = mybir.dt.float32
BF16 = mybir.dt.bfloat16
AF = mybir.ActivationFunctionType
ALU = mybir.AluOpType

D = 2048
NFC = 16
HD = 128
NH = 8
DFF = 5632
NFF = 44
INW = 5648
EPS = 1e-6
NS = 4
LS = 16
ENG_NAMES = ("pe", "act", "dve", "pool", "sp")


class Buf:
    __slots__ = ("name", "w", "r")

    def __init__(self, name=""):
        self.name = name
        self.w = None
        self.r = []


class Sched:
    def __init__(self, nc, n_dma_sems=8):
        self.nc = nc
        self.prog = {e: [] for e in ENG_NAMES}
        self.count = {e: 0 for e in ENG_NAMES}
        self.seen = {e: {} for e in ENG_NAMES}
        self.sems = {}
        self.n_dma_sems = n_dma_sems
        self.dma_rr = {"sp": 0, "pool": 0}
        self.dma_val = {}
        self.all_dma_events = []

    def _waits_for(self, e, reads, writes, is_dma=False):
        deps = []
        for b in reads:
            if b.w is not None:
                deps.append(b.w)
        for b in writes:
            if b.w is not None:
                deps.append(b.w)
            deps.extend(b.r)
        waits = {}
        for (sk, val, src) in deps:
            if src == e and e == "pe" and not is_dma:
                continue
            if self.seen[e].get(sk, 0) >= val:
                continue
            if waits.get(sk, 0) < val:
                waits[sk] = val
        for sk, val in waits.items():
            self.seen[e][sk] = val
        return list(waits.items())

    def op(self, e, fn, reads=(), writes=(), inc=True):
        waits = self._waits_for(e, reads, writes)
        if inc:
            self.count[e] += 1
            ev = (("eng", e), self.count[e], e)
        else:
            ev = (("eng", e), self.count[e] + 1, e)
        self.prog[e].append((waits, fn, inc))
        for b in reads:
            b.r.append(ev)
        for b in writes:
            b.w = ev
            b.r = []
        return ev

    def dma(self, q, fn, reads=(), writes=()):
        i = self.dma_rr[q]
        self.dma_rr[q] = (i + 1) % self.n_dma_sems
        sk = ("dma", q, i)
        prev = self.dma_val.get(sk, 0)
        waits = dict(self._waits_for(q, reads, writes, is_dma=True))
        if prev > 0 and self.seen[q].get(sk, 0) < prev:
            waits[sk] = prev
            self.seen[q][sk] = prev
        val = prev + 16
        self.dma_val[sk] = val
        ev = (sk, val, "dma_" + q)
        self.prog[q].append((list(waits.items()), fn, ("dma", sk)))
        for b in reads:
            b.r.append(ev)
        for b in writes:
            b.w = ev
            b.r = []
        return ev

    def barrier(self):
        evs = [(("eng", e), self.count[e]) for e in ENG_NAMES if self.count[e] > 0]
        evs += [(sk, v) for sk, v in self.dma_val.items()]
        for e in ENG_NAMES:
            waits = []
            for sk, val in evs:
                if sk == ("eng", e) and e == "pe":
                    continue
                if self.seen[e].get(sk, 0) < val:
                    self.seen[e][sk] = val
                    waits.append((sk, val))
            if waits:
                self.prog[e].append((waits, None, False))

    def finish(self, e="sp"):
        waits = [(sk, v) for sk, v in self.dma_val.items()]
        waits += [(("eng", x), self.count[x]) for x in ENG_NAMES if self.count[x] > 0 and x != e]
        self.prog[e].append((waits, None, False))

    def emit(self):
        nc = self.nc
        with ExitStack() as es:
            for e in ENG_NAMES:
                self.sems[("eng", e)] = es.enter_context(nc.semaphore("s_" + e))
            for q in ("sp", "pool"):
                for i in range(self.n_dma_sems):
                    self.sems[("dma", q, i)] = es.enter_context(nc.semaphore("d_%s%d" % (q, i)))
            block = es.enter_context(nc.Block())
            sems = self.sems
            prog = self.prog

            def run(ename):
                def body(engine):
                    for (waits, fn, inc) in prog[ename]:
                        for sk, val in waits:
                            engine.wait_ge(sems[sk], val)
                        if fn is None:
                            continue
                        ins = fn(engine)
                        if inc is True:
                            ins.then_inc(sems[("eng", ename)], 1)
                        elif isinstance(inc, tuple):
                            ins.then_inc(sems[inc[1]], 16)
                return body

            block.tensor(run("pe"))
            block.scalar(run("act"))
            block.vector(run("dve"))
            block.gpsimd(run("pool"))
            block.sync(run("sp"))


class Tn:
    def __init__(self, ap, name=""):
        self.ap = ap
        self.b = Buf(name)

    def __getitem__(self, k):
        return self.ap[k]


class Builder:
    def __init__(self, NP, NM, TS):
        self.NP, self.NM, self.TS = NP, NM, TS
        self.nc = bass.Bass("TRN2", target_bir_lowering=False)
        self.S = Sched(self.nc)
        self.es = ExitStack()
        self.dram = {}
        self.rr = 0

    def din(self, name, shape):
        self.dram[name] = self.nc.dram_tensor(name, list(shape), F32, kind="ExternalInput").ap()
        return self.dram[name]

    def dout(self, name, shape):
        self.dram[name] = self.nc.dram_tensor(name, list(shape), F32, kind="ExternalOutput").ap()
        return self.dram[name]

    def sb(self, name, shape, dt=F32):
        t = self.es.enter_context(self.nc.sbuf_tensor("sb_" + name, list(shape), dt))
        return Tn(t[:], name)

    def ps(self, name, shape, dt=F32):
        t = self.es.enter_context(self.nc.psum_tensor("ps_" + name, list(shape), dt))
        return Tn(t[:], name)

    def op(self, e, fn, r=(), w=(), inc=True):
        if e == "pool":
            e = "dve"
        return self.S.op(e, fn, reads=[t.b for t in r], writes=[t.b for t in w], inc=inc)

    def dma(self, q, out, in_, r=(), w=()):
        return self.S.dma(q, lambda e: e.dma_start(out=out, in_=in_), reads=[t.b for t in r], writes=[t.b for t in w])

    def mm(self, out, lhsT, rhs, start=True, stop=True, r=(), w=(), inc=True):
        return self.op("pe", lambda e: e.matmul(out, lhsT, rhs, start=start, stop=stop), r, w, inc)

    def tr(self, out, in_, ident, r=(), w=(), inc=True):
        return self.op("pe", lambda e: e.transpose(out, in_, ident), r, w, inc)

    def act(self, out, in_, func, r=(), w=(), scale=1.0, bias=0.0, accum_out=None, eng="act"):
        kw = {}
        if accum_out is not None:
            kw["accum_out"] = accum_out
        return self.op("act", lambda e: e.activation(out=out, in_=in_, func=func, scale=scale, bias=bias, **kw), r, w)

    def ew_engine(self):
        self.rr += 1
        return "dve" if self.rr % 3 else "pool"

    def tt(self, out, a, b, op, r=(), w=(), eng="dve"):
        return self.op(eng, lambda e: e.tensor_tensor(out=out, in0=a, in1=b, op=op), r, w)

    def ts(self, out, a, s1, op0, s2=None, op1=None, r=(), w=(), eng="dve"):
        if op1 is None:
            return self.op(eng, lambda e: e.tensor_scalar(out=out, in0=a, scalar1=s1, scalar2=None, op0=op0), r, w)
        return self.op(eng, lambda e: e.tensor_scalar(out=out, in0=a, scalar1=s1, scalar2=s2, op0=op0, op1=op1), r, w)

    def stt(self, out, a, s, b, op0, op1, r=(), w=()):
        return self.op("dve", lambda e: e.scalar_tensor_tensor(out=out, in0=a, scalar=s, in1=b, op0=op0, op1=op1), r, w)

    def cp(self, out, in_, r=(), w=(), eng="dve"):
        if eng == "act":
            return self.op("act", lambda e: e.copy(out=out, in_=in_), r, w)
        return self.op(eng, lambda e: e.tensor_copy(out=out, in_=in_), r, w)

    def recip(self, out, in_, r=(), w=()):
        return self.op("dve", lambda e: e.reciprocal(out=out, in_=in_), r, w)

    def memset(self, t, ap, val, eng="dve"):
        return self.op(eng, lambda e: e.memset(ap, val), (), [t])

    def rsqrt(self, tn, ap, mul):
        self.ts(ap, ap, mul, ALU.mult, EPS, ALU.add, r=[tn], w=[tn])
        self.act(ap, ap, AF.Sqrt, r=[tn], w=[tn])
        self.recip(ap, ap, r=[tn], w=[tn])

    def build(self):
        nc, S = self.nc, self.S
        NP, NM, TS = self.NP, self.NM, self.TS
        T = TS * 128
        TE = T + 128
        din, dout, sb, ps = self.din, self.dout, self.sb, self.ps

        xm = din("xm", [NM * 128, D])
        xp = din("xp", [max(NP, 1) * 128, D])
        xs = din("xs", [NS * LS, D])
        flag_d = din("flag", [128, 1])
        cT_d = din("cT", [128, NFC, 5])
        ada_w = din("ada_w", [D, 6 * D])
        ada_b = din("ada_b", [128, 96])
        n1w_d = din("norm1_w", [128, NFC])
        n2w_d = din("norm2_w", [128, NFC])
        w_in = din("w_in", [D, INW])
        cw_d = din("conv_w", [128, 24, 4])
        alog_d = din("a_log", [8])
        dtb_d = din("dt_bias", [8])
        gnw_d = din("gdn_norm_w", [128, 1])
        qnw_d = din("q_norm_w", [128, 1])
        knw_d = din("k_norm_w", [128, 1])
        sinks_d = din("sinks", [8])
        w_o = din("w_o", [D, D])
        w_up = din("w_up", [D, 2 * DFF])
        fcw_d = din("ffn_conv_w", [128, 2 * NFF, 3])
        fcb_d = din("ffn_conv_b", [128, 2 * NFF])
        w_down = din("w_down", [DFF, D])
        scr = lambda n, sh: nc.dram_tensor(n, list(sh), BF16, kind="Internal").ap()
        w_in_b, w_o_b = scr("w_in_b", [D, INW]), scr("w_o_b", [D, D])
        w_up_b, w_down_b = scr("w_up_b", [D, 2 * DFF]), scr("w_down_b", [DFF, D])
        s_conv_d = din("s_conv", [NS, 128, 24, 3])
        s_delta_d = din("s_delta", [NS, NH, 128, 128])
        s_kT_d = din("s_kT", [NS, 128, 2, 128])
        s_v_d = din("s_v", [NS, 128, 2, 128])
        s_ffn_d = din("s_ffn", [NS, 128, 2 * NFF, 2])
        ident_d = din("ident", [128, 128])
        U_d = din("U", [128, 128])
        Lst_d = din("Lst", [128, 128])
        Linc_d = din("Linc", [128, 128])
        biasT_d = din("biasT", [64, 2, 3, 4, 64])
        maskc_d = din("maskc", [64, 2 * NM * 3])

        ym = dout("ym", [NM * 128, D])
        ys = dout("ys", [NS * LS, D])
        o_conv = dout("o_conv", [128, 24, 3])
        o_delta = dout("o_delta", [NH, 128, 128])
        o_kT = dout("o_kT", [128, 2, 128])
        o_v = dout("o_v", [128, 2, 128])
        o_ffn = dout("o_ffn", [128, 2 * NFF, 2])
        os_conv = dout("os_conv", [NS, 128, 24, 3])
        os_delta = dout("os_delta", [NS, NH, 128, 128])
        os_kT = dout("os_kT", [NS, 128, 2, 128])
        os_v = dout("os_v", [NS, 128, 2, 128])
        os_ffn = dout("os_ffn", [NS, 128, 2 * NFF, 2])

        ident = sb("ident", [128, 128]); identb = sb("identb", [128, 128], BF16)
        U = sb("U", [128, 128]); Lst = sb("Lst", [128, 128]); Linc = sb("Linc", [128, 128])
        onesb = sb("onesb", [128, 128], BF16); onesf = sb("onesf", [128, 128])
        biasT = sb("biasT", [64, 2, 3, 4, 64]); maskc = sb("maskc", [64, 2 * NM * 3])
        flag = sb("flag", [128, 1])
        cT = sb("cT", [128, NFC, 5]); scT = sb("scT", [128, NFC, 5], BF16)
        adab = sb("adab", [128, 96]); n1w = sb("n1w", [128, NFC]); n2w = sb("n2w", [128, NFC])
        cw = sb("cw", [128, 24, 4]); fcw = sb("fcw", [128, 2 * NFF, 3]); fcb = sb("fcb", [128, 2 * NFF])
        alog = sb("alog", [128, 8]); dtb = sb("dtb", [128, 8]); sinks = sb("sinks", [128, 8])
        gnw = sb("gnw", [128, 1]); qnw = sb("qnw", [128, 1]); knw = sb("knw", [128, 1])
        modT = sb("modT", [128, 96, 5])
        A1 = sb("A1", [128, NFC, 5]); A2 = sb("A2", [128, NFC, 5])
        A1f = sb("A1f", [128, NFC]); B1f = sb("B1f", [128, NFC]); A2f = sb("A2f", [128, NFC]); B2f = sb("B2f", [128, NFC])
        esink = sb("esink", [128, 8])
        negea = sb("negea", [128, 8])

        ld = [(ident, ident_d), (U, U_d), (Lst, Lst_d), (Linc, Linc_d), (biasT, biasT_d), (maskc, maskc_d),
              (flag, flag_d), (cT, cT_d), (adab, ada_b), (n1w, n1w_d), (n2w, n2w_d), (cw, cw_d), (fcw, fcw_d),
              (fcb, fcb_d), (gnw, gnw_d), (qnw, qnw_d), (knw, knw_d)]
        for t, d_ in ld:
            self.dma("sp", t.ap[:], d_, w=[t])
        for t, d_ in [(alog, alog_d), (dtb, dtb_d), (sinks, sinks_d)]:
            self.dma("sp", t.ap[:], d_.partition_broadcast(128), w=[t])
        self.cp(identb.ap[:], ident.ap[:], r=[ident], w=[identb])
        self.memset(onesb, onesb.ap[:], 1.0)
        self.memset(onesf, onesf.ap[:], 1.0)
        self.act(scT.ap[:], cT.ap[:], AF.Silu, r=[cT], w=[scT])
        self.act(esink.ap[:], sinks.ap[:], AF.Exp, r=[sinks], w=[esink])
        self.act(negea.ap[:], alog.ap[:], AF.Exp, r=[alog], w=[negea])
        self.ts(negea.ap[:], negea.ap[:], -1.0, ALU.mult, r=[negea], w=[negea])
        self.ts(qnw.ap[:], qnw.ap[:], HD ** -0.5, ALU.mult, r=[qnw], w=[qnw])

        psA = [ps("psA%d" % i, [128, 512]) for i in range(2)]
        psT = ps("psT", [128, 512])
        psG = ps("psG", [128, 1024])
        psH = ps("psH", [128, 1024])
        psGb = Tn(psG.ap[:].bitcast(BF16), "psGb"); psGb.b = psG.b
        psG1 = Tn(psG.ap, "psG1")
        psHb = Tn(psH.ap[:].bitcast(BF16), "psHb"); psHb.b = psH.b
        psS = ps("psS", [128, 512])
        psAi = [0]

        def next_psA():
            psAi[0] ^= 1
            return psA[psAi[0]]

        WSLOT = 4096
        NWS = 4
        wslots = [sb("wslot%d" % i, [128, WSLOT], BF16) for i in range(NWS)]
        wsi = [0]

        def wload(src_ap):
            wsi[0] = (wsi[0] + 1) % NWS
            slot = wslots[wsi[0]]
            n = src_ap.shape[1] * src_ap.shape[2]
            assert n <= WSLOT
            view = slot.ap[:, 0:n].rearrange("p (a b) -> p a b", a=src_ap.shape[1])
            q = "sp" if src_ap.dtype == BF16 else "pool"
            self.S.dma(q, lambda e: e.dma_start(out=view, in_=src_ap), reads=[], writes=[slot.b])
            return slot, view

        conv_done = {}

        def convert(name, src, dst, cols=None):
            R, Cc = src.shape
            cols = cols or [(c0, min(Cc, c0 + 2048)) for c0 in range(0, Cc, 2048)]
            for r0 in range(0, R, 128):
                for (c0, c1) in cols:
                    self.S.dma("pool", lambda e, r0=r0, c0=c0, c1=c1: e.dma_start(out=dst[r0:r0 + 128, c0:c1], in_=src[r0:r0 + 128, c0:c1]))
            conv_done[name] = {sk: v for sk, v in self.S.dma_val.items() if sk[1] == "pool"}

        def need_conv(name):
            waits = []
            for sk, v in conv_done[name].items():
                if self.S.seen["sp"].get(sk, 0) < v:
                    self.S.seen["sp"][sk] = v
                    waits.append((sk, v))
            if waits:
                self.S.prog["sp"].append((waits, None, False))

        def wload_k(src2d, kps):
            nk = src2d.shape[0] // 128
            parts = []
            for k0 in range(0, nk, kps):
                k1 = min(nk, k0 + kps)
                slot, view = wload(src2d[k0 * 128:k1 * 128, :].rearrange("(kc p) n -> p kc n", p=128))
                parts.append((k0, k1, slot, view))

            def wk(k):
                for (k0, k1, slot, view) in parts:
                    if k0 <= k < k1:
                        return slot, view[:, k - k0, :]
                raise IndexError(k)
            return wk

        hT = sb("hT", [128, NFC, T], BF16)
        mixT = sb("mixT", [128, NFC, T], BF16)
        xres = sb("xres", [128, TS, D])
        stat = sb("stat", [128, 8])
        qn = sb("qn", [128, NH, T], BF16); kn = sb("kn", [128, NH, T], BF16); vT = sb("vT", [128, NH, T], BF16)
        kn_main, vT_main = kn, vT
        zs = sb("zs", [128, NH, T], BF16)
        qa = sb("qa", [128, NH, T], BF16)
        kaf = sb("kaf", [128, 2, TE]); kab = sb("kab", [128, 2, TE], BF16)
        vaf = sb("vaf", [128, 2, TE])
        vtok = sb("vtok", [64, TE // 64, 2, 128], BF16)
        vtokf = sb("vtokf", [128, 2, 128])
        cacc = sb("cacc", [128, T])
        sqb = sb("sqb", [128, T], BF16)
        rsd = sb("rsd", [128, T])
        gtok = sb("gtok", [128, max(4 * TS, 2 * TS + NS), 16])
        convprev = sb("convprev", [128, 24, 3])
        uprev = sb("uprev", [128, 2 * NFF, 2])
        Sst = sb("Sst", [128, NH, 128]); Sbf = sb("Sbf", [128, NH, 128], BF16)
        f1 = sb("f1", [128, 512]); f2 = sb("f2", [128, 512]); f3 = sb("f3", [128, 512])
        Dm = sb("Dm", [128, 512])
        xsb = sb("xsb", [128, D], BF16)
        psTb = Tn(psT.ap.bitcast(BF16), "psTb"); psTb.b = psT.b
        b_ = {n: sb(n, [128, 1024], BF16) for n in ["N0", "N1", "M0", "M1", "Pm", "vb", "kbg", "kd", "vnew"]}
        for n in ["QKD", "QKDT", "qg"]:
            b_[n] = sb(n, [128, 512], BF16)
        b_["Xb"] = sb("Xb", [128, 512], BF16)
        nf = {n: Tn(b_[n].ap.bitcast(F32), n + "_f") for n in ["N0", "N1", "M0", "M1", "Pm"]}
        for n in nf:
            nf[n].b = b_[n].b
        b_["ktok"] = b_["vnew"]
        b_["nwT"] = b_["QKD"]
        sm = sb("sm", [128, 64])
        sm_main = sm
        def actT_c(c):
            tn = (qn, kn, vT)[c // 8]
            return tn, tn.ap[:, c % 8, :]
        ug, uu = f3, Dm
        fa, fb = cacc, rsd
        evac = f1
        sc1 = sb("sc1", [64, 512]); sc2 = sb("sc2", [64, 256])
        pT1 = sb("pT1", [64, 512], BF16); pT2 = sb("pT2", [64, 256], BF16)
        den = f2
        dens = sb("dens", [128, 256])
        mixTg = Tn(mixT.ap, "mixTg"); mixTs = Tn(mixT.ap, "mixTs")

        self.memset(convprev, convprev.ap[:], 0.0)
        self.memset(uprev, uprev.ap[:], 0.0)
        self.memset(Sst, Sst.ap[:], 0.0)
        self.memset(Sbf, Sbf.ap[:], 0.0)
        self.memset(kaf, kaf.ap[:], 0.0)
        self.memset(vaf, vaf.ap[:], 0.0)

        convert("w_in_kv", w_in, w_in_b, [(1024, 3072), (4096, 4112)])
        for nb in range(24):
            wk = wload_k(ada_w[:, nb * 512:(nb + 1) * 512], 8)
            pt = psS
            for m in range(4):
                for kc in range(NFC):
                    slot, wap = wk(kc)
                    self.mm(pt.ap[:, m * 8:m * 8 + 5], wap[:, m * 128:(m + 1) * 128], scT.ap[:, kc, :],
                            start=(kc == 0), stop=(kc == NFC - 1), r=[slot, scT], w=[pt], inc=(kc == NFC - 1))
            for m in range(4):
                ch = nb * 4 + m
                self.ts(modT.ap[:, ch, :], pt.ap[:, m * 8:m * 8 + 5], adab.ap[:, ch:ch + 1], ALU.add, r=[pt, adab], w=[modT])
        convert("w_in", w_in, w_in_b, [(0, 1024), (3072, 4096), (4112, INW)])
        convert("w_o", w_o, w_o_b)
        convert("w_up", w_up, w_up_b)
        convert("w_down", w_down, w_down_b)
        self.stt(A1.ap[:], modT.ap[:, 16:32, :], 1.0, n1w.ap[:].unsqueeze(2).to_broadcast([128, NFC, 5]), ALU.add, ALU.mult,
                 r=[modT, n1w], w=[A1])
        self.stt(A2.ap[:], modT.ap[:, 64:80, :], 1.0, n2w.ap[:].unsqueeze(2).to_broadcast([128, NFC, 5]), ALU.add, ALU.mult,
                 r=[modT, n2w], w=[A2])
        B1 = lambda fc, s: modT.ap[:, fc, s:s + 1]
        B2 = lambda fc, s: modT.ap[:, 48 + fc, s:s + 1]
        G1 = lambda fc, s: modT.ap[:, 32 + fc, s:s + 1]
        G2 = lambda fc, s: modT.ap[:, 80 + fc, s:s + 1]
        self.ts(A1f.ap[:], A1.ap[:, :, 0], flag.ap[:, 0:1], ALU.mult, r=[A1, flag], w=[A1f])
        self.ts(B1f.ap[:], modT.ap[:, 0:16, 0], flag.ap[:, 0:1], ALU.mult, r=[modT, flag], w=[B1f])
        self.ts(A2f.ap[:], A2.ap[:, :, 0], flag.ap[:, 0:1], ALU.mult, r=[A2, flag], w=[A2f])
        self.ts(B2f.ap[:], modT.ap[:, 48:64, 0], flag.ap[:, 0:1], ALU.mult, r=[modT, flag], w=[B2f])

        def norm_transpose(xt_tn, xt_ap, nrows, col0, groups, which, hdst=None):
            self.act(xsb.ap[0:nrows, :], xt_ap, AF.Square, r=[xt_tn], w=[xsb, stat], accum_out=stat.ap[0:nrows, 0:1])
            self.rsqrt(stat, stat.ap[0:nrows, 0:1], 1.0 / D)
            self.ts(xsb.ap[0:nrows, :], xt_ap, stat.ap[0:nrows, 0:1], ALU.mult, r=[xt_tn, stat], w=[xsb])
            for g4 in range(4):
                for q in range(4):
                    fc = g4 * 4 + q
                    self.tr(psTb.ap[:, q * 128:q * 128 + nrows], xsb.ap[0:nrows, fc * 128:(fc + 1) * 128], identb.ap[0:nrows, 0:nrows],
                            r=[xsb, identb], w=[psTb])
                for q in range(4):
                    fc = g4 * 4 + q
                    for (c0, n, s, fl) in groups:
                        if which == 1:
                            sc = A1f.ap[:, fc:fc + 1] if fl else A1.ap[:, fc, s:s + 1]
                            bi = B1f.ap[:, fc:fc + 1] if fl else B1(fc, s)
                            rd = [psTb, A1f, B1f, A1, modT]
                        else:
                            sc = A2f.ap[:, fc:fc + 1] if fl else A2.ap[:, fc, s:s + 1]
                            bi = B2f.ap[:, fc:fc + 1] if fl else B2(fc, s)
                            rd = [psTb, A2f, B2f, A2, modT]
                        hd = hT if hdst is None else hdst
                        self.act(hd.ap[:, fc, col0 + c0:col0 + c0 + n], psTb.ap[:, q * 128 + c0:q * 128 + c0 + n], AF.Identity,
                                 r=rd, w=[hd], scale=sc, bias=bi)

        def proj_chunk(wk, mcol, ncols, Tn_, nk, rhs_of, rd=None):
            pt = next_psA()
            rd = [hT] if rd is None else rd
            for k in range(nk):
                slot, wap = wk(k)
                self.mm(pt.ap[0:ncols, 0:Tn_], wap[:, mcol:mcol + ncols], rhs_of(k), start=(k == 0), stop=(k == nk - 1),
                        r=[slot] + rd, w=[pt], inc=(k == nk - 1))
            return pt

        par = [0]
        pmode = [False]
        extp = sb("extp", [128, 3 + T])
        smg = sb("smg", [128, 16])

        def PB():
            if pmode[0]:
                return (extp, cacc, rsd, sqb, psT)
            return [(f3, cacc, rsd, sqb, psT), (Dm, f1, f2, b_["qg"], psS)][par[0] % 2]

        def head_rstd(src_ap, src_tn, Tn_, mul, bufs):
            _, _, rs, sq, pb = bufs
            self.act(sq.ap[:, 0:Tn_], src_ap, AF.Square, r=[src_tn], w=[sq])
            self.mm(pb.ap[:, 0:Tn_], onesb.ap[:], sq.ap[:, 0:Tn_], r=[onesb, sq], w=[pb])
            self.ts(rs.ap[:, 0:Tn_], pb.ap[:, 0:Tn_], mul, ALU.mult, EPS, ALU.add, r=[pb], w=[rs])
            self.act(rs.ap[:, 0:Tn_], rs.ap[:, 0:Tn_], AF.Sqrt, r=[rs], w=[rs])
            self.recip(rs.ap[:, 0:Tn_], rs.ap[:, 0:Tn_], r=[rs], w=[rs])

        def conv_silu(pt, ch, Tn_, segs, dst_ap, dst_tn, norm):
            par[0] += 1
            bufs = PB()
            ext, acc, rs = bufs[0], bufs[1], bufs[2]
            for (c0, n, prv, sav) in segs:
                self.cp(ext.ap[:, 0:3], prv[0], r=[prv[1]], w=[ext])
                self.cp(ext.ap[:, 3:3 + n], pt.ap[:, c0:c0 + n], r=[pt], w=[ext], eng="act")
                self.cp(sav[0], ext.ap[:, n:n + 3], r=[ext], w=[sav[1]])
                self.ts(acc.ap[:, c0:c0 + n], ext.ap[:, 0:n], cw.ap[:, ch, 0:1], ALU.mult, r=[ext, cw], w=[acc])
                for tp in range(1, 4):
                    self.stt(acc.ap[:, c0:c0 + n], ext.ap[:, tp:tp + n], cw.ap[:, ch, tp:tp + 1], acc.ap[:, c0:c0 + n],
                             ALU.mult, ALU.add, r=[ext, cw, acc], w=[acc])
            self.act(acc.ap[:, 0:Tn_], acc.ap[:, 0:Tn_], AF.Silu, r=[acc], w=[acc])
            if norm is None:
                self.cp(dst_ap, acc.ap[:, 0:Tn_], r=[acc], w=[dst_tn], eng="act")
            else:
                head_rstd(acc.ap[:, 0:Tn_], acc, Tn_, 1.0, bufs)
                self.stt(dst_ap, acc.ap[:, 0:Tn_], norm, rs.ap[:, 0:Tn_], ALU.mult, ALU.mult, r=[acc, rs], w=[dst_tn])

        def rms_head(pt, Tn_, wcol, dst_ap, dst_tn):
            par[0] += 1
            bufs = PB()
            acc, rs = bufs[1], bufs[2]
            self.cp(acc.ap[:, 0:Tn_], pt.ap[:, 0:Tn_], r=[pt], w=[acc], eng="act")
            head_rstd(acc.ap[:, 0:Tn_], acc, Tn_, 1.0 / HD, bufs)
            self.stt(dst_ap, acc.ap[:, 0:Tn_], wcol, rs.ap[:, 0:Tn_], ALU.mult, ALU.mult, r=[acc, rs, qnw, knw], w=[dst_tn])

        def gdn_chunk(C, cols, gcol, want_out, kvb=None):
            c0 = cols
            kn, vT = kvb if kvb is not None else (kn_main, vT_main)
            g = gtok.ap[0:C, gcol, 0:8]
            beta = gtok.ap[0:C, gcol, 8:16]
            HC = NH * C
            v3 = lambda t, n=C: t.ap[0:n, 0:NH * 128].rearrange("p (h d) -> p h d", h=NH)
            c3 = lambda t, n=C: t.ap[0:n, 0:HC].rearrange("p (h c) -> p h c", h=NH)
            d3 = lambda t: t.ap[:, 0:HC].rearrange("p (h c) -> p h c", h=NH)
            banks = [(h0, min(h0 + max(1, 512 // C), NH)) for h0 in range(0, NH, max(1, 512 // C))] if C * NH > 512 else [(0, NH)]
            self.mm(psS.ap[0:C, 0:8], U.ap[0:C, 0:C], g, r=[U, gtok], w=[psS])
            self.mm(psS.ap[:, 8:16], onesf.ap[0:C, :], g, r=[onesf, gtok], w=[psS])
            self.cp(sm.ap[0:C, 0:8], psS.ap[0:C, 0:8], r=[psS], w=[sm])
            self.act(sm.ap[0:C, 8:16], psS.ap[0:C, 0:8], AF.Exp, r=[psS], w=[sm])
            self.tt(sm.ap[0:C, 16:24], psS.ap[0:C, 8:16], sm.ap[0:C, 0:8], ALU.subtract, r=[psS, sm], w=[sm])
            self.act(sm.ap[0:C, 16:24], sm.ap[0:C, 16:24], AF.Exp, r=[sm], w=[sm])
            self.tt(sm.ap[0:C, 24:32], beta, sm.ap[0:C, 8:16], ALU.mult, r=[gtok, sm], w=[sm])
            self.act(sm.ap[:, 32:40], psS.ap[:, 8:16], AF.Exp, r=[psS], w=[sm])
            self.ts(sm.ap[0:C, 40:48], beta, -1.0, ALU.mult, r=[gtok], w=[sm])
            yield
            self.tt(c3(f1), g.unsqueeze(2).to_broadcast([C, NH, C]), Lst.ap[0:C, 0:C].unsqueeze(1).to_broadcast([C, NH, C]),
                    ALU.mult, r=[gtok, Lst], w=[f1])
            for (h0, h1) in banks:
                self.mm(psG.ap[0:C, h0 * C:h1 * C], U.ap[0:C, 0:C], f1.ap[0:C, h0 * C:h1 * C], r=[U, f1], w=[psG])
            self.act(Dm.ap[0:C, 0:HC], psG.ap[0:C, 0:HC], AF.Exp, r=[psG], w=[Dm])
            yield
            for h in range(NH):
                self.tr(psHb.ap[0:C, h * 128:(h + 1) * 128], kn.ap[:, h, c0:c0 + C], identb.ap[:], r=[kn, identb], w=[psHb])
            self.cp(v3(b_["ktok"]), psHb.ap[0:C, 0:1024].rearrange("p (h d) -> p h d", h=NH), r=[psHb], w=[b_["ktok"]])
            for h in range(NH):
                self.tr(psHb.ap[0:C, h * 128:(h + 1) * 128], vT.ap[:, h, c0:c0 + C], identb.ap[:], r=[vT, identb], w=[psHb])
            self.tt(v3(b_["vb"]), psHb.ap[0:C, 0:1024].rearrange("p (h d) -> p h d", h=NH),
                    beta.unsqueeze(2).to_broadcast([C, NH, 128]), ALU.mult, r=[psHb, gtok], w=[b_["vb"]])
            self.tt(v3(b_["kbg"]), v3(b_["ktok"]), sm.ap[0:C, 24:32].unsqueeze(2).to_broadcast([C, NH, 128]), ALU.mult,
                    r=[b_["ktok"], sm], w=[b_["kbg"]])
            self.tt(v3(b_["kd"]), v3(b_["ktok"]), sm.ap[0:C, 16:24].unsqueeze(2).to_broadcast([C, NH, 128]), ALU.mult,
                    r=[b_["ktok"], sm], w=[b_["kd"]], eng="pool")
            yield
            for h in range(NH):
                self.mm(psG.ap[0:C, h * C:(h + 1) * C], kn.ap[:, h, c0:c0 + C], kn.ap[:, h, c0:c0 + C], r=[kn], w=[psG])
            self.tt(f2.ap[0:C, 0:HC], psG.ap[0:C, 0:HC], Dm.ap[0:C, 0:HC], ALU.mult, r=[psG, Dm], w=[f2])
            self.tt(c3(f3), sm.ap[0:C, 40:48].unsqueeze(2).to_broadcast([C, NH, C]),
                    Lst.ap[0:C, 0:C].unsqueeze(1).to_broadcast([C, NH, C]), ALU.mult, r=[sm, Lst], w=[f3], eng="pool")
            self.tt(nf["N0"].ap[0:C, 0:HC], f2.ap[0:C, 0:HC], f3.ap[0:C, 0:HC], ALU.mult, r=[f2, f3], w=[nf["N0"]])
            if want_out:
                for h in range(NH):
                    self.mm(psG.ap[0:C, h * C:(h + 1) * C], qn.ap[:, h, c0:c0 + C], kn.ap[:, h, c0:c0 + C], r=[qn, kn], w=[psG])
                self.tt(f2.ap[0:C, 0:HC], psG.ap[0:C, 0:HC], Dm.ap[0:C, 0:HC], ALU.mult, r=[psG, Dm], w=[f2])
                self.tt(c3(b_["QKD"]), c3(f2), Linc.ap[0:C, 0:C].unsqueeze(1).to_broadcast([C, NH, C]), ALU.mult,
                        r=[f2, Linc], w=[b_["QKD"]])
                for h in range(NH):
                    self.tr(psHb.ap[0:C, h * C:(h + 1) * C], b_["QKD"].ap[0:C, h * C:(h + 1) * C], identb.ap[0:C, 0:C],
                            r=[b_["QKD"], identb], w=[psHb])
                self.cp(b_["QKDT"].ap[0:C, 0:HC], psHb.ap[0:C, 0:HC], r=[psHb], w=[b_["QKDT"]], eng="act")
                for h in range(NH):
                    self.mm(psH.ap[:, h * C:(h + 1) * C] if C * NH <= 1024 else None, g[:, h:h + 1].to_broadcast([C, 128]),
                            U.ap[0:C, 0:C], r=[gtok, U], w=[psH])
                self.act(f2.ap[:, 0:HC], psH.ap[:, 0:HC], AF.Exp, r=[psH], w=[f2])
                self.tt(d3(b_["qg"]), qn.ap[:, :, c0:c0 + C], d3(f2), ALU.mult, r=[qn, f2], w=[b_["qg"]])
            yield
            for h in range(NH):
                self.tr(psH.ap[0:C, h * C:(h + 1) * C], nf["N0"].ap[0:C, h * C:(h + 1) * C], ident.ap[0:C, 0:C],
                        r=[nf["N0"], ident], w=[psH])
            self.cp(nf["M0"].ap[0:C, 0:HC], psH.ap[0:C, 0:HC], r=[psH], w=[nf["M0"]], eng="act")
            yield
            c3f = lambda t: t.ap[0:C, 0:HC].rearrange("p (h c) -> p h c", h=NH)
            self.tt(c3f(nf["Pm"]), c3f(nf["M0"]), ident.ap[0:C, 0:C].unsqueeze(1).to_broadcast([C, NH, C]), ALU.add,
                    r=[nf["M0"], ident], w=[nf["Pm"]])
            nlev = int(np.log2(C)) - 1
            Nc, Mc, Nn, Mn = nf["N0"], nf["M0"], nf["N1"], nf["M1"]
            for lev in range(nlev):
                for h in range(NH):
                    sl = slice(h * C, (h + 1) * C)
                    self.mm(psG.ap[0:C, sl], Mc.ap[0:C, sl], Nc.ap[0:C, sl], r=[Mc, Nc], w=[psG])
                self.cp(Nn.ap[0:C, 0:HC], psG.ap[0:C, 0:HC], r=[psG], w=[Nn], eng="act")
                yield
                if lev < nlev - 1:
                    for h in range(NH):
                        sl = slice(h * C, (h + 1) * C)
                        self.mm(psH.ap[0:C, sl], Nc.ap[0:C, sl], Mc.ap[0:C, sl], r=[Mc, Nc], w=[psH])
                    self.cp(Mn.ap[0:C, 0:HC], psH.ap[0:C, 0:HC], r=[psH], w=[Mn])
                    yield
                for h in range(NH):
                    sl = slice(h * C, (h + 1) * C)
                    self.mm(psG.ap[0:C, 512 + h * C:512 + (h + 1) * C], Nn.ap[0:C, sl], nf["Pm"].ap[0:C, sl],
                            r=[Nn, nf["Pm"]], w=[psG1])
                self.tt(nf["Pm"].ap[0:C, 0:HC], nf["Pm"].ap[0:C, 0:HC], psG.ap[0:C, 512:512 + HC], ALU.add, r=[psG1, nf["Pm"]], w=[nf["Pm"]])
                Nc, Nn = Nn, Nc
                Mc, Mn = Mn, Mc
                yield
            X = b_["Xb"]
            self.cp(X.ap[0:C, 0:HC], nf["Pm"].ap[0:C, 0:HC], r=[nf["Pm"]], w=[X], eng="pool")
            yield
            for h in range(NH):
                self.mm(psH.ap[:, h * C:(h + 1) * C], b_["kbg"].ap[0:C, h * 128:(h + 1) * 128], X.ap[0:C, h * C:(h + 1) * C],
                        r=[b_["kbg"], X], w=[psH])
            self.act(b_["nwT"].ap[:, 0:HC], psH.ap[:, 0:HC], AF.Copy, r=[psH], w=[b_["nwT"]], scale=-1.0)
            yield
            for h in range(NH):
                sl = slice(h * 128, (h + 1) * 128)
                self.mm(psG.ap[0:C, sl], X.ap[0:C, h * C:(h + 1) * C], b_["vb"].ap[0:C, sl], start=True, stop=False,
                        r=[X, b_["vb"]], w=[psG, psG1], inc=False)
                self.mm(psG.ap[0:C, sl], b_["nwT"].ap[:, h * C:(h + 1) * C], Sbf.ap[:, h, :], start=False, stop=True,
                        r=[b_["nwT"], Sbf], w=[psG, psG1])
            self.cp(b_["vnew"].ap[0:C, :], psG.ap[0:C, :], r=[psG, psG1], w=[b_["vnew"]], eng="act")
            yield
            if want_out:
                for h in range(NH):
                    self.mm(psH.ap[:, h * C:(h + 1) * C], Sbf.ap[:, h, :], b_["qg"].ap[:, h * C:(h + 1) * C], start=True, stop=False,
                            r=[Sbf, b_["qg"]], w=[psH], inc=False)
                    self.mm(psH.ap[:, h * C:(h + 1) * C], b_["vnew"].ap[0:C, h * 128:(h + 1) * 128],
                            b_["QKDT"].ap[0:C, h * C:(h + 1) * C], start=False, stop=True, r=[b_["vnew"], b_["QKDT"]], w=[psH])
                self.cp(f1.ap[:, 0:HC], psH.ap[:, 0:HC], r=[psH], w=[f1], eng="act")
            yield
            for h in range(NH):
                sl = slice(h * 128, (h + 1) * 128)
                self.mm(psG.ap[:, sl], b_["kd"].ap[0:C, sl], b_["vnew"].ap[0:C, sl], r=[b_["kd"], b_["vnew"]], w=[psG, psG1])
            self.tt(Sst.ap[:], Sst.ap[:], sm.ap[:, 32:40].unsqueeze(2).to_broadcast([128, NH, 128]), ALU.mult, r=[Sst, sm], w=[Sst])
            self.tt(Sst.ap[:], Sst.ap[:], psG.ap[:, :].rearrange("p (h d) -> p h d", h=NH), ALU.add, r=[Sst, psG, psG1], w=[Sst])
            self.cp(Sbf.ap[:], Sst.ap[:], r=[Sst], w=[Sbf], eng="act")
            yield
            if want_out:
                self.act(b_["N1"].ap[:, 0:HC], f1.ap[:, 0:HC], AF.Square, r=[f1], w=[b_["N1"]])
                for (h0, h1) in banks:
                    self.mm(psH.ap[:, h0 * C:h1 * C], onesb.ap[:], b_["N1"].ap[:, h0 * C:h1 * C], r=[onesb, b_["N1"]], w=[psH])
                self.ts(f2.ap[:, 0:HC], psH.ap[:, 0:HC], 1.0 / HD, ALU.mult, EPS, ALU.add, r=[psH], w=[f2])
                self.act(f2.ap[:, 0:HC], f2.ap[:, 0:HC], AF.Sqrt, r=[f2], w=[f2])
                self.recip(f2.ap[:, 0:HC], f2.ap[:, 0:HC], r=[f2], w=[f2])
                self.tt(f1.ap[:, 0:HC], f1.ap[:, 0:HC], f2.ap[:, 0:HC], ALU.mult, r=[f1, f2], w=[f1])
                self.stt(mixT.ap[:, 0:NH, c0:c0 + C], d3(f1), gnw.ap[:, 0:1], zs.ap[:, :, c0:c0 + C], ALU.mult, ALU.mult,
                         r=[f1, gnw, zs], w=[mixTg])

        def swa_chunk(qcol, Cq, kblocks, mcols):
            for g in range(2):
                NQ = 4 * Cq
                rhs_q = qa.ap[:, g * 4:(g + 1) * 4, qcol:qcol + Cq]
                dsts = [(psA[0], sc1, pT1, 0), (psA[0], sc1, pT1, 256), (psA[1], sc2, pT2, 0)]
                for bi, (ec, nk, vfn) in enumerate(kblocks):
                    pt, sc, pTt, off = dsts[bi]
                    self.mm(pt.ap[0:nk, off:off + NQ], kab.ap[:, g, ec:ec + nk], rhs_q, r=[kab, qa], w=[pt])
                yield
                for bi, (ec, nk, vfn) in enumerate(kblocks):
                    pt, sc, pTt, off = dsts[bi]
                    bias_ap = biasT.ap[0:nk, g, bi, :, 0:Cq]
                    o3 = sc.ap[0:nk, off:off + NQ].rearrange("p (h c) -> p h c", h=4)
                    i3 = pt.ap[0:nk, off:off + NQ].rearrange("p (h c) -> p h c", h=4)
                    if mcols is None:
                        self.tt(o3, i3, bias_ap, ALU.add, r=[pt, biasT], w=[sc])
                    else:
                        mc = mcols[bi]
                        self.stt(o3, i3, maskc.ap[0:nk, mc:mc + 1], bias_ap, ALU.add, ALU.add, r=[pt, biasT, maskc], w=[sc])
                    self.act(pTt.ap[0:nk, off:off + NQ], sc.ap[0:nk, off:off + NQ], AF.Exp, r=[sc], w=[pTt])
                    yield
                for bi, (ec, nk, vfn) in enumerate(kblocks):
                    pt, sc, pTt, off = dsts[bi]
                    self.mm(psT.ap[:, 0:NQ], vfn(g), pTt.ap[0:nk, off:off + NQ], start=(bi == 0), stop=(bi == 2),
                            r=[vtok, pTt], w=[psT], inc=(bi == 2))
                for bi, (ec, nk, vfn) in enumerate(kblocks):
                    pt, sc, pTt, off = dsts[bi]
                    self.mm(psT.ap[:, 256:256 + NQ], onesb.ap[0:nk, :], pTt.ap[0:nk, off:off + NQ], start=(bi == 0), stop=(bi == 2),
                            r=[onesb, pTt], w=[psT], inc=(bi == 2))
                yield
                self.tt(dens.ap[:, 0:NQ].rearrange("p (h c) -> p h c", h=4), psT.ap[:, 256:256 + NQ].rearrange("p (h c) -> p h c", h=4),
                        esink.ap[:, g * 4:(g + 1) * 4].unsqueeze(2).to_broadcast([128, 4, Cq]), ALU.add, r=[psT, esink], w=[dens])
                self.recip(dens.ap[:, 0:NQ], dens.ap[:, 0:NQ], r=[dens], w=[dens])
                yield
                self.tt(mixT.ap[:, 8 + g * 4:8 + (g + 1) * 4, qcol:qcol + Cq], psT.ap[:, 0:NQ].rearrange("p (h c) -> p h c", h=4),
                        dens.ap[:, 0:NQ].rearrange("p (h c) -> p h c", h=4), ALU.mult, r=[psT, dens], w=[mixTs])
                yield

        def interleave(ga, gb, ratio):
            da = db = False
            while not (da and db):
                for _ in range(ratio):
                    if not da:
                        try:
                            next(ga)
                        except StopIteration:
                            da = True
                if not db:
                    try:
                        next(gb)
                    except StopIteration:
                        db = True

        def chain(gens):
            for g_ in gens:
                yield from g_

        def run_pipeline(tasks):
            starts = [i for i, t in enumerate(tasks) if t[0] is not None]
            done = set()
            pend = None
            for i, (ld, pj, post) in enumerate(tasks):
                if ld is not None:
                    if i not in done:
                        ld(); done.add(i)
                    nxt = [j for j in starts if j > i]
                    if nxt and nxt[0] not in done:
                        tasks[nxt[0]][0](); done.add(nxt[0])
                pt = pj()
                if pend is not None:
                    pend[1](pend[0])
                pend = (pt, post)
                yield
            if pend is not None:
                pend[1](pend[0])
            yield

        def smg_or_sm():
            return smg if pmode[0] else sm_main

        def gates_mm(wk_ab, col0, n, go, hTx):
            pt = psT if pmode[0] else psS
            go = go + (432 if pmode[0] else 0)
            for k in range(NFC):
                wslot, wap = wk_ab(k)
                self.mm(pt.ap[0:n, go:go + 16], hTx.ap[:, k, col0:col0 + n], wap[:, 0:16], start=(k == 0), stop=(k == NFC - 1),
                        r=[hTx, wslot], w=[pt], inc=(k == NFC - 1))
            return pt

        def gates_post(pt, n, gcol, go):
            go = go + (432 if pmode[0] else 0)
            sm = smg_or_sm()
            x_ = sm.ap[0:n, 0:8] if pmode[0] else sm.ap[0:n, 48:56]
            t_ = sm.ap[0:n, 8:16] if pmode[0] else sm.ap[0:n, 56:64]
            self.tt(x_, pt.ap[0:n, go:go + 8], dtb.ap[0:n, :], ALU.add, r=[pt, dtb], w=[sm])
            self.ts(t_, x_, -1.0, ALU.mult, r=[sm], w=[sm])
            self.tt(t_, t_, x_, ALU.min, r=[sm], w=[sm])
            self.act(t_, t_, AF.Exp, r=[sm], w=[sm])
            self.act(t_, t_, AF.Ln, r=[sm], w=[sm], bias=1.0)
            self.stt(x_, x_, 0.0, t_, ALU.max, ALU.add, r=[sm], w=[sm])
            self.tt(gtok.ap[0:n, gcol, 0:8], x_, negea.ap[0:n, :], ALU.mult, r=[sm, negea], w=[gtok])
            self.act(gtok.ap[0:n, gcol, 8:16], pt.ap[0:n, go + 8:go + 16], AF.Sigmoid, r=[pt], w=[gtok])

        QKV0, Z0, AB0, QA0, KA0, VA0 = 0, 3072, 4096, 4112, 5136, 5392

        def w_in_block(c0, ncols):
            need_conv("w_in_kv" if (1024 <= c0 < 3072 or c0 == 4096) else "w_in")
            return wload_k(w_in_b[:, c0:c0 + ncols], 8 if ncols > 256 else 16)

        def inproj(Tn_, segs_conv, need_q, need_z, need_swa, kvcol0, gate_list, bufs=None):
            hTx, knx, vTx = bufs if bufs is not None else (hT, kn, vT)
            hrhs = lambda k: hTx.ap[:, k, 0:Tn_]
            tasks = []

            def add_block(c0, ncols, chunks):
                holder = {}

                def load():
                    holder["wk"] = w_in_block(c0, ncols)
                for i, (mcol, post) in enumerate(chunks):
                    tasks.append((load if i == 0 else None,
                                  (lambda mcol=mcol: proj_chunk(holder["wk"], mcol, 128, Tn_, NFC, hrhs, rd=[hTx])), post))
                return holder

            def qkv_post(ch):
                which, h = ch // 8, ch % 8
                if which == 0:
                    return lambda pt: conv_silu(pt, ch, Tn_, segs_conv(ch), qn.ap[:, h, 0:Tn_], qn, HD ** -0.5)
                if which == 1:
                    return lambda pt: conv_silu(pt, ch, Tn_, segs_conv(ch), knx.ap[:, h, 0:Tn_], knx, 1.0)
                return lambda pt: conv_silu(pt, ch, Tn_, segs_conv(ch), vTx.ap[:, h, 0:Tn_], vTx, None)
            for blk in range(6):
                if blk < 2 and not need_q:
                    continue
                add_block(QKV0 + blk * 512, 512, [(m * 128, qkv_post(blk * 4 + m)) for m in range(4)])
            if need_z:
                for blk in range(2):
                    add_block(Z0 + blk * 512, 512,
                              [(m * 128, (lambda pt, c=blk * 4 + m: self.act(zs.ap[:, c, 0:Tn_], pt.ap[:, 0:Tn_], AF.Silu, r=[pt], w=[zs])))
                               for m in range(4)])
            hab = {}

            def load_ab():
                hab["wk"] = w_in_block(AB0, 16)
            for gi, (col0, n, gcol) in enumerate(gate_list):
                go = 16 + 16 * (gi % 2)
                tasks.append((load_ab if gi == 0 else None, (lambda col0=col0, n=n, go=go: gates_mm(hab["wk"], col0, n, go, hTx)),
                              (lambda pt, n=n, gcol=gcol, go=go: gates_post(pt, n, gcol, go))))
            if need_swa:
                for blk in range(2):
                    add_block(QA0 + blk * 512, 512,
                              [(m * 128, (lambda pt, c=blk * 4 + m: rms_head(pt, Tn_, qnw.ap[:, 0:1], qa.ap[:, c, 0:Tn_], qa)))
                               for m in range(4)])
            if kvcol0 is not None:
                def k_post(m):
                    def f(pt):
                        rms_head(pt, Tn_, knw.ap[:, 0:1], kaf.ap[:, m, kvcol0:kvcol0 + Tn_], kaf)
                        self.cp(kab.ap[:, m, kvcol0:kvcol0 + Tn_], kaf.ap[:, m, kvcol0:kvcol0 + Tn_], r=[kaf], w=[kab], eng="act")
                    return f
                add_block(KA0, 512, [(m * 128, k_post(m)) for m in range(2)] +
                          [(256 + m * 128, (lambda pt, m=m: self.cp(vaf.ap[:, m, kvcol0:kvcol0 + Tn_], pt.ap[:, 0:Tn_], r=[pt], w=[vaf], eng="act")))
                           for m in range(2)])
            yield from run_pipeline(tasks)

        def make_vtok(ext_c0, n, blk_idx, rows0=0):
            for g in range(2):
                self.tr(psT.ap[0:n, g * 128:(g + 1) * 128], vaf.ap[:, g, ext_c0:ext_c0 + n], ident.ap[:], r=[vaf, ident], w=[psT])
            self.cp(vtok.ap[rows0:rows0 + n, blk_idx, :, :], psT.ap[0:n, 0:256].rearrange("p (g d) -> p g d", g=2), r=[psT], w=[vtok])

        def proj_residual(wdram, nk, rhs_of, rhs_tn, Tn_, gate_groups, xtiles, bcols, kps):
            tasks = []
            for mb in range(D // bcols):
                holder = {}

                def load(mb=mb, holder=holder):
                    holder["wk"] = wload_k(wdram[:, mb * bcols:(mb + 1) * bcols], kps)
                for m in range(bcols // 128):
                    fc = mb * (bcols // 128) + m

                    def post(pt, fc=fc):
                        for (c0, n, s, gfn) in gate_groups:
                            self.act(evac.ap[:, c0:c0 + n], pt.ap[:, c0:c0 + n], AF.Copy, r=[pt, modT], w=[evac], scale=gfn(fc, s))
                        for (col0, nrows, x_ap, x_tn) in xtiles:
                            self.tr(psT.ap[0:nrows, 0:128], evac.ap[:, col0:col0 + nrows], ident.ap[:], r=[evac, ident], w=[psT])
                            self.tt(x_ap[:, fc * 128:(fc + 1) * 128], x_ap[:, fc * 128:(fc + 1) * 128], psT.ap[0:nrows, 0:128], ALU.add,
                                    r=[psT, x_tn], w=[x_tn])
                    tasks.append((load if m == 0 else None,
                                  (lambda m=m, holder=holder: proj_chunk(holder["wk"], m * 128, 128, Tn_, nk, rhs_of, rd=rhs_tn)), post))
            for _ in run_pipeline(tasks):
                pass

        def ffn_conv(pt, ch, ext, acc, segs_ffn):
            for (c0, n, prev_fn, save_fn) in segs_ffn:
                pa, ptn = prev_fn(ch)
                self.cp(ext.ap[:, 0:2], pa, r=[ptn], w=[ext])
                self.cp(ext.ap[:, 2:2 + n], pt.ap[:, c0:c0 + n], r=[pt], w=[ext], eng="act")
                if save_fn is not None:
                    sa, stn = save_fn(ch)
                    self.cp(sa, ext.ap[:, n:n + 2], r=[ext], w=[stn])
                self.act(acc.ap[:, c0:c0 + n], ext.ap[:, 0:n], AF.Identity, r=[ext, fcw, fcb], w=[acc],
                         scale=fcw.ap[:, ch, 0:1], bias=fcb.ap[:, ch:ch + 1])
                for tp in (1, 2):
                    self.stt(acc.ap[:, c0:c0 + n], ext.ap[:, tp:tp + n], fcw.ap[:, ch, tp:tp + 1], acc.ap[:, c0:c0 + n],
                             ALU.mult, ALU.add, r=[ext, fcw, acc], w=[acc])

        def ffn(Tn_, segs_ffn, gate_groups, xtiles):
            hrhs = lambda k: hT.ap[:, k, 0:Tn_]
            cc = 0
            for (c_lo, c_hi) in ((0, 24), (24, NFF)):
                for blk in range(c_lo, c_hi, 4):
                    need_conv("w_up")
                    wk = wload_k(w_up_b[:, blk * 128:blk * 128 + 512], 8)
                    for m in range(4):
                        ch = blk + m
                        cc += 1
                        ext, acc = [(ug, fa), (uu, fb)][cc % 2]
                        pt = proj_chunk(wk, m * 128, 128, Tn_, NFC, hrhs)
                        ffn_conv(pt, ch, ext, acc, segs_ffn)
                        atn, aap = actT_c(ch - c_lo)
                        self.act(aap[:, 0:Tn_], acc.ap[:, 0:Tn_], AF.Silu, r=[acc], w=[atn])
                    wk = wload_k(w_up_b[:, DFF + blk * 128:DFF + blk * 128 + 512], 8)
                    for m in range(4):
                        ch = blk + m
                        cc += 1
                        ext, acc = [(ug, fa), (uu, fb)][cc % 2]
                        pt = proj_chunk(wk, m * 128, 128, Tn_, NFC, hrhs)
                        ffn_conv(pt, NFF + ch, ext, acc, segs_ffn)
                        atn, aap = actT_c(ch - c_lo)
                        self.tt(aap[:, 0:Tn_], aap[:, 0:Tn_], acc.ap[:, 0:Tn_], ALU.mult, r=[atn, acc], w=[atn])
                nk = c_hi - c_lo
                need_conv("w_down")
                proj_residual(w_down_b[c_lo * 128:c_hi * 128, :], nk, lambda k: actT_c(k)[1][:, 0:Tn_], [qn, kn, vT], Tn_,
                              gate_groups, xtiles, 256, nk // 2)

        xstage = xres
        n_pst = (NP + TS - 1) // TS
        pbufs = [(hT, kn, vT), (mixT, zs, qa)]

        def prefix_inputs(st):
            t0 = st * TS
            nt = min(TS, NP - t0)
            Tn_ = nt * 128
            last = (st == n_pst - 1)
            hTx, knx, vTx = pbufs[st % 2]
            gb = (st % 2) * 2 * TS
            for j in range(nt):
                self.dma("sp", xstage.ap[:, j, :], xp[(t0 + j) * 128:(t0 + j + 1) * 128, :], w=[xstage])
                norm_transpose(xstage, xstage.ap[:, j, :], 128, j * 128, [(0, 128, 0, True)], 1, hdst=hTx)
                yield
            segs_conv = lambda ch: [(0, Tn_, (convprev.ap[:, ch, :], convprev), (convprev.ap[:, ch, :], convprev))]
            yield from inproj(Tn_, segs_conv, need_q=last, need_z=False, need_swa=False, kvcol0=(0 if last else None),
                              gate_list=[(j * 64, 64, gb + j) for j in range(2 * nt)], bufs=(hTx, knx, vTx))
            if last and Tn_ > 128:
                for g in range(2):
                    self.cp(cacc.ap[:, 0:128], kaf.ap[:, g, Tn_ - 128:Tn_], r=[kaf], w=[cacc])
                    self.cp(kaf.ap[:, g, 0:128], cacc.ap[:, 0:128], r=[cacc], w=[kaf])
                    self.cp(cacc.ap[:, 0:128], vaf.ap[:, g, Tn_ - 128:Tn_], r=[vaf], w=[cacc])
                    self.cp(vaf.ap[:, g, 0:128], cacc.ap[:, 0:128], r=[cacc], w=[vaf])
                self.cp(kab.ap[:, :, 0:128], kaf.ap[:, :, 0:128], r=[kaf], w=[kab])
            yield

        def prefix_gdn(st):
            nt = min(TS, NP - st * TS)
            hTx, knx, vTx = pbufs[st % 2]
            gb = (st % 2) * 2 * TS
            for j in range(2 * nt):
                yield from gdn_chunk(64, j * 64, gb + j, want_out=False, kvb=(knx, vTx))

        pmode[0] = True
        if n_pst > 0:
            for _ in prefix_inputs(0):
                pass
        for st in range(n_pst):
            nxt = prefix_inputs(st + 1) if st + 1 < n_pst else iter(())
            interleave(prefix_gdn(st), nxt, 4)
        pmode[0] = False

        n_mst = (NM + TS - 1) // TS
        assert NM - (n_mst - 1) * TS < TS, "last main super-tile needs a free tile slot for the sample tokens"
        out_events = []
        TSM = NS * LS
        sconv = sb("sconv", [128, NS, 24, 3]); sffn = sb("sffn", [128, NS, 2 * NFF, 2])
        sconv_o = sb("sconv_o", [128, NS, 24, 3]); sffn_o = sb("sffn_o", [128, NS, 2 * NFF, 2])
        scv = sb("scv", [64, 2, 2, 128])
        scvb = sb("scvb", [64, 2, 2, 128], BF16)
        skT = sb("skT", [128, 2, 128])
        kout = sb("kout", [128, 2, 128])
        for st in range(n_mst):
            t0 = st * TS
            nt = min(TS, NM - t0)
            Tm = nt * 128
            smp = (st == n_mst - 1)
            Tn_ = Tm + (TSM if smp else 0)
            grp = [(s * LS, LS, 1 + s, False) for s in range(NS)]
            for j in range(nt):
                self.dma("sp", xres.ap[:, j, :], xm[(t0 + j) * 128:(t0 + j + 1) * 128, :], w=[xres])
                norm_transpose(xres, xres.ap[:, j, :], 128, j * 128, [(0, 128, 0, (t0 + j) == 0)], 1)
            if smp:
                for s in range(NS):
                    self.dma("sp", sconv.ap[:, s], s_conv_d[s], w=[sconv])
                    self.dma("sp", sffn.ap[:, s], s_ffn_d[s], w=[sffn])
                self.dma("sp", xres.ap[0:TSM, nt, :], xs, w=[xres])
                norm_transpose(xres, xres.ap[0:TSM, nt, :], TSM, Tm, grp, 1)

            def segs_conv(ch, Tm=Tm, smp=smp):
                sg = [(0, Tm, (convprev.ap[:, ch, :], convprev), (convprev.ap[:, ch, :], convprev))]
                if smp:
                    sg += [(Tm + s * LS, LS, (sconv.ap[:, s, ch, :], sconv), (sconv_o.ap[:, s, ch, :], sconv_o)) for s in range(NS)]
                return sg
            gl_ = [(j * 64, 64, j) for j in range(2 * nt)]
            if smp:
                gl_ += [(Tm + s * LS, LS, 2 * nt + s) for s in range(NS)]
            for _ in inproj(Tn_, segs_conv, need_q=True, need_z=True, need_swa=True, kvcol0=128, gate_list=gl_):
                pass
            for bk in range((128 + Tm) // 64):
                make_vtok(bk * 64, 64, bk)
            swa_gens = []
            for cq in range(Tm // 64):
                gch = t0 * 2 + cq
                kbl = [((cq + r_) * 64, 64, (lambda g, b=cq + r_: vtok.ap[0:64, b, g, :])) for r_ in range(3)]
                mcols = [gch * 3 + r_ for r_ in range(3)] if gch < 4 else None
                swa_gens.append(swa_chunk(cq * 64, 64, kbl, mcols))
            interleave(chain([gdn_chunk(64, j * 64, j, want_out=True) for j in range(2 * nt)]), chain(swa_gens), 3)
            if smp:
                out_events.append(self.dma("sp", o_kT, kaf.ap[:, :, Tm:Tm + 128], r=[kaf]))
                for g in range(2):
                    self.tr(psT.ap[:, g * 128:(g + 1) * 128], vaf.ap[:, g, Tm:Tm + 128], ident.ap[:], r=[vaf, ident], w=[psT])
                self.cp(vtokf.ap[:], psT.ap[:, 0:256].rearrange("p (g d) -> p g d", g=2), r=[psT], w=[vtokf])
                out_events.append(self.dma("sp", o_v, vtokf.ap[:], r=[vtokf]))
                out_events.append(self.dma("sp", o_conv, convprev.ap[:], r=[convprev]))
                out_events.append(self.dma("sp", o_delta.rearrange("h k v -> k h v"), Sst.ap[:], r=[Sst]))
                for s in range(NS):
                    sc0 = Tm + s * LS
                    self.dma("sp", scv.ap[:], s_v_d[s].rearrange("(b p) g d -> p b g d", p=64), w=[scv])
                    self.dma("sp", skT.ap[:], s_kT_d[s], w=[skT])
                    self.cp(scvb.ap[:], scv.ap[:], r=[scv], w=[scvb], eng="pool")
                    self.dma("sp", Sst.ap[:], s_delta_d[s].rearrange("h k v -> k h v"), w=[Sst])
                    self.cp(Sbf.ap[:], Sst.ap[:], r=[Sst], w=[Sbf], eng="act")
                    for _ in gdn_chunk(LS, sc0, 2 * nt + s, want_out=True):
                        pass
                    out_events.append(self.dma("sp", os_delta[s].rearrange("h k v -> k h v"), Sst.ap[:], r=[Sst]))
                    self.cp(kab.ap[:, :, 0:128], skT.ap[:], r=[skT], w=[kab])
                    make_vtok(128 + sc0, LS, TE // 64 - 1)
                    self._swa_sample(sc0, 128 + sc0, TE // 64 - 1, scvb, vtok, qa, kab, psS, psT, psG, psH, sc1, sc2, pT1, pT2,
                                     biasT, onesb, esink, den, mixT)
                    self.cp(kout.ap[:, :, 0:112], skT.ap[:, :, 16:128], r=[skT], w=[kout])
                    self.cp(kout.ap[:, :, 112:128], kaf.ap[:, :, 128 + sc0:128 + sc0 + LS], r=[kaf], w=[kout])
                    out_events.append(self.dma("sp", os_kT[s], kout.ap[:], r=[kout]))
                    out_events.append(self.dma("sp", os_v[s, 0:112], s_v_d[s, 16:128]))
                    for g in range(2):
                        self.tr(psT.ap[0:LS, g * 128:(g + 1) * 128], vaf.ap[:, g, 128 + sc0:128 + sc0 + LS], ident.ap[:],
                                r=[vaf, ident], w=[psT])
                    self.cp(vtokf.ap[0:LS], psT.ap[0:LS, 0:256].rearrange("p (g d) -> p g d", g=2), r=[psT], w=[vtokf])
                    out_events.append(self.dma("sp", os_v[s, 112:128], vtokf.ap[0:LS], r=[vtokf]))
            else:
                for g in range(2):
                    self.cp(f1.ap[:, 0:128], kaf.ap[:, g, Tm:Tm + 128], r=[kaf], w=[f1])
                    self.cp(kaf.ap[:, g, 0:128], f1.ap[:, 0:128], r=[f1], w=[kaf])
                    self.cp(f1.ap[:, 0:128], vaf.ap[:, g, Tm:Tm + 128], r=[vaf], w=[f1])
                    self.cp(vaf.ap[:, g, 0:128], f1.ap[:, 0:128], r=[f1], w=[vaf])
                self.cp(kab.ap[:, :, 0:128], kaf.ap[:, :, 0:128], r=[kaf], w=[kab])
            xt = [(j * 128, 128, xres.ap[:, j, :], xres) for j in range(nt)]
            gg1 = [(0, Tm, 0, G1)]
            gg2 = [(0, Tm, 0, G2)]
            if smp:
                xt.append((Tm, TSM, xres.ap[0:TSM, nt, :], xres))
                gg1 += [(Tm + s * LS, LS, 1 + s, G1) for s in range(NS)]
                gg2 += [(Tm + s * LS, LS, 1 + s, G2) for s in range(NS)]
            need_conv("w_o")
            proj_residual(w_o_b, NFC, lambda k: mixT.ap[:, k, 0:Tn_], [mixT, mixTg, mixTs], Tn_, gg1, xt, 512, 8)
            for j in range(nt):
                norm_transpose(xres, xres.ap[:, j, :], 128, j * 128, [(0, 128, 0, (t0 + j) == 0)], 2)
            if smp:
                norm_transpose(xres, xres.ap[0:TSM, nt, :], TSM, Tm, grp, 2)
            segs_ffn = [(0, Tm, lambda ch: (uprev.ap[:, ch, :], uprev), lambda ch: (uprev.ap[:, ch, :], uprev))]
            if smp:
                segs_ffn += [(Tm + s * LS, LS, (lambda ch, s=s: (sffn.ap[:, s, ch, :], sffn)),
                              (lambda ch, s=s: (sffn_o.ap[:, s, ch, :], sffn_o))) for s in range(NS)]
            ffn(Tn_, segs_ffn, gg2, xt)
            for j in range(nt):
                out_events.append(self.dma("sp", ym[(t0 + j) * 128:(t0 + j + 1) * 128, :], xres.ap[:, j, :], r=[xres]))
            if smp:
                out_events.append(self.dma("sp", ys, xres.ap[0:TSM, nt, :], r=[xres]))
                for s in range(NS):
                    out_events.append(self.dma("sp", os_conv[s], sconv_o.ap[:, s], r=[sconv_o]))
                    out_events.append(self.dma("sp", os_ffn[s], sffn_o.ap[:, s], r=[sffn_o]))
        out_events.append(self.dma("sp", o_ffn, uprev.ap[:], r=[uprev]))

        S.finish("sp")
        S.emit()
        self.es.close()
        return nc

    def _swa_sample(self, qcol, kcol, vblk, scvb, vtok, qa, kab, psS, psT, psG, psH, sc1, sc2, pT1, pT2, biasT, onesb, esink, den, mixT):
        Cq = LS
        for g in range(2):
            NQ = 4 * Cq
            rhs_q = qa.ap[:, g * 4:(g + 1) * 4, qcol:qcol + Cq]
            blocks = [(kab.ap[:, g, 0:64], 64, scvb.ap[0:64, 0, g, :], kab),
                      (kab.ap[:, g, 64:128], 64, scvb.ap[0:64, 1, g, :], kab),
                      (kab.ap[:, g, kcol:kcol + LS], LS, vtok.ap[0:LS, vblk, g, :], kab)]
            dsts = [(psS, sc1, pT1, 0), (psS, sc1, pT1, 256), (psT, sc2, pT2, 0)]
            for bi, (kap, nk, vap, ktn) in enumerate(blocks):
                pt, sc, pTt, off = dsts[bi]
                self.mm(pt.ap[0:nk, off:off + NQ], kap, rhs_q, r=[ktn, qa], w=[pt])
            for bi, (kap, nk, vap, ktn) in enumerate(blocks):
                pt, sc, pTt, off = dsts[bi]
                bias_ap = biasT.ap[0:nk, g, bi, :, 0:Cq]
                o3 = sc.ap[0:nk, off:off + NQ].rearrange("p (h c) -> p h c", h=4)
                i3 = pt.ap[0:nk, off:off + NQ].rearrange("p (h c) -> p h c", h=4)
                self.tt(o3, i3, bias_ap, ALU.add, r=[pt, biasT], w=[sc])
                self.act(pTt.ap[0:nk, off:off + NQ], sc.ap[0:nk, off:off + NQ], AF.Exp, r=[sc], w=[pTt])
            for bi, (kap, nk, vap, ktn) in enumerate(blocks):
                pt, sc, pTt, off = dsts[bi]
                self.mm(psG.ap[:, 0:NQ], vap, pTt.ap[0:nk, off:off + NQ], start=(bi == 0), stop=(bi == 2),
                        r=[vtok, scvb, pTt], w=[psG], inc=(bi == 2))
            for bi, (kap, nk, vap, ktn) in enumerate(blocks):
                pt, sc, pTt, off = dsts[bi]
                self.mm(psH.ap[:, 0:NQ], onesb.ap[0:nk, :], pTt.ap[0:nk, off:off + NQ], start=(bi == 0), stop=(bi == 2),
                        r=[onesb, pTt], w=[psH], inc=(bi == 2))
            self.tt(den.ap[:, 0:NQ].rearrange("p (h c) -> p h c", h=4), psH.ap[:, 0:NQ].rearrange("p (h c) -> p h c", h=4),
                    esink.ap[:, g * 4:(g + 1) * 4].unsqueeze(2).to_broadcast([128, 4, Cq]), ALU.add, r=[psH, esink], w=[den])
            self.recip(den.ap[:, 0:NQ], den.ap[:, 0:NQ], r=[den], w=[den])
            self.tt(mixT.ap[:, 8 + g * 4:8 + (g + 1) * 4, qcol:qcol + Cq], psG.ap[:, 0:NQ].rearrange("p (h c) -> p h c", h=4),
                    den.ap[:, 0:NQ].rearrange("p (h c) -> p h c", h=4), ALU.mult, r=[psG, den], w=[mixT])


def _consts(NM, half):
    idx = np.arange(128)
    c = {}
    c["ident"] = np.eye(128, dtype=np.float32)
    c["U"] = (idx[:, None] <= idx[None, :]).astype(np.float32)
    c["Lst"] = (idx[:, None] > idx[None, :]).astype(np.float32)
    c["Linc"] = (idx[:, None] >= idx[None, :]).astype(np.float32)
    slopes = 2.0 ** (-8.0 * np.arange(1, 9, dtype=np.float64) / 8.0)
    p = np.arange(64)[:, None, None, None, None]
    g = np.arange(2)[None, :, None, None, None]
    r = np.arange(3)[None, None, :, None, None]
    hh = np.arange(4)[None, None, None, :, None]
    i = np.arange(64)[None, None, None, None, :]
    dist = np.abs(128 + i - (64 * r + p)).astype(np.float64)
    c["biasT"] = (-(slopes[(g * 4 + hh)]) * dist).astype(np.float32)
    mk = np.zeros((64, 2 * NM * 3), np.float32)
    if half == 0:
        for m in range(2 * NM):
            for r_ in range(3):
                if 64 * (m + r_) - 128 < 128:
                    mk[:, m * 3 + r_] = -30000.0
    c["maskc"] = mk
    return c


def _fm(v, n):
    return np.ascontiguousarray(np.asarray(v, np.float32).reshape(n, 128).T)


_CACHE = {}


def run_cores(inputs, SEQ, TS=3, n_cores=8):
    f32 = np.float32
    half_len = SEQ // 2
    NM = (half_len + 128) // 128
    NP = (half_len - 128) // 128
    key = (NP, NM, TS)
    if key not in _CACHE:
        _CACHE[key] = Builder(NP, NM, TS).build()
    nc = _CACHE[key]
    x_prompt = np.asarray(inputs["x_prompt"], f32)
    x_sample = np.asarray(inputs["x_sample"], f32)
    shared = {
        "ada_w": np.ascontiguousarray(np.asarray(inputs["ada_w"], f32)[0]),
        "ada_b": _fm(inputs["ada_b"][0], 96),
        "norm1_w": _fm(inputs["norm1_w"][0], 16), "norm2_w": _fm(inputs["norm2_w"][0], 16),
        "w_in": np.ascontiguousarray(np.asarray(inputs["w_in"], f32)[0]),
        "conv_w": np.ascontiguousarray(np.asarray(inputs["conv_qkv_w"], f32)[0].reshape(4, 24, 128).transpose(2, 1, 0)),
        "a_log": np.asarray(inputs["a_log"], f32)[0], "dt_bias": np.asarray(inputs["dt_bias"], f32)[0],
        "gdn_norm_w": np.asarray(inputs["gdn_norm_w"], f32)[0].reshape(128, 1),
        "q_norm_w": np.asarray(inputs["q_norm_w"], f32)[0].reshape(128, 1),
        "k_norm_w": np.asarray(inputs["k_norm_w"], f32)[0].reshape(128, 1),
        "sinks": np.asarray(inputs["sinks"], f32)[0],
        "w_o": np.ascontiguousarray(np.asarray(inputs["w_o"], f32)[0]),
        "w_up": np.ascontiguousarray(np.asarray(inputs["w_up"], f32)[0]),
        "ffn_conv_w": np.ascontiguousarray(np.asarray(inputs["ffn_conv_w"], f32)[0].reshape(3, 88, 128).transpose(2, 1, 0)),
        "ffn_conv_b": _fm(inputs["ffn_conv_b"][0], 88),
        "w_down": np.ascontiguousarray(np.asarray(inputs["w_down"], f32)[0]),
    }
    sc = np.asarray(inputs["state_conv_qkv"], f32)[0]
    sd = np.asarray(inputs["state_delta"], f32)[0]
    sk = np.asarray(inputs["cache_swa_k"], f32)[0]
    sv = np.asarray(inputs["cache_swa_v"], f32)[0]
    sf = np.asarray(inputs["state_ffn_conv"], f32)[0]
    cp = np.asarray(inputs["c_prompt"], f32)
    cs = np.asarray(inputs["c_sample"], f32)
    in_maps = []
    for c in range(n_cores):
        b, half = c // 2, c % 2
        start = half * half_len
        m = dict(shared)
        m.update(_consts(NM, half))
        xm = np.zeros((NM * 128, D), f32)
        xp = np.zeros((max(NP, 1) * 128, D), f32)
        if half == 0:
            xm[128:] = x_prompt[b, 0:half_len]
        else:
            xm[:] = x_prompt[b, start - 128:start + half_len]
            xp[:NP * 128] = x_prompt[b, 0:start - 128]
        m["xm"], m["xp"] = xm, xp
        ss = slice(c * NS, (c + 1) * NS)
        m["xs"] = np.ascontiguousarray(x_sample[ss].reshape(NS * LS, D))
        m["flag"] = np.full((128, 1), float(half), f32)
        cc = np.concatenate([cp[b:b + 1], cs[ss]], axis=0)
        m["cT"] = np.ascontiguousarray(cc.reshape(5, 16, 128).transpose(2, 1, 0))
        m["s_conv"] = np.ascontiguousarray(sc[ss].reshape(NS, 3, 24, 128).transpose(0, 3, 2, 1))
        m["s_delta"] = np.ascontiguousarray(sd[ss])
        m["s_kT"] = np.ascontiguousarray(sk[ss].transpose(0, 3, 2, 1))
        m["s_v"] = np.ascontiguousarray(sv[ss])
        m["s_ffn"] = np.ascontiguousarray(sf[ss].reshape(NS, 2, 88, 128).transpose(0, 3, 2, 1))
        in_maps.append(m)
    res = run_bass_kernel_spmd(nc, in_maps, core_ids=list(range(n_cores)))
    R = res.results
    B = n_cores // 2
    y_p = np.zeros((B, SEQ, D), f32)
    for c in range(n_cores):
        b, half = c // 2, c % 2
        y_p[b, half * half_len:(half + 1) * half_len] = R[c]["ym"][128:]
    y_s = np.concatenate([R[c]["ys"].reshape(NS, LS, D) for c in range(n_cores)], axis=0)
    odd = [R[c] for c in range(1, n_cores, 2)]
    p_conv = np.stack([r["o_conv"].transpose(2, 1, 0).reshape(3, 3072) for r in odd])[None]
    p_delta = np.stack([r["o_delta"] for r in odd])[None]
    p_k = np.stack([r["o_kT"].transpose(2, 1, 0) for r in odd])[None]
    p_v = np.stack([r["o_v"] for r in odd])[None]
    p_ffn = np.stack([r["o_ffn"].transpose(2, 1, 0).reshape(2, 2 * DFF) for r in odd])[None]
    s_conv = np.concatenate([r["os_conv"].transpose(0, 3, 2, 1).reshape(NS, 3, 3072) for r in R])[None]
    s_delta = np.concatenate([r["os_delta"] for r in R])[None]
    s_k = np.concatenate([r["os_kT"].transpose(0, 3, 2, 1) for r in R])[None]
    s_v = np.concatenate([r["os_v"] for r in R])[None]
    s_ffn = np.concatenate([r["os_ffn"].transpose(0, 3, 2, 1).reshape(NS, 2, 2 * DFF) for r in R])[None]
    outs = (y_p, y_s, p_conv, p_delta, p_k, p_v, p_ffn, s_conv, s_delta, s_k, s_v, s_ffn)
    return tuple(np.ascontiguousarray(o, dtype=f32) for o in outs)


def kernel(**inputs):
    return run_cores(inputs, SEQ=4096, TS=3, n_cores=8)
```

```python
import numpy as np
from contextlib import ExitStack
import concourse.bass as bass
import concourse.mybir as mybir
from concourse.bass_utils import run_bass_kernel_spmd

F32 = mybir.dt.float32
BF16 = mybir.dt.bfloat16
AF = mybir.ActivationFunctionType
ALU = mybir.AluOpType

D = 2048
NFC = 16
HD = 128
NH = 8
DFF = 5632
NFF = 44
INW = 5648
EPS = 1e-6
NS = 4
LS = 16
ENG_NAMES = ("pe", "act", "dve", "pool", "sp")


class Buf:
    __slots__ = ("name", "w", "r")

    def __init__(self, name=""):
        self.name = name
        self.w = None
        self.r = []


class Sched:
    def __init__(self, nc, n_dma_sems=8):
        self.nc = nc
        self.prog = {e: [] for e in ENG_NAMES}
        self.count = {e: 0 for e in ENG_NAMES}
        self.seen = {e: {} for e in ENG_NAMES}
        self.sems = {}
        self.n_dma_sems = n_dma_sems
        self.dma_rr = {"sp": 0, "pool": 0}
        self.dma_val = {}
        self.all_dma_events = []

    def _waits_for(self, e, reads, writes, is_dma=False):
        deps = []
        for b in reads:
            if b.w is not None:
                deps.append(b.w)
        for b in writes:
            if b.w is not None:
                deps.append(b.w)
            deps.extend(b.r)
        waits = {}
        for (sk, val, src) in deps:
            if src == e and e == "pe" and not is_dma:
                continue
            if self.seen[e].get(sk, 0) >= val:
                continue
            if waits.get(sk, 0) < val:
                waits[sk] = val
        for sk, val in waits.items():
            self.seen[e][sk] = val
        return list(waits.items())

    def op(self, e, fn, reads=(), writes=(), inc=True):
        waits = self._waits_for(e, reads, writes)
        if inc:
            self.count[e] += 1
            ev = (("eng", e), self.count[e], e)
        else:
            ev = (("eng", e), self.count[e] + 1, e)
        self.prog[e].append((waits, fn, inc))
        for b in reads:
            b.r.append(ev)
        for b in writes:
            b.w = ev
            b.r = []
        return ev

    def dma(self, q, fn, reads=(), writes=()):
        i = self.dma_rr[q]
        self.dma_rr[q] = (i + 1) % self.n_dma_sems
        sk = ("dma", q, i)
        prev = self.dma_val.get(sk, 0)
        waits = dict(self._waits_for(q, reads, writes, is_dma=True))
        if prev > 0 and self.seen[q].get(sk, 0) < prev:
            waits[sk] = prev
            self.seen[q][sk] = prev
        val = prev + 16
        self.dma_val[sk] = val
        ev = (sk, val, "dma_" + q)
        self.prog[q].append((list(waits.items()), fn, ("dma", sk)))
        for b in reads:
            b.r.append(ev)
        for b in writes:
            b.w = ev
            b.r = []
        return ev

    def barrier(self):
        evs = [(("eng", e), self.count[e]) for e in ENG_NAMES if self.count[e] > 0]
        evs += [(sk, v) for sk, v in self.dma_val.items()]
        for e in ENG_NAMES:
            waits = []
            for sk, val in evs:
                if sk == ("eng", e) and e == "pe":
                    continue
                if self.seen[e].get(sk, 0) < val:
                    self.seen[e][sk] = val
                    waits.append((sk, val))
            if waits:
                self.prog[e].append((waits, None, False))

    def finish(self, e="sp"):
        waits = [(sk, v) for sk, v in self.dma_val.items()]
        waits += [(("eng", x), self.count[x]) for x in ENG_NAMES if self.count[x] > 0 and x != e]
        self.prog[e].append((waits, None, False))

    def emit(self):
        nc = self.nc
        with ExitStack() as es:
            for e in ENG_NAMES:
                self.sems[("eng", e)] = es.enter_context(nc.semaphore("s_" + e))
            for q in ("sp", "pool"):
                for i in range(self.n_dma_sems):
                    self.sems[("dma", q, i)] = es.enter_context(nc.semaphore("d_%s%d" % (q, i)))
            block = es.enter_context(nc.Block())
            sems = self.sems
            prog = self.prog

            def run(ename):
                def body(engine):
                    for (waits, fn, inc) in prog[ename]:
                        for sk, val in waits:
                            engine.wait_ge(sems[sk], val)
                        if fn is None:
                            continue
                        ins = fn(engine)
                        if inc is True:
                            ins.then_inc(sems[("eng", ename)], 1)
                        elif isinstance(inc, tuple):
                            ins.then_inc(sems[inc[1]], 16)
                return body

            block.tensor(run("pe"))
            block.scalar(run("act"))
            block.vector(run("dve"))
            block.gpsimd(run("pool"))
            block.sync(run("sp"))


class Tn:
    def __init__(self, ap, name=""):
        self.ap = ap
        self.b = Buf(name)

    def __getitem__(self, k):
        return self.ap[k]


class Builder:
    def __init__(self, NP, NM, TS):
        self.NP, self.NM, self.TS = NP, NM, TS
        self.nc = bass.Bass("TRN2", target_bir_lowering=False)
        self.S = Sched(self.nc)
        self.es = ExitStack()
        self.dram = {}
        self.rr = 0

    def din(self, name, shape):
        self.dram[name] = self.nc.dram_tensor(name, list(shape), F32, kind="ExternalInput").ap()
        return self.dram[name]

    def dout(self, name, shape):
        self.dram[name] = self.nc.dram_tensor(name, list(shape), F32, kind="ExternalOutput").ap()
        return self.dram[name]

    def sb(self, name, shape, dt=F32):
        t = self.es.enter_context(self.nc.sbuf_tensor("sb_" + name, list(shape), dt))
        return Tn(t[:], name)

    def ps(self, name, shape, dt=F32):
        t = self.es.enter_context(self.nc.psum_tensor("ps_" + name, list(shape), dt))
        return Tn(t[:], name)

    def op(self, e, fn, r=(), w=(), inc=True):
        if e == "pool":
            e = "dve"
        return self.S.op(e, fn, reads=[t.b for t in r], writes=[t.b for t in w], inc=inc)

    def dma(self, q, out, in_, r=(), w=()):
        return self.S.dma(q, lambda e: e.dma_start(out=out, in_=in_), reads=[t.b for t in r], writes=[t.b for t in w])

    def mm(self, out, lhsT, rhs, start=True, stop=True, r=(), w=(), inc=True):
        return self.op("pe", lambda e: e.matmul(out, lhsT, rhs, start=start, stop=stop), r, w, inc)

    def tr(self, out, in_, ident, r=(), w=(), inc=True):
        return self.op("pe", lambda e: e.transpose(out, in_, ident), r, w, inc)

    def act(self, out, in_, func, r=(), w=(), scale=1.0, bias=0.0, accum_out=None, eng="act"):
        kw = {}
        if accum_out is not None:
            kw["accum_out"] = accum_out
        return self.op("act", lambda e: e.activation(out=out, in_=in_, func=func, scale=scale, bias=bias, **kw), r, w)

    def ew_engine(self):
        self.rr += 1
        return "dve" if self.rr % 3 else "pool"

    def tt(self, out, a, b, op, r=(), w=(), eng="dve"):
        return self.op(eng, lambda e: e.tensor_tensor(out=out, in0=a, in1=b, op=op), r, w)

    def ts(self, out, a, s1, op0, s2=None, op1=None, r=(), w=(), eng="dve"):
        if op1 is None:
            return self.op(eng, lambda e: e.tensor_scalar(out=out, in0=a, scalar1=s1, scalar2=None, op0=op0), r, w)
        return self.op(eng, lambda e: e.tensor_scalar(out=out, in0=a, scalar1=s1, scalar2=s2, op0=op0, op1=op1), r, w)

    def stt(self, out, a, s, b, op0, op1, r=(), w=()):
        return self.op("dve", lambda e: e.scalar_tensor_tensor(out=out, in0=a, scalar=s, in1=b, op0=op0, op1=op1), r, w)

    def cp(self, out, in_, r=(), w=(), eng="dve"):
        if eng == "act":
            return self.op("act", lambda e: e.copy(out=out, in_=in_), r, w)
        return self.op(eng, lambda e: e.tensor_copy(out=out, in_=in_), r, w)

    def recip(self, out, in_, r=(), w=()):
        return self.op("dve", lambda e: e.reciprocal(out=out, in_=in_), r, w)

    def memset(self, t, ap, val, eng="dve"):
        return self.op(eng, lambda e: e.memset(ap, val), (), [t])

    def rsqrt(self, tn, ap, mul):
        self.ts(ap, ap, mul, ALU.mult, EPS, ALU.add, r=[tn], w=[tn])
        self.act(ap, ap, AF.Sqrt, r=[tn], w=[tn])
        self.recip(ap, ap, r=[tn], w=[tn])

    def build(self):
        nc, S = self.nc, self.S
        NP, NM, TS = self.NP, self.NM, self.TS
        T = TS * 128
        TE = T + 128
        din, dout, sb, ps = self.din, self.dout, self.sb, self.ps

        xm = din("xm", [NM * 128, D])
        xp = din("xp", [max(NP, 1) * 128, D])
        xs = din("xs", [NS * LS, D])
        flag_d = din("flag", [128, 1])
        cT_d = din("cT", [128, NFC, 5])
        ada_w = din("ada_w", [D, 6 * D])
        ada_b = din("ada_b", [128, 96])
        n1w_d = din("norm1_w", [128, NFC])
        n2w_d = din("norm2_w", [128, NFC])
        w_in = din("w_in", [D, INW])
        cw_d = din("conv_w", [128, 24, 4])
        alog_d = din("a_log", [8])
        dtb_d = din("dt_bias", [8])
        gnw_d = din("gdn_norm_w", [128, 1])
        qnw_d = din("q_norm_w", [128, 1])
        knw_d = din("k_norm_w", [128, 1])
        sinks_d = din("sinks", [8])
        w_o = din("w_o", [D, D])
        w_up = din("w_up", [D, 2 * DFF])
        fcw_d = din("ffn_conv_w", [128, 2 * NFF, 3])
        fcb_d = din("ffn_conv_b", [128, 2 * NFF])
        w_down = din("w_down", [DFF, D])
        scr = lambda n, sh: nc.dram_tensor(n, list(sh), BF16, kind="Internal").ap()
        w_in_b, w_o_b = scr("w_in_b", [D, INW]), scr("w_o_b", [D, D])
        w_up_b, w_down_b = scr("w_up_b", [D, 2 * DFF]), scr("w_down_b", [DFF, D])
        s_conv_d = din("s_conv", [NS, 128, 24, 3])
        s_delta_d = din("s_delta", [NS, NH, 128, 128])
        s_kT_d = din("s_kT", [NS, 128, 2, 128])
        s_v_d = din("s_v", [NS, 128, 2, 128])
        s_ffn_d = din("s_ffn", [NS, 128, 2 * NFF, 2])
        ident_d = din("ident", [128, 128])
        U_d = din("U", [128, 128])
        Lst_d = din("Lst", [128, 128])
        Linc_d = din("Linc", [128, 128])
        biasT_d = din("biasT", [64, 2, 3, 4, 64])
        maskc_d = din("maskc", [64, 2 * NM * 3])

        ym = dout("ym", [NM * 128, D])
        ys = dout("ys", [NS * LS, D])
        o_conv = dout("o_conv", [128, 24, 3])
        o_delta = dout("o_delta", [NH, 128, 128])
        o_kT = dout("o_kT", [128, 2, 128])
        o_v = dout("o_v", [128, 2, 128])
        o_ffn = dout("o_ffn", [128, 2 * NFF, 2])
        os_conv = dout("os_conv", [NS, 128, 24, 3])
        os_delta = dout("os_delta", [NS, NH, 128, 128])
        os_kT = dout("os_kT", [NS, 128, 2, 128])
        os_v = dout("os_v", [NS, 128, 2, 128])
        os_ffn = dout("os_ffn", [NS, 128, 2 * NFF, 2])

        ident = sb("ident", [128, 128]); identb = sb("identb", [128, 128], BF16)
        U = sb("U", [128, 128]); Lst = sb("Lst", [128, 128]); Linc = sb("Linc", [128, 128])
        onesb = sb("onesb", [128, 128], BF16); onesf = sb("onesf", [128, 128])
        biasT = sb("biasT", [64, 2, 3, 4, 64]); maskc = sb("maskc", [64, 2 * NM * 3])
        flag = sb("flag", [128, 1])
        cT = sb("cT", [128, NFC, 5]); scT = sb("scT", [128, NFC, 5], BF16)
        adab = sb("adab", [128, 96]); n1w = sb("n1w", [128, NFC]); n2w = sb("n2w", [128, NFC])
        cw = sb("cw", [128, 24, 4]); fcw = sb("fcw", [128, 2 * NFF, 3]); fcb = sb("fcb", [128, 2 * NFF])
        alog = sb("alog", [128, 8]); dtb = sb("dtb", [128, 8]); sinks = sb("sinks", [128, 8])
        gnw = sb("gnw", [128, 1]); qnw = sb("qnw", [128, 1]); knw = sb("knw", [128, 1])
        modT = sb("modT", [128, 96, 5])
        A1 = sb("A1", [128, NFC, 5]); A2 = sb("A2", [128, NFC, 5])
        A1f = sb("A1f", [128, NFC]); B1f = sb("B1f", [128, NFC]); A2f = sb("A2f", [128, NFC]); B2f = sb("B2f", [128, NFC])
        esink = sb("esink", [128, 8])
        negea = sb("negea", [128, 8])

        ld = [(ident, ident_d), (U, U_d), (Lst, Lst_d), (Linc, Linc_d), (biasT, biasT_d), (maskc, maskc_d),
              (flag, flag_d), (cT, cT_d), (adab, ada_b), (n1w, n1w_d), (n2w, n2w_d), (cw, cw_d), (fcw, fcw_d),
              (fcb, fcb_d), (gnw, gnw_d), (qnw, qnw_d), (knw, knw_d)]
        for t, d_ in ld:
            self.dma("sp", t.ap[:], d_, w=[t])
        for t, d_ in [(alog, alog_d), (dtb, dtb_d), (sinks, sinks_d)]:
            self.dma("sp", t.ap[:], d_.partition_broadcast(128), w=[t])
        self.cp(identb.ap[:], ident.ap[:], r=[ident], w=[identb])
        self.memset(onesb, onesb.ap[:], 1.0)
        self.memset(onesf, onesf.ap[:], 1.0)
        self.act(scT.ap[:], cT.ap[:], AF.Silu, r=[cT], w=[scT])
        self.act(esink.ap[:], sinks.ap[:], AF.Exp, r=[sinks], w=[esink])
        self.act(negea.ap[:], alog.ap[:], AF.Exp, r=[alog], w=[negea])
        self.ts(negea.ap[:], negea.ap[:], -1.0, ALU.mult, r=[negea], w=[negea])
        self.ts(qnw.ap[:], qnw.ap[:], HD ** -0.5, ALU.mult, r=[qnw], w=[qnw])

        psA = [ps("psA%d" % i, [128, 512]) for i in range(2)]
        psT = ps("psT", [128, 512])
        psG = ps("psG", [128, 1024])
        psH = ps("psH", [128, 1024])
        psGb = Tn(psG.ap[:].bitcast(BF16), "psGb"); psGb.b = psG.b
        psG1 = Tn(psG.ap, "psG1")
        psHb = Tn(psH.ap[:].bitcast(BF16), "psHb"); psHb.b = psH.b
        psS = ps("psS", [128, 512])
        psAi = [0]

        def next_psA():
            psAi[0] ^= 1
            return psA[psAi[0]]

        WSLOT = 4096
        NWS = 4
        wslots = [sb("wslot%d" % i, [128, WSLOT], BF16) for i in range(NWS)]
        wsi = [0]

        def wload(src_ap):
            wsi[0] = (wsi[0] + 1) % NWS
            slot = wslots[wsi[0]]
            n = src_ap.shape[1] * src_ap.shape[2]
            assert n <= WSLOT
            view = slot.ap[:, 0:n].rearrange("p (a b) -> p a b", a=src_ap.shape[1])
            q = "sp" if src_ap.dtype == BF16 else "pool"
            self.S.dma(q, lambda e: e.dma_start(out=view, in_=src_ap), reads=[], writes=[slot.b])
            return slot, view

        conv_done = {}

        def convert(name, src, dst, cols=None):
            R, Cc = src.shape
            cols = cols or [(c0, min(Cc, c0 + 2048)) for c0 in range(0, Cc, 2048)]
            for r0 in range(0, R, 128):
                for (c0, c1) in cols:
                    self.S.dma("pool", lambda e, r0=r0, c0=c0, c1=c1: e.dma_start(out=dst[r0:r0 + 128, c0:c1], in_=src[r0:r0 + 128, c0:c1]))
            conv_done[name] = {sk: v for sk, v in self.S.dma_val.items() if sk[1] == "pool"}

        def need_conv(name):
            waits = []
            for sk, v in conv_done[name].items():
                if self.S.seen["sp"].get(sk, 0) < v:
                    self.S.seen["sp"][sk] = v
                    waits.append((sk, v))
            if waits:
                self.S.prog["sp"].append((waits, None, False))

        def wload_k(src2d, kps):
            nk = src2d.shape[0] // 128
            parts = []
            for k0 in range(0, nk, kps):
                k1 = min(nk, k0 + kps)
                slot, view = wload(src2d[k0 * 128:k1 * 128, :].rearrange("(kc p) n -> p kc n", p=128))
                parts.append((k0, k1, slot, view))

            def wk(k):
                for (k0, k1, slot, view) in parts:
                    if k0 <= k < k1:
                        return slot, view[:, k - k0, :]
                raise IndexError(k)
            return wk

        hT = sb("hT", [128, NFC, T], BF16)
        mixT = sb("mixT", [128, NFC, T], BF16)
        xres = sb("xres", [128, TS, D])
        stat = sb("stat", [128, 8])
        qn = sb("qn", [128, NH, T], BF16); kn = sb("kn", [128, NH, T], BF16); vT = sb("vT", [128, NH, T], BF16)
        kn_main, vT_main = kn, vT
        zs = sb("zs", [128, NH, T], BF16)
        qa = sb("qa", [128, NH, T], BF16)
        kaf = sb("kaf", [128, 2, TE]); kab = sb("kab", [128, 2, TE], BF16)
        vaf = sb("vaf", [128, 2, TE])
        vtok = sb("vtok", [64, TE // 64, 2, 128], BF16)
        vtokf = sb("vtokf", [128, 2, 128])
        cacc = sb("cacc", [128, T])
        sqb = sb("sqb", [128, T], BF16)
        rsd = sb("rsd", [128, T])
        gtok = sb("gtok", [128, max(4 * TS, 2 * TS + NS), 16])
        convprev = sb("convprev", [128, 24, 3])
        uprev = sb("uprev", [128, 2 * NFF, 2])
        Sst = sb("Sst", [128, NH, 128]); Sbf = sb("Sbf", [128, NH, 128], BF16)
        f1 = sb("f1", [128, 512]); f2 = sb("f2", [128, 512]); f3 = sb("f3", [128, 512])
        Dm = sb("Dm", [128, 512])
        xsb = sb("xsb", [128, D], BF16)
        psTb = Tn(psT.ap.bitcast(BF16), "psTb"); psTb.b = psT.b
        b_ = {n: sb(n, [128, 1024], BF16) for n in ["N0", "N1", "M0", "M1", "Pm", "vb", "kbg", "kd", "vnew"]}
        for n in ["QKD", "QKDT", "qg"]:
            b_[n] = sb(n, [128, 512], BF16)
        b_["Xb"] = sb("Xb", [128, 512], BF16)
        nf = {n: Tn(b_[n].ap.bitcast(F32), n + "_f") for n in ["N0", "N1", "M0", "M1", "Pm"]}
        for n in nf:
            nf[n].b = b_[n].b
        b_["ktok"] = b_["vnew"]
        b_["nwT"] = b_["QKD"]
        sm = sb("sm", [128, 64])
        sm_main = sm
        def actT_c(c):
            tn = (qn, kn, vT)[c // 8]
            return tn, tn.ap[:, c % 8, :]
        ug, uu = f3, Dm
        fa, fb = cacc, rsd
        evac = f1
        sc1 = sb("sc1", [64, 512]); sc2 = sb("sc2", [64, 256])
        pT1 = sb("pT1", [64, 512], BF16); pT2 = sb("pT2", [64, 256], BF16)
        den = f2
        dens = sb("dens", [128, 256])
        mixTg = Tn(mixT.ap, "mixTg"); mixTs = Tn(mixT.ap, "mixTs")

        self.memset(convprev, convprev.ap[:], 0.0)
        self.memset(uprev, uprev.ap[:], 0.0)
        self.memset(Sst, Sst.ap[:], 0.0)
        self.memset(Sbf, Sbf.ap[:], 0.0)
        self.memset(kaf, kaf.ap[:], 0.0)
        self.memset(vaf, vaf.ap[:], 0.0)

        convert("w_in_kv", w_in, w_in_b, [(1024, 3072), (4096, 4112)])
        for nb in range(24):
            wk = wload_k(ada_w[:, nb * 512:(nb + 1) * 512], 8)
            pt = psS
            for m in range(4):
                for kc in range(NFC):
                    slot, wap = wk(kc)
                    self.mm(pt.ap[:, m * 8:m * 8 + 5], wap[:, m * 128:(m + 1) * 128], scT.ap[:, kc, :],
                            start=(kc == 0), stop=(kc == NFC - 1), r=[slot, scT], w=[pt], inc=(kc == NFC - 1))
            for m in range(4):
                ch = nb * 4 + m
                self.ts(modT.ap[:, ch, :], pt.ap[:, m * 8:m * 8 + 5], adab.ap[:, ch:ch + 1], ALU.add, r=[pt, adab], w=[modT])
        convert("w_in", w_in, w_in_b, [(0, 1024), (3072, 4096), (4112, INW)])
        convert("w_o", w_o, w_o_b)
        convert("w_up", w_up, w_up_b)
        convert("w_down", w_down, w_down_b)
        self.stt(A1.ap[:], modT.ap[:, 16:32, :], 1.0, n1w.ap[:].unsqueeze(2).to_broadcast([128, NFC, 5]), ALU.add, ALU.mult,
                 r=[modT, n1w], w=[A1])
        self.stt(A2.ap[:], modT.ap[:, 64:80, :], 1.0, n2w.ap[:].unsqueeze(2).to_broadcast([128, NFC, 5]), ALU.add, ALU.mult,
                 r=[modT, n2w], w=[A2])
        B1 = lambda fc, s: modT.ap[:, fc, s:s + 1]
        B2 = lambda fc, s: modT.ap[:, 48 + fc, s:s + 1]
        G1 = lambda fc, s: modT.ap[:, 32 + fc, s:s + 1]
        G2 = lambda fc, s: modT.ap[:, 80 + fc, s:s + 1]
        self.ts(A1f.ap[:], A1.ap[:, :, 0], flag.ap[:, 0:1], ALU.mult, r=[A1, flag], w=[A1f])
        self.ts(B1f.ap[:], modT.ap[:, 0:16, 0], flag.ap[:, 0:1], ALU.mult, r=[modT, flag], w=[B1f])
        self.ts(A2f.ap[:], A2.ap[:, :, 0], flag.ap[:, 0:1], ALU.mult, r=[A2, flag], w=[A2f])
        self.ts(B2f.ap[:], modT.ap[:, 48:64, 0], flag.ap[:, 0:1], ALU.mult, r=[modT, flag], w=[B2f])

        def norm_transpose(xt_tn, xt_ap, nrows, col0, groups, which, hdst=None):
            self.act(xsb.ap[0:nrows, :], xt_ap, AF.Square, r=[xt_tn], w=[xsb, stat], accum_out=stat.ap[0:nrows, 0:1])
            self.rsqrt(stat, stat.ap[0:nrows, 0:1], 1.0 / D)
            self.ts(xsb.ap[0:nrows, :], xt_ap, stat.ap[0:nrows, 0:1], ALU.mult, r=[xt_tn, stat], w=[xsb])
            for g4 in range(4):
                for q in range(4):
                    fc = g4 * 4 + q
                    self.tr(psTb.ap[:, q * 128:q * 128 + nrows], xsb.ap[0:nrows, fc * 128:(fc + 1) * 128], identb.ap[0:nrows, 0:nrows],
                            r=[xsb, identb], w=[psTb])
                for q in range(4):
                    fc = g4 * 4 + q
                    for (c0, n, s, fl) in groups:
                        if which == 1:
                            sc = A1f.ap[:, fc:fc + 1] if fl else A1.ap[:, fc, s:s + 1]
                            bi = B1f.ap[:, fc:fc + 1] if fl else B1(fc, s)
                            rd = [psTb, A1f, B1f, A1, modT]
                        else:
                            sc = A2f.ap[:, fc:fc + 1] if fl else A2.ap[:, fc, s:s + 1]
                            bi = B2f.ap[:, fc:fc + 1] if fl else B2(fc, s)
                            rd = [psTb, A2f, B2f, A2, modT]
                        hd = hT if hdst is None else hdst
                        self.act(hd.ap[:, fc, col0 + c0:col0 + c0 + n], psTb.ap[:, q * 128 + c0:q * 128 + c0 + n], AF.Identity,
                                 r=rd, w=[hd], scale=sc, bias=bi)

        def proj_chunk(wk, mcol, ncols, Tn_, nk, rhs_of, rd=None):
            pt = next_psA()
            rd = [hT] if rd is None else rd
            for k in range(nk):
                slot, wap = wk(k)
                self.mm(pt.ap[0:ncols, 0:Tn_], wap[:, mcol:mcol + ncols], rhs_of(k), start=(k == 0), stop=(k == nk - 1),
                        r=[slot] + rd, w=[pt], inc=(k == nk - 1))
            return pt

        par = [0]
        pmode = [False]
        extp = sb("extp", [128, 3 + T])
        smg = sb("smg", [128, 16])

        def PB():
            if pmode[0]:
                return (extp, cacc, rsd, sqb, psT)
            return [(f3, cacc, rsd, sqb, psT), (Dm, f1, f2, b_["qg"], psS)][par[0] % 2]

        def head_rstd(src_ap, src_tn, Tn_, mul, bufs):
            _, _, rs, sq, pb = bufs
            self.act(sq.ap[:, 0:Tn_], src_ap, AF.Square, r=[src_tn], w=[sq])
            self.mm(pb.ap[:, 0:Tn_], onesb.ap[:], sq.ap[:, 0:Tn_], r=[onesb, sq], w=[pb])
            self.ts(rs.ap[:, 0:Tn_], pb.ap[:, 0:Tn_], mul, ALU.mult, EPS, ALU.add, r=[pb], w=[rs])
            self.act(rs.ap[:, 0:Tn_], rs.ap[:, 0:Tn_], AF.Sqrt, r=[rs], w=[rs])
            self.recip(rs.ap[:, 0:Tn_], rs.ap[:, 0:Tn_], r=[rs], w=[rs])

        def conv_silu(pt, ch, Tn_, segs, dst_ap, dst_tn, norm):
            par[0] += 1
            bufs = PB()
            ext, acc, rs = bufs[0], bufs[1], bufs[2]
            for (c0, n, prv, sav) in segs:
                self.cp(ext.ap[:, 0:3], prv[0], r=[prv[1]], w=[ext])
                self.cp(ext.ap[:, 3:3 + n], pt.ap[:, c0:c0 + n], r=[pt], w=[ext], eng="act")
                self.cp(sav[0], ext.ap[:, n:n + 3], r=[ext], w=[sav[1]])
                self.ts(acc.ap[:, c0:c0 + n], ext.ap[:, 0:n], cw.ap[:, ch, 0:1], ALU.mult, r=[ext, cw], w=[acc])
                for tp in range(1, 4):
                    self.stt(acc.ap[:, c0:c0 + n], ext.ap[:, tp:tp + n], cw.ap[:, ch, tp:tp + 1], acc.ap[:, c0:c0 + n],
                             ALU.mult, ALU.add, r=[ext, cw, acc], w=[acc])
            self.act(acc.ap[:, 0:Tn_], acc.ap[:, 0:Tn_], AF.Silu, r=[acc], w=[acc])
            if norm is None:
                self.cp(dst_ap, acc.ap[:, 0:Tn_], r=[acc], w=[dst_tn], eng="act")
            else:
                head_rstd(acc.ap[:, 0:Tn_], acc, Tn_, 1.0, bufs)
                self.stt(dst_ap, acc.ap[:, 0:Tn_], norm, rs.ap[:, 0:Tn_], ALU.mult, ALU.mult, r=[acc, rs], w=[dst_tn])

        def rms_head(pt, Tn_, wcol, dst_ap, dst_tn):
            par[0] += 1
            bufs = PB()
            acc, rs = bufs[1], bufs[2]
            self.cp(acc.ap[:, 0:Tn_], pt.ap[:, 0:Tn_], r=[pt], w=[acc], eng="act")
            head_rstd(acc.ap[:, 0:Tn_], acc, Tn_, 1.0 / HD, bufs)
            self.stt(dst_ap, acc.ap[:, 0:Tn_], wcol, rs.ap[:, 0:Tn_], ALU.mult, ALU.mult, r=[acc, rs, qnw, knw], w=[dst_tn])

        def gdn_chunk(C, cols, gcol, want_out, kvb=None):
            c0 = cols
            kn, vT = kvb if kvb is not None else (kn_main, vT_main)
            g = gtok.ap[0:C, gcol, 0:8]
            beta = gtok.ap[0:C, gcol, 8:16]
            HC = NH * C
            v3 = lambda t, n=C: t.ap[0:n, 0:NH * 128].rearrange("p (h d) -> p h d", h=NH)
            c3 = lambda t, n=C: t.ap[0:n, 0:HC].rearrange("p (h c) -> p h c", h=NH)
            d3 = lambda t: t.ap[:, 0:HC].rearrange("p (h c) -> p h c", h=NH)
            banks = [(h0, min(h0 + max(1, 512 // C), NH)) for h0 in range(0, NH, max(1, 512 // C))] if C * NH > 512 else [(0, NH)]
            self.mm(psS.ap[0:C, 0:8], U.ap[0:C, 0:C], g, r=[U, gtok], w=[psS])
            self.mm(psS.ap[:, 8:16], onesf.ap[0:C, :], g, r=[onesf, gtok], w=[psS])
            self.cp(sm.ap[0:C, 0:8], psS.ap[0:C, 0:8], r=[psS], w=[sm])
            self.act(sm.ap[0:C, 8:16], psS.ap[0:C, 0:8], AF.Exp, r=[psS], w=[sm])
            self.tt(sm.ap[0:C, 16:24], psS.ap[0:C, 8:16], sm.ap[0:C, 0:8], ALU.subtract, r=[psS, sm], w=[sm])
            self.act(sm.ap[0:C, 16:24], sm.ap[0:C, 16:24], AF.Exp, r=[sm], w=[sm])
            self.tt(sm.ap[0:C, 24:32], beta, sm.ap[0:C, 8:16], ALU.mult, r=[gtok, sm], w=[sm])
            self.act(sm.ap[:, 32:40], psS.ap[:, 8:16], AF.Exp, r=[psS], w=[sm])
            self.ts(sm.ap[0:C, 40:48], beta, -1.0, ALU.mult, r=[gtok], w=[sm])
            yield
            self.tt(c3(f1), g.unsqueeze(2).to_broadcast([C, NH, C]), Lst.ap[0:C, 0:C].unsqueeze(1).to_broadcast([C, NH, C]),
                    ALU.mult, r=[gtok, Lst], w=[f1])
            for (h0, h1) in banks:
                self.mm(psG.ap[0:C, h0 * C:h1 * C], U.ap[0:C, 0:C], f1.ap[0:C, h0 * C:h1 * C], r=[U, f1], w=[psG])
            self.act(Dm.ap[0:C, 0:HC], psG.ap[0:C, 0:HC], AF.Exp, r=[psG], w=[Dm])
            yield
            for h in range(NH):
                self.tr(psHb.ap[0:C, h * 128:(h + 1) * 128], kn.ap[:, h, c0:c0 + C], identb.ap[:], r=[kn, identb], w=[psHb])
            self.cp(v3(b_["ktok"]), psHb.ap[0:C, 0:1024].rearrange("p (h d) -> p h d", h=NH), r=[psHb], w=[b_["ktok"]])
            for h in range(NH):
                self.tr(psHb.ap[0:C, h * 128:(h + 1) * 128], vT.ap[:, h, c0:c0 + C], identb.ap[:], r=[vT, identb], w=[psHb])
            self.tt(v3(b_["vb"]), psHb.ap[0:C, 0:1024].rearrange("p (h d) -> p h d", h=NH),
                    beta.unsqueeze(2).to_broadcast([C, NH, 128]), ALU.mult, r=[psHb, gtok], w=[b_["vb"]])
            self.tt(v3(b_["kbg"]), v3(b_["ktok"]), sm.ap[0:C, 24:32].unsqueeze(2).to_broadcast([C, NH, 128]), ALU.mult,
                    r=[b_["ktok"], sm], w=[b_["kbg"]])
            self.tt(v3(b_["kd"]), v3(b_["ktok"]), sm.ap[0:C, 16:24].unsqueeze(2).to_broadcast([C, NH, 128]), ALU.mult,
                    r=[b_["ktok"], sm], w=[b_["kd"]], eng="pool")
            yield
            for h in range(NH):
                self.mm(psG.ap[0:C, h * C:(h + 1) * C], kn.ap[:, h, c0:c0 + C], kn.ap[:, h, c0:c0 + C], r=[kn], w=[psG])
            self.tt(f2.ap[0:C, 0:HC], psG.ap[0:C, 0:HC], Dm.ap[0:C, 0:HC], ALU.mult, r=[psG, Dm], w=[f2])
            self.tt(c3(f3), sm.ap[0:C, 40:48].unsqueeze(2).to_broadcast([C, NH, C]),
                    Lst.ap[0:C, 0:C].unsqueeze(1).to_broadcast([C, NH, C]), ALU.mult, r=[sm, Lst], w=[f3], eng="pool")
            self.tt(nf["N0"].ap[0:C, 0:HC], f2.ap[0:C, 0:HC], f3.ap[0:C, 0:HC], ALU.mult, r=[f2, f3], w=[nf["N0"]])
            if want_out:
                for h in range(NH):
                    self.mm(psG.ap[0:C, h * C:(h + 1) * C], qn.ap[:, h, c0:c0 + C], kn.ap[:, h, c0:c0 + C], r=[qn, kn], w=[psG])
                self.tt(f2.ap[0:C, 0:HC], psG.ap[0:C, 0:HC], Dm.ap[0:C, 0:HC], ALU.mult, r=[psG, Dm], w=[f2])
                self.tt(c3(b_["QKD"]), c3(f2), Linc.ap[0:C, 0:C].unsqueeze(1).to_broadcast([C, NH, C]), ALU.mult,
                        r=[f2, Linc], w=[b_["QKD"]])
                for h in range(NH):
                    self.tr(psHb.ap[0:C, h * C:(h + 1) * C], b_["QKD"].ap[0:C, h * C:(h + 1) * C], identb.ap[0:C, 0:C],
                            r=[b_["QKD"], identb], w=[psHb])
                self.cp(b_["QKDT"].ap[0:C, 0:HC], psHb.ap[0:C, 0:HC], r=[psHb], w=[b_["QKDT"]], eng="act")
                for h in range(NH):
                    self.mm(psH.ap[:, h * C:(h + 1) * C] if C * NH <= 1024 else None, g[:, h:h + 1].to_broadcast([C, 128]),
                            U.ap[0:C, 0:C], r=[gtok, U], w=[psH])
                self.act(f2.ap[:, 0:HC], psH.ap[:, 0:HC], AF.Exp, r=[psH], w=[f2])
                self.tt(d3(b_["qg"]), qn.ap[:, :, c0:c0 + C], d3(f2), ALU.mult, r=[qn, f2], w=[b_["qg"]])
            yield
            for h in range(NH):
                self.tr(psH.ap[0:C, h * C:(h + 1) * C], nf["N0"].ap[0:C, h * C:(h + 1) * C], ident.ap[0:C, 0:C],
                        r=[nf["N0"], ident], w=[psH])
            self.cp(nf["M0"].ap[0:C, 0:HC], psH.ap[0:C, 0:HC], r=[psH], w=[nf["M0"]], eng="act")
            yield
            c3f = lambda t: t.ap[0:C, 0:HC].rearrange("p (h c) -> p h c", h=NH)
            self.tt(c3f(nf["Pm"]), c3f(nf["M0"]), ident.ap[0:C, 0:C].unsqueeze(1).to_broadcast([C, NH, C]), ALU.add,
                    r=[nf["M0"], ident], w=[nf["Pm"]])
            nlev = int(np.log2(C)) - 1
            Nc, Mc, Nn, Mn = nf["N0"], nf["M0"], nf["N1"], nf["M1"]
            for lev in range(nlev):
                for h in range(NH):
                    sl = slice(h * C, (h + 1) * C)
                    self.mm(psG.ap[0:C, sl], Mc.ap[0:C, sl], Nc.ap[0:C, sl], r=[Mc, Nc], w=[psG])
                self.cp(Nn.ap[0:C, 0:HC], psG.ap[0:C, 0:HC], r=[psG], w=[Nn], eng="act")
                yield
                if lev < nlev - 1:
                    for h in range(NH):
                        sl = slice(h * C, (h + 1) * C)
                        self.mm(psH.ap[0:C, sl], Nc.ap[0:C, sl], Mc.ap[0:C, sl], r=[Mc, Nc], w=[psH])
                    self.cp(Mn.ap[0:C, 0:HC], psH.ap[0:C, 0:HC], r=[psH], w=[Mn])
                    yield
                for h in range(NH):
                    sl = slice(h * C, (h + 1) * C)
                    self.mm(psG.ap[0:C, 512 + h * C:512 + (h + 1) * C], Nn.ap[0:C, sl], nf["Pm"].ap[0:C, sl],
                            r=[Nn, nf["Pm"]], w=[psG1])
                self.tt(nf["Pm"].ap[0:C, 0:HC], nf["Pm"].ap[0:C, 0:HC], psG.ap[0:C, 512:512 + HC], ALU.add, r=[psG1, nf["Pm"]], w=[nf["Pm"]])
                Nc, Nn = Nn, Nc
                Mc, Mn = Mn, Mc
                yield
            X = b_["Xb"]
            self.cp(X.ap[0:C, 0:HC], nf["Pm"].ap[0:C, 0:HC], r=[nf["Pm"]], w=[X], eng="pool")
            yield
            for h in range(NH):
                self.mm(psH.ap[:, h * C:(h + 1) * C], b_["kbg"].ap[0:C, h * 128:(h + 1) * 128], X.ap[0:C, h * C:(h + 1) * C],
                        r=[b_["kbg"], X], w=[psH])
            self.act(b_["nwT"].ap[:, 0:HC], psH.ap[:, 0:HC], AF.Copy, r=[psH], w=[b_["nwT"]], scale=-1.0)
            yield
            for h in range(NH):
                sl = slice(h * 128, (h + 1) * 128)
                self.mm(psG.ap[0:C, sl], X.ap[0:C, h * C:(h + 1) * C], b_["vb"].ap[0:C, sl], start=True, stop=False,
                        r=[X, b_["vb"]], w=[psG, psG1], inc=False)
                self.mm(psG.ap[0:C, sl], b_["nwT"].ap[:, h * C:(h + 1) * C], Sbf.ap[:, h, :], start=False, stop=True,
                        r=[b_["nwT"], Sbf], w=[psG, psG1])
            self.cp(b_["vnew"].ap[0:C, :], psG.ap[0:C, :], r=[psG, psG1], w=[b_["vnew"]], eng="act")
            yield
            if want_out:
                for h in range(NH):
                    self.mm(psH.ap[:, h * C:(h + 1) * C], Sbf.ap[:, h, :], b_["qg"].ap[:, h * C:(h + 1) * C], start=True, stop=False,
                            r=[Sbf, b_["qg"]], w=[psH], inc=False)
                    self.mm(psH.ap[:, h * C:(h + 1) * C], b_["vnew"].ap[0:C, h * 128:(h + 1) * 128],
                            b_["QKDT"].ap[0:C, h * C:(h + 1) * C], start=False, stop=True, r=[b_["vnew"], b_["QKDT"]], w=[psH])
                self.cp(f1.ap[:, 0:HC], psH.ap[:, 0:HC], r=[psH], w=[f1], eng="act")
            yield
            for h in range(NH):
                sl = slice(h * 128, (h + 1) * 128)
                self.mm(psG.ap[:, sl], b_["kd"].ap[0:C, sl], b_["vnew"].ap[0:C, sl], r=[b_["kd"], b_["vnew"]], w=[psG, psG1])
            self.tt(Sst.ap[:], Sst.ap[:], sm.ap[:, 32:40].unsqueeze(2).to_broadcast([128, NH, 128]), ALU.mult, r=[Sst, sm], w=[Sst])
            self.tt(Sst.ap[:], Sst.ap[:], psG.ap[:, :].rearrange("p (h d) -> p h d", h=NH), ALU.add, r=[Sst, psG, psG1], w=[Sst])
            self.cp(Sbf.ap[:], Sst.ap[:], r=[Sst], w=[Sbf], eng="act")
            yield
            if want_out:
                self.act(b_["N1"].ap[:, 0:HC], f1.ap[:, 0:HC], AF.Square, r=[f1], w=[b_["N1"]])
                for (h0, h1) in banks:
                    self.mm(psH.ap[:, h0 * C:h1 * C], onesb.ap[:], b_["N1"].ap[:, h0 * C:h1 * C], r=[onesb, b_["N1"]], w=[psH])
                self.ts(f2.ap[:, 0:HC], psH.ap[:, 0:HC], 1.0 / HD, ALU.mult, EPS, ALU.add, r=[psH], w=[f2])
                self.act(f2.ap[:, 0:HC], f2.ap[:, 0:HC], AF.Sqrt, r=[f2], w=[f2])
                self.recip(f2.ap[:, 0:HC], f2.ap[:, 0:HC], r=[f2], w=[f2])
                self.tt(f1.ap[:, 0:HC], f1.ap[:, 0:HC], f2.ap[:, 0:HC], ALU.mult, r=[f1, f2], w=[f1])
                self.stt(mixT.ap[:, 0:NH, c0:c0 + C], d3(f1), gnw.ap[:, 0:1], zs.ap[:, :, c0:c0 + C], ALU.mult, ALU.mult,
                         r=[f1, gnw, zs], w=[mixTg])

        def swa_chunk(qcol, Cq, kblocks, mcols):
            for g in range(2):
                NQ = 4 * Cq
                rhs_q = qa.ap[:, g * 4:(g + 1) * 4, qcol:qcol + Cq]
                dsts = [(psA[0], sc1, pT1, 0), (psA[0], sc1, pT1, 256), (psA[1], sc2, pT2, 0)]
                for bi, (ec, nk, vfn) in enumerate(kblocks):
                    pt, sc, pTt, off = dsts[bi]
                    self.mm(pt.ap[0:nk, off:off + NQ], kab.ap[:, g, ec:ec + nk], rhs_q, r=[kab, qa], w=[pt])
                yield
                for bi, (ec, nk, vfn) in enumerate(kblocks):
                    pt, sc, pTt, off = dsts[bi]
                    bias_ap = biasT.ap[0:nk, g, bi, :, 0:Cq]
                    o3 = sc.ap[0:nk, off:off + NQ].rearrange("p (h c) -> p h c", h=4)
                    i3 = pt.ap[0:nk, off:off + NQ].rearrange("p (h c) -> p h c", h=4)
                    if mcols is None:
                        self.tt(o3, i3, bias_ap, ALU.add, r=[pt, biasT], w=[sc])
                    else:
                        mc = mcols[bi]
                        self.stt(o3, i3, maskc.ap[0:nk, mc:mc + 1], bias_ap, ALU.add, ALU.add, r=[pt, biasT, maskc], w=[sc])
                    self.act(pTt.ap[0:nk, off:off + NQ], sc.ap[0:nk, off:off + NQ], AF.Exp, r=[sc], w=[pTt])
                    yield
                for bi, (ec, nk, vfn) in enumerate(kblocks):
                    pt, sc, pTt, off = dsts[bi]
                    self.mm(psT.ap[:, 0:NQ], vfn(g), pTt.ap[0:nk, off:off + NQ], start=(bi == 0), stop=(bi == 2),
                            r=[vtok, pTt], w=[psT], inc=(bi == 2))
                for bi, (ec, nk, vfn) in enumerate(kblocks):
                    pt, sc, pTt, off = dsts[bi]
                    self.mm(psT.ap[:, 256:256 + NQ], onesb.ap[0:nk, :], pTt.ap[0:nk, off:off + NQ], start=(bi == 0), stop=(bi == 2),
                            r=[onesb, pTt], w=[psT], inc=(bi == 2))
                yield
                self.tt(dens.ap[:, 0:NQ].rearrange("p (h c) -> p h c", h=4), psT.ap[:, 256:256 + NQ].rearrange("p (h c) -> p h c", h=4),
                        esink.ap[:, g * 4:(g + 1) * 4].unsqueeze(2).to_broadcast([128, 4, Cq]), ALU.add, r=[psT, esink], w=[dens])
                self.recip(dens.ap[:, 0:NQ], dens.ap[:, 0:NQ], r=[dens], w=[dens])
                yield
                self.tt(mixT.ap[:, 8 + g * 4:8 + (g + 1) * 4, qcol:qcol + Cq], psT.ap[:, 0:NQ].rearrange("p (h c) -> p h c", h=4),
                        dens.ap[:, 0:NQ].rearrange("p (h c) -> p h c", h=4), ALU.mult, r=[psT, dens], w=[mixTs])
                yield

        def interleave(ga, gb, ratio):
            da = db = False
            while not (da and db):
                for _ in range(ratio):
                    if not da:
                        try:
                            next(ga)
                        except StopIteration:
                            da = True
                if not db:
                    try:
                        next(gb)
                    except StopIteration:
                        db = True

        def chain(gens):
            for g_ in gens:
                yield from g_

        def run_pipeline(tasks):
            starts = [i for i, t in enumerate(tasks) if t[0] is not None]
            done = set()
            pend = None
            for i, (ld, pj, post) in enumerate(tasks):
                if ld is not None:
                    if i not in done:
                        ld(); done.add(i)
                    nxt = [j for j in starts if j > i]
                    if nxt and nxt[0] not in done:
                        tasks[nxt[0]][0](); done.add(nxt[0])
                pt = pj()
                if pend is not None:
                    pend[1](pend[0])
                pend = (pt, post)
                yield
            if pend is not None:
                pend[1](pend[0])
            yield

        def smg_or_sm():
            return smg if pmode[0] else sm_main

        def gates_mm(wk_ab, col0, n, go, hTx):
            pt = psT if pmode[0] else psS
            go = go + (432 if pmode[0] else 0)
            for k in range(NFC):
                wslot, wap = wk_ab(k)
                self.mm(pt.ap[0:n, go:go + 16], hTx.ap[:, k, col0:col0 + n], wap[:, 0:16], start=(k == 0), stop=(k == NFC - 1),
                        r=[hTx, wslot], w=[pt], inc=(k == NFC - 1))
            return pt

        def gates_post(pt, n, gcol, go):
            go = go + (432 if pmode[0] else 0)
            sm = smg_or_sm()
            x_ = sm.ap[0:n, 0:8] if pmode[0] else sm.ap[0:n, 48:56]
            t_ = sm.ap[0:n, 8:16] if pmode[0] else sm.ap[0:n, 56:64]
            self.tt(x_, pt.ap[0:n, go:go + 8], dtb.ap[0:n, :], ALU.add, r=[pt, dtb], w=[sm])
            self.ts(t_, x_, -1.0, ALU.mult, r=[sm], w=[sm])
            self.tt(t_, t_, x_, ALU.min, r=[sm], w=[sm])
            self.act(t_, t_, AF.Exp, r=[sm], w=[sm])
            self.act(t_, t_, AF.Ln, r=[sm], w=[sm], bias=1.0)
            self.stt(x_, x_, 0.0, t_, ALU.max, ALU.add, r=[sm], w=[sm])
            self.tt(gtok.ap[0:n, gcol, 0:8], x_, negea.ap[0:n, :], ALU.mult, r=[sm, negea], w=[gtok])
            self.act(gtok.ap[0:n, gcol, 8:16], pt.ap[0:n, go + 8:go + 16], AF.Sigmoid, r=[pt], w=[gtok])

        QKV0, Z0, AB0, QA0, KA0, VA0 = 0, 3072, 4096, 4112, 5136, 5392

        def w_in_block(c0, ncols):
            need_conv("w_in_kv" if (1024 <= c0 < 3072 or c0 == 4096) else "w_in")
            return wload_k(w_in_b[:, c0:c0 + ncols], 8 if ncols > 256 else 16)

        def inproj(Tn_, segs_conv, need_q, need_z, need_swa, kvcol0, gate_list, bufs=None):
            hTx, knx, vTx = bufs if bufs is not None else (hT, kn, vT)
            hrhs = lambda k: hTx.ap[:, k, 0:Tn_]
            tasks = []

            def add_block(c0, ncols, chunks):
                holder = {}

                def load():
                    holder["wk"] = w_in_block(c0, ncols)
                for i, (mcol, post) in enumerate(chunks):
                    tasks.append((load if i == 0 else None,
                                  (lambda mcol=mcol: proj_chunk(holder["wk"], mcol, 128, Tn_, NFC, hrhs, rd=[hTx])), post))
                return holder

            def qkv_post(ch):
                which, h = ch // 8, ch % 8
                if which == 0:
                    return lambda pt: conv_silu(pt, ch, Tn_, segs_conv(ch), qn.ap[:, h, 0:Tn_], qn, HD ** -0.5)
                if which == 1:
                    return lambda pt: conv_silu(pt, ch, Tn_, segs_conv(ch), knx.ap[:, h, 0:Tn_], knx, 1.0)
                return lambda pt: conv_silu(pt, ch, Tn_, segs_conv(ch), vTx.ap[:, h, 0:Tn_], vTx, None)
            for blk in range(6):
                if blk < 2 and not need_q:
                    continue
                add_block(QKV0 + blk * 512, 512, [(m * 128, qkv_post(blk * 4 + m)) for m in range(4)])
            if need_z:
                for blk in range(2):
                    add_block(Z0 + blk * 512, 512,
                              [(m * 128, (lambda pt, c=blk * 4 + m: self.act(zs.ap[:, c, 0:Tn_], pt.ap[:, 0:Tn_], AF.Silu, r=[pt], w=[zs])))
                               for m in range(4)])
            hab = {}

            def load_ab():
                hab["wk"] = w_in_block(AB0, 16)
            for gi, (col0, n, gcol) in enumerate(gate_list):
                go = 16 + 16 * (gi % 2)
                tasks.append((load_ab if gi == 0 else None, (lambda col0=col0, n=n, go=go: gates_mm(hab["wk"], col0, n, go, hTx)),
                              (lambda pt, n=n, gcol=gcol, go=go: gates_post(pt, n, gcol, go))))
            if need_swa:
                for blk in range(2):
                    add_block(QA0 + blk * 512, 512,
                              [(m * 128, (lambda pt, c=blk * 4 + m: rms_head(pt, Tn_, qnw.ap[:, 0:1], qa.ap[:, c, 0:Tn_], qa)))
                               for m in range(4)])
            if kvcol0 is not None:
                def k_post(m):
                    def f(pt):
                        rms_head(pt, Tn_, knw.ap[:, 0:1], kaf.ap[:, m, kvcol0:kvcol0 + Tn_], kaf)
                        self.cp(kab.ap[:, m, kvcol0:kvcol0 + Tn_], kaf.ap[:, m, kvcol0:kvcol0 + Tn_], r=[kaf], w=[kab], eng="act")
                    return f
                add_block(KA0, 512, [(m * 128, k_post(m)) for m in range(2)] +
                          [(256 + m * 128, (lambda pt, m=m: self.cp(vaf.ap[:, m, kvcol0:kvcol0 + Tn_], pt.ap[:, 0:Tn_], r=[pt], w=[vaf], eng="act")))
                           for m in range(2)])
            yield from run_pipeline(tasks)

        def make_vtok(ext_c0, n, blk_idx, rows0=0):
            for g in range(2):
                self.tr(psT.ap[0:n, g * 128:(g + 1) * 128], vaf.ap[:, g, ext_c0:ext_c0 + n], ident.ap[:], r=[vaf, ident], w=[psT])
            self.cp(vtok.ap[rows0:rows0 + n, blk_idx, :, :], psT.ap[0:n, 0:256].rearrange("p (g d) -> p g d", g=2), r=[psT], w=[vtok])

        def proj_residual(wdram, nk, rhs_of, rhs_tn, Tn_, gate_groups, xtiles, bcols, kps):
            tasks = []
            for mb in range(D // bcols):
                holder = {}

                def load(mb=mb, holder=holder):
                    holder["wk"] = wload_k(wdram[:, mb * bcols:(mb + 1) * bcols], kps)
                for m in range(bcols // 128):
                    fc = mb * (bcols // 128) + m

                    def post(pt, fc=fc):
                        for (c0, n, s, gfn) in gate_groups:
                            self.act(evac.ap[:, c0:c0 + n], pt.ap[:, c0:c0 + n], AF.Copy, r=[pt, modT], w=[evac], scale=gfn(fc, s))
                        for (col0, nrows, x_ap, x_tn) in xtiles:
                            self.tr(psT.ap[0:nrows, 0:128], evac.ap[:, col0:col0 + nrows], ident.ap[:], r=[evac, ident], w=[psT])
                            self.tt(x_ap[:, fc * 128:(fc + 1) * 128], x_ap[:, fc * 128:(fc + 1) * 128], psT.ap[0:nrows, 0:128], ALU.add,
                                    r=[psT, x_tn], w=[x_tn])
                    tasks.append((load if m == 0 else None,
                                  (lambda m=m, holder=holder: proj_chunk(holder["wk"], m * 128, 128, Tn_, nk, rhs_of, rd=rhs_tn)), post))
            for _ in run_pipeline(tasks):
                pass

        def ffn_conv(pt, ch, ext, acc, segs_ffn):
            for (c0, n, prev_fn, save_fn) in segs_ffn:
                pa, ptn = prev_fn(ch)
                self.cp(ext.ap[:, 0:2], pa, r=[ptn], w=[ext])
                self.cp(ext.ap[:, 2:2 + n], pt.ap[:, c0:c0 + n], r=[pt], w=[ext], eng="act")
                if save_fn is not None:
                    sa, stn = save_fn(ch)
                    self.cp(sa, ext.ap[:, n:n + 2], r=[ext], w=[stn])
                self.act(acc.ap[:, c0:c0 + n], ext.ap[:, 0:n], AF.Identity, r=[ext, fcw, fcb], w=[acc],
                         scale=fcw.ap[:, ch, 0:1], bias=fcb.ap[:, ch:ch + 1])
                for tp in (1, 2):
                    self.stt(acc.ap[:, c0:c0 + n], ext.ap[:, tp:tp + n], fcw.ap[:, ch, tp:tp + 1], acc.ap[:, c0:c0 + n],
                             ALU.mult, ALU.add, r=[ext, fcw, acc], w=[acc])

        def ffn(Tn_, segs_ffn, gate_groups, xtiles):
            hrhs = lambda k: hT.ap[:, k, 0:Tn_]
            cc = 0
            for (c_lo, c_hi) in ((0, 24), (24, NFF)):
                for blk in range(c_lo, c_hi, 4):
                    need_conv("w_up")
                    wk = wload_k(w_up_b[:, blk * 128:blk * 128 + 512], 8)
                    for m in range(4):
                        ch = blk + m
                        cc += 1
                        ext, acc = [(ug, fa), (uu, fb)][cc % 2]
                        pt = proj_chunk(wk, m * 128, 128, Tn_, NFC, hrhs)
                        ffn_conv(pt, ch, ext, acc, segs_ffn)
                        atn, aap = actT_c(ch - c_lo)
                        self.act(aap[:, 0:Tn_], acc.ap[:, 0:Tn_], AF.Silu, r=[acc], w=[atn])
                    wk = wload_k(w_up_b[:, DFF + blk * 128:DFF + blk * 128 + 512], 8)
                    for m in range(4):
                        ch = blk + m
                        cc += 1
                        ext, acc = [(ug, fa), (uu, fb)][cc % 2]
                        pt = proj_chunk(wk, m * 128, 128, Tn_, NFC, hrhs)
                        ffn_conv(pt, NFF + ch, ext, acc, segs_ffn)
                        atn, aap = actT_c(ch - c_lo)
                        self.tt(aap[:, 0:Tn_], aap[:, 0:Tn_], acc.ap[:, 0:Tn_], ALU.mult, r=[atn, acc], w=[atn])
                nk = c_hi - c_lo
                need_conv("w_down")
                proj_residual(w_down_b[c_lo * 128:c_hi * 128, :], nk, lambda k: actT_c(k)[1][:, 0:Tn_], [qn, kn, vT], Tn_,
                              gate_groups, xtiles, 256, nk // 2)

        xstage = xres
        n_pst = (NP + TS - 1) // TS
        pbufs = [(hT, kn, vT), (mixT, zs, qa)]

        def prefix_inputs(st):
            t0 = st * TS
            nt = min(TS, NP - t0)
            Tn_ = nt * 128
            last = (st == n_pst - 1)
            hTx, knx, vTx = pbufs[st % 2]
            gb = (st % 2) * 2 * TS
            for j in range(nt):
                self.dma("sp", xstage.ap[:, j, :], xp[(t0 + j) * 128:(t0 + j + 1) * 128, :], w=[xstage])
                norm_transpose(xstage, xstage.ap[:, j, :], 128, j * 128, [(0, 128, 0, True)], 1, hdst=hTx)
                yield
            segs_conv = lambda ch: [(0, Tn_, (convprev.ap[:, ch, :], convprev), (convprev.ap[:, ch, :], convprev))]
            yield from inproj(Tn_, segs_conv, need_q=last, need_z=False, need_swa=False, kvcol0=(0 if last else None),
                              gate_list=[(j * 64, 64, gb + j) for j in range(2 * nt)], bufs=(hTx, knx, vTx))
            if last and Tn_ > 128:
                for g in range(2):
                    self.cp(cacc.ap[:, 0:128], kaf.ap[:, g, Tn_ - 128:Tn_], r=[kaf], w=[cacc])
                    self.cp(kaf.ap[:, g, 0:128], cacc.ap[:, 0:128], r=[cacc], w=[kaf])
                    self.cp(cacc.ap[:, 0:128], vaf.ap[:, g, Tn_ - 128:Tn_], r=[vaf], w=[cacc])
                    self.cp(vaf.ap[:, g, 0:128], cacc.ap[:, 0:128], r=[cacc], w=[vaf])
                self.cp(kab.ap[:, :, 0:128], kaf.ap[:, :, 0:128], r=[kaf], w=[kab])
            yield

        def prefix_gdn(st):
            nt = min(TS, NP - st * TS)
            hTx, knx, vTx = pbufs[st % 2]
            gb = (st % 2) * 2 * TS
            for j in range(2 * nt):
                yield from gdn_chunk(64, j * 64, gb + j, want_out=False, kvb=(knx, vTx))

        pmode[0] = True
        if n_pst > 0:
            for _ in prefix_inputs(0):
                pass
        for st in range(n_pst):
            nxt = prefix_inputs(st + 1) if st + 1 < n_pst else iter(())
            interleave(prefix_gdn(st), nxt, 8)
        pmode[0] = False

        n_mst = (NM + TS - 1) // TS
        assert NM - (n_mst - 1) * TS < TS, "last main super-tile needs a free tile slot for the sample tokens"
        out_events = []
        TSM = NS * LS
        sconv = sb("sconv", [128, NS, 24, 3]); sffn = sb("sffn", [128, NS, 2 * NFF, 2])
        sconv_o = sb("sconv_o", [128, NS, 24, 3]); sffn_o = sb("sffn_o", [128, NS, 2 * NFF, 2])
        scv = sb("scv", [64, 2, 2, 128])
        scvb = sb("scvb", [64, 2, 2, 128], BF16)
        skT = sb("skT", [128, 2, 128])
        kout = sb("kout", [128, 2, 128])
        for st in range(n_mst):
            t0 = st * TS
            nt = min(TS, NM - t0)
            Tm = nt * 128
            smp = (st == n_mst - 1)
            Tn_ = Tm + (TSM if smp else 0)
            grp = [(s * LS, LS, 1 + s, False) for s in range(NS)]
            for j in range(nt):
                self.dma("sp", xres.ap[:, j, :], xm[(t0 + j) * 128:(t0 + j + 1) * 128, :], w=[xres])
                norm_transpose(xres, xres.ap[:, j, :], 128, j * 128, [(0, 128, 0, (t0 + j) == 0)], 1)
            if smp:
                for s in range(NS):
                    self.dma("sp", sconv.ap[:, s], s_conv_d[s], w=[sconv])
                    self.dma("sp", sffn.ap[:, s], s_ffn_d[s], w=[sffn])
                self.dma("sp", xres.ap[0:TSM, nt, :], xs, w=[xres])
                norm_transpose(xres, xres.ap[0:TSM, nt, :], TSM, Tm, grp, 1)

            def segs_conv(ch, Tm=Tm, smp=smp):
                sg = [(0, Tm, (convprev.ap[:, ch, :], convprev), (convprev.ap[:, ch, :], convprev))]
                if smp:
                    sg += [(Tm + s * LS, LS, (sconv.ap[:, s, ch, :], sconv), (sconv_o.ap[:, s, ch, :], sconv_o)) for s in range(NS)]
                return sg
            gl_ = [(j * 64, 64, j) for j in range(2 * nt)]
            if smp:
                gl_ += [(Tm + s * LS, LS, 2 * nt + s) for s in range(NS)]
            for _ in inproj(Tn_, segs_conv, need_q=True, need_z=True, need_swa=True, kvcol0=128, gate_list=gl_):
                pass
            for bk in range((128 + Tm) // 64):
                make_vtok(bk * 64, 64, bk)
            swa_gens = []
            for cq in range(Tm // 64):
                gch = t0 * 2 + cq
                kbl = [((cq + r_) * 64, 64, (lambda g, b=cq + r_: vtok.ap[0:64, b, g, :])) for r_ in range(3)]
                mcols = [gch * 3 + r_ for r_ in range(3)] if gch < 4 else None
                swa_gens.append(swa_chunk(cq * 64, 64, kbl, mcols))
            interleave(chain([gdn_chunk(64, j * 64, j, want_out=True) for j in range(2 * nt)]), chain(swa_gens), 3)
            if smp:
                out_events.append(self.dma("sp", o_kT, kaf.ap[:, :, Tm:Tm + 128], r=[kaf]))
                for g in range(2):
                    self.tr(psT.ap[:, g * 128:(g + 1) * 128], vaf.ap[:, g, Tm:Tm + 128], ident.ap[:], r=[vaf, ident], w=[psT])
                self.cp(vtokf.ap[:], psT.ap[:, 0:256].rearrange("p (g d) -> p g d", g=2), r=[psT], w=[vtokf])
                out_events.append(self.dma("sp", o_v, vtokf.ap[:], r=[vtokf]))
                out_events.append(self.dma("sp", o_conv, convprev.ap[:], r=[convprev]))
                out_events.append(self.dma("sp", o_delta.rearrange("h k v -> k h v"), Sst.ap[:], r=[Sst]))
                for s in range(NS):
                    sc0 = Tm + s * LS
                    self.dma("sp", scv.ap[:], s_v_d[s].rearrange("(b p) g d -> p b g d", p=64), w=[scv])
                    self.dma("sp", skT.ap[:], s_kT_d[s], w=[skT])
                    self.cp(scvb.ap[:], scv.ap[:], r=[scv], w=[scvb], eng="pool")
                    self.dma("sp", Sst.ap[:], s_delta_d[s].rearrange("h k v -> k h v"), w=[Sst])
                    self.cp(Sbf.ap[:], Sst.ap[:], r=[Sst], w=[Sbf], eng="act")
                    for _ in gdn_chunk(LS, sc0, 2 * nt + s, want_out=True):
                        pass
                    out_events.append(self.dma("sp", os_delta[s].rearrange("h k v -> k h v"), Sst.ap[:], r=[Sst]))
                    self.cp(kab.ap[:, :, 0:128], skT.ap[:], r=[skT], w=[kab])
                    make_vtok(128 + sc0, LS, TE // 64 - 1)
                    self._swa_sample(sc0, 128 + sc0, TE // 64 - 1, scvb, vtok, qa, kab, psS, psT, psG, psH, sc1, sc2, pT1, pT2,
                                     biasT, onesb, esink, den, mixT)
                    self.cp(kout.ap[:, :, 0:112], skT.ap[:, :, 16:128], r=[skT], w=[kout])
                    self.cp(kout.ap[:, :, 112:128], kaf.ap[:, :, 128 + sc0:128 + sc0 + LS], r=[kaf], w=[kout])
                    out_events.append(self.dma("sp", os_kT[s], kout.ap[:], r=[kout]))
                    out_events.append(self.dma("sp", os_v[s, 0:112], s_v_d[s, 16:128]))
                    for g in range(2):
                        self.tr(psT.ap[0:LS, g * 128:(g + 1) * 128], vaf.ap[:, g, 128 + sc0:128 + sc0 + LS], ident.ap[:],
                                r=[vaf, ident], w=[psT])
                    self.cp(vtokf.ap[0:LS], psT.ap[0:LS, 0:256].rearrange("p (g d) -> p g d", g=2), r=[psT], w=[vtokf])
                    out_events.append(self.dma("sp", os_v[s, 112:128], vtokf.ap[0:LS], r=[vtokf]))
            else:
                for g in range(2):
                    self.cp(f1.ap[:, 0:128], kaf.ap[:, g, Tm:Tm + 128], r=[kaf], w=[f1])
                    self.cp(kaf.ap[:, g, 0:128], f1.ap[:, 0:128], r=[f1], w=[kaf])
                    self.cp(f1.ap[:, 0:128], vaf.ap[:, g, Tm:Tm + 128], r=[vaf], w=[f1])
                    self.cp(vaf.ap[:, g, 0:128], f1.ap[:, 0:128], r=[f1], w=[vaf])
                self.cp(kab.ap[:, :, 0:128], kaf.ap[:, :, 0:128], r=[kaf], w=[kab])
            xt = [(j * 128, 128, xres.ap[:, j, :], xres) for j in range(nt)]
            gg1 = [(0, Tm, 0, G1)]
            gg2 = [(0, Tm, 0, G2)]
            if smp:
                xt.append((Tm, TSM, xres.ap[0:TSM, nt, :], xres))
                gg1 += [(Tm + s * LS, LS, 1 + s, G1) for s in range(NS)]
                gg2 += [(Tm + s * LS, LS, 1 + s, G2) for s in range(NS)]
            need_conv("w_o")
            proj_residual(w_o_b, NFC, lambda k: mixT.ap[:, k, 0:Tn_], [mixT, mixTg, mixTs], Tn_, gg1, xt, 512, 8)
            for j in range(nt):
                norm_transpose(xres, xres.ap[:, j, :], 128, j * 128, [(0, 128, 0, (t0 + j) == 0)], 2)
            if smp:
                norm_transpose(xres, xres.ap[0:TSM, nt, :], TSM, Tm, grp, 2)
            segs_ffn = [(0, Tm, lambda ch: (uprev.ap[:, ch, :], uprev), lambda ch: (uprev.ap[:, ch, :], uprev))]
            if smp:
                segs_ffn += [(Tm + s * LS, LS, (lambda ch, s=s: (sffn.ap[:, s, ch, :], sffn)),
                              (lambda ch, s=s: (sffn_o.ap[:, s, ch, :], sffn_o))) for s in range(NS)]
            ffn(Tn_, segs_ffn, gg2, xt)
            for j in range(nt):
                out_events.append(self.dma("sp", ym[(t0 + j) * 128:(t0 + j + 1) * 128, :], xres.ap[:, j, :], r=[xres]))
            if smp:
                out_events.append(self.dma("sp", ys, xres.ap[0:TSM, nt, :], r=[xres]))
                for s in range(NS):
                    out_events.append(self.dma("sp", os_conv[s], sconv_o.ap[:, s], r=[sconv_o]))
                    out_events.append(self.dma("sp", os_ffn[s], sffn_o.ap[:, s], r=[sffn_o]))
        out_events.append(self.dma("sp", o_ffn, uprev.ap[:], r=[uprev]))

        S.finish("sp")
        S.emit()
        self.es.close()
        return nc

    def _swa_sample(self, qcol, kcol, vblk, scvb, vtok, qa, kab, psS, psT, psG, psH, sc1, sc2, pT1, pT2, biasT, onesb, esink, den, mixT):
        Cq = LS
        for g in range(2):
            NQ = 4 * Cq
            rhs_q = qa.ap[:, g * 4:(g + 1) * 4, qcol:qcol + Cq]
            blocks = [(kab.ap[:, g, 0:64], 64, scvb.ap[0:64, 0, g, :], kab),
                      (kab.ap[:, g, 64:128], 64, scvb.ap[0:64, 1, g, :], kab),
                      (kab.ap[:, g, kcol:kcol + LS], LS, vtok.ap[0:LS, vblk, g, :], kab)]
            dsts = [(psS, sc1, pT1, 0), (psS, sc1, pT1, 256), (psT, sc2, pT2, 0)]
            for bi, (kap, nk, vap, ktn) in enumerate(blocks):
                pt, sc, pTt, off = dsts[bi]
                self.mm(pt.ap[0:nk, off:off + NQ], kap, rhs_q, r=[ktn, qa], w=[pt])
            for bi, (kap, nk, vap, ktn) in enumerate(blocks):
                pt, sc, pTt, off = dsts[bi]
                bias_ap = biasT.ap[0:nk, g, bi, :, 0:Cq]
                o3 = sc.ap[0:nk, off:off + NQ].rearrange("p (h c) -> p h c", h=4)
                i3 = pt.ap[0:nk, off:off + NQ].rearrange("p (h c) -> p h c", h=4)
                self.tt(o3, i3, bias_ap, ALU.add, r=[pt, biasT], w=[sc])
                self.act(pTt.ap[0:nk, off:off + NQ], sc.ap[0:nk, off:off + NQ], AF.Exp, r=[sc], w=[pTt])
            for bi, (kap, nk, vap, ktn) in enumerate(blocks):
                pt, sc, pTt, off = dsts[bi]
                self.mm(psG.ap[:, 0:NQ], vap, pTt.ap[0:nk, off:off + NQ], start=(bi == 0), stop=(bi == 2),
                        r=[vtok, scvb, pTt], w=[psG], inc=(bi == 2))
            for bi, (kap, nk, vap, ktn) in enumerate(blocks):
                pt, sc, pTt, off = dsts[bi]
                self.mm(psH.ap[:, 0:NQ], onesb.ap[0:nk, :], pTt.ap[0:nk, off:off + NQ], start=(bi == 0), stop=(bi == 2),
                        r=[onesb, pTt], w=[psH], inc=(bi == 2))
            self.tt(den.ap[:, 0:NQ].rearrange("p (h c) -> p h c", h=4), psH.ap[:, 0:NQ].rearrange("p (h c) -> p h c", h=4),
                    esink.ap[:, g * 4:(g + 1) * 4].unsqueeze(2).to_broadcast([128, 4, Cq]), ALU.add, r=[psH, esink], w=[den])
            self.recip(den.ap[:, 0:NQ], den.ap[:, 0:NQ], r=[den], w=[den])
            self.tt(mixT.ap[:, 8 + g * 4:8 + (g + 1) * 4, qcol:qcol + Cq], psG.ap[:, 0:NQ].rearrange("p (h c) -> p h c", h=4),
                    den.ap[:, 0:NQ].rearrange("p (h c) -> p h c", h=4), ALU.mult, r=[psG, den], w=[mixT])


def _consts(NM, half):
    idx = np.arange(128)
    c = {}
    c["ident"] = np.eye(128, dtype=np.float32)
    c["U"] = (idx[:, None] <= idx[None, :]).astype(np.float32)
    c["Lst"] = (idx[:, None] > idx[None, :]).astype(np.float32)
    c["Linc"] = (idx[:, None] >= idx[None, :]).astype(np.float32)
    slopes = 2.0 ** (-8.0 * np.arange(1, 9, dtype=np.float64) / 8.0)
    p = np.arange(64)[:, None, None, None, None]
    g = np.arange(2)[None, :, None, None, None]
    r = np.arange(3)[None, None, :, None, None]
    hh = np.arange(4)[None, None, None, :, None]
    i = np.arange(64)[None, None, None, None, :]
    dist = np.abs(128 + i - (64 * r + p)).astype(np.float64)
    c["biasT"] = (-(slopes[(g * 4 + hh)]) * dist).astype(np.float32)
    mk = np.zeros((64, 2 * NM * 3), np.float32)
    if half == 0:
        for m in range(2 * NM):
            for r_ in range(3):
                if 64 * (m + r_) - 128 < 128:
                    mk[:, m * 3 + r_] = -30000.0
    c["maskc"] = mk
    return c


def _fm(v, n):
    return np.ascontiguousarray(np.asarray(v, np.float32).reshape(n, 128).T)


_CACHE = {}


def run_cores(inputs, SEQ, TS=3, n_cores=8):
    f32 = np.float32
    half_len = SEQ // 2
    NM = (half_len + 128) // 128
    NP = (half_len - 128) // 128
    key = (NP, NM, TS)
    if key not in _CACHE:
        _CACHE[key] = Builder(NP, NM, TS).build()
    nc = _CACHE[key]
    x_prompt = np.asarray(inputs["x_prompt"], f32)
    x_sample = np.asarray(inputs["x_sample"], f32)
    shared = {
        "ada_w": np.ascontiguousarray(np.asarray(inputs["ada_w"], f32)[0]),
        "ada_b": _fm(inputs["ada_b"][0], 96),
        "norm1_w": _fm(inputs["norm1_w"][0], 16), "norm2_w": _fm(inputs["norm2_w"][0], 16),
        "w_in": np.ascontiguousarray(np.asarray(inputs["w_in"], f32)[0]),
        "conv_w": np.ascontiguousarray(np.asarray(inputs["conv_qkv_w"], f32)[0].reshape(4, 24, 128).transpose(2, 1, 0)),
        "a_log": np.asarray(inputs["a_log"], f32)[0], "dt_bias": np.asarray(inputs["dt_bias"], f32)[0],
        "gdn_norm_w": np.asarray(inputs["gdn_norm_w"], f32)[0].reshape(128, 1),
        "q_norm_w": np.asarray(inputs["q_norm_w"], f32)[0].reshape(128, 1),
        "k_norm_w": np.asarray(inputs["k_norm_w"], f32)[0].reshape(128, 1),
        "sinks": np.asarray(inputs["sinks"], f32)[0],
        "w_o": np.ascontiguousarray(np.asarray(inputs["w_o"], f32)[0]),
        "w_up": np.ascontiguousarray(np.asarray(inputs["w_up"], f32)[0]),
        "ffn_conv_w": np.ascontiguousarray(np.asarray(inputs["ffn_conv_w"], f32)[0].reshape(3, 88, 128).transpose(2, 1, 0)),
        "ffn_conv_b": _fm(inputs["ffn_conv_b"][0], 88),
        "w_down": np.ascontiguousarray(np.asarray(inputs["w_down"], f32)[0]),
    }
    sc = np.asarray(inputs["state_conv_qkv"], f32)[0]
    sd = np.asarray(inputs["state_delta"], f32)[0]
    sk = np.asarray(inputs["cache_swa_k"], f32)[0]
    sv = np.asarray(inputs["cache_swa_v"], f32)[0]
    sf = np.asarray(inputs["state_ffn_conv"], f32)[0]
    cp = np.asarray(inputs["c_prompt"], f32)
    cs = np.asarray(inputs["c_sample"], f32)
    in_maps = []
    for c in range(n_cores):
        b, half = c // 2, c % 2
        start = half * half_len
        m = dict(shared)
        m.update(_consts(NM, half))
        xm = np.zeros((NM * 128, D), f32)
        xp = np.zeros((max(NP, 1) * 128, D), f32)
        if half == 0:
            xm[128:] = x_prompt[b, 0:half_len]
        else:
            xm[:] = x_prompt[b, start - 128:start + half_len]
            xp[:NP * 128] = x_prompt[b, 0:start - 128]
        m["xm"], m["xp"] = xm, xp
        ss = slice(c * NS, (c + 1) * NS)
        m["xs"] = np.ascontiguousarray(x_sample[ss].reshape(NS * LS, D))
        m["flag"] = np.full((128, 1), float(half), f32)
        cc = np.concatenate([cp[b:b + 1], cs[ss]], axis=0)
        m["cT"] = np.ascontiguousarray(cc.reshape(5, 16, 128).transpose(2, 1, 0))
        m["s_conv"] = np.ascontiguousarray(sc[ss].reshape(NS, 3, 24, 128).transpose(0, 3, 2, 1))
        m["s_delta"] = np.ascontiguousarray(sd[ss])
        m["s_kT"] = np.ascontiguousarray(sk[ss].transpose(0, 3, 2, 1))
        m["s_v"] = np.ascontiguousarray(sv[ss])
        m["s_ffn"] = np.ascontiguousarray(sf[ss].reshape(NS, 2, 88, 128).transpose(0, 3, 2, 1))
        in_maps.append(m)
    res = run_bass_kernel_spmd(nc, in_maps, core_ids=list(range(n_cores)))
    R = res.results
    B = n_cores // 2
    y_p = np.zeros((B, SEQ, D), f32)
    for c in range(n_cores):
        b, half = c // 2, c % 2
        y_p[b, half * half_len:(half + 1) * half_len] = R[c]["ym"][128:]
    y_s = np.concatenate([R[c]["ys"].reshape(NS, LS, D) for c in range(n_cores)], axis=0)
    odd = [R[c] for c in range(1, n_cores, 2)]
    p_conv = np.stack([r["o_conv"].transpose(2, 1, 0).reshape(3, 3072) for r in odd])[None]
    p_delta = np.stack([r["o_delta"] for r in odd])[None]
    p_k = np.stack([r["o_kT"].transpose(2, 1, 0) for r in odd])[None]
    p_v = np.stack([r["o_v"] for r in odd])[None]
    p_ffn = np.stack([r["o_ffn"].transpose(2, 1, 0).reshape(2, 2 * DFF) for r in odd])[None]
    s_conv = np.concatenate([r["os_conv"].transpose(0, 3, 2, 1).reshape(NS, 3, 3072) for r in R])[None]
    s_delta = np.concatenate([r["os_delta"] for r in R])[None]
    s_k = np.concatenate([r["os_kT"].transpose(0, 3, 2, 1) for r in R])[None]
    s_v = np.concatenate([r["os_v"] for r in R])[None]
    s_ffn = np.concatenate([r["os_ffn"].transpose(0, 3, 2, 1).reshape(NS, 2, 2 * DFF) for r in R])[None]
    outs = (y_p, y_s, p_conv, p_delta, p_k, p_v, p_ffn, s_conv, s_delta, s_k, s_v, s_ffn)
    return tuple(np.ascontiguousarray(o, dtype=f32) for o in outs)


def kernel(**inputs):
    return run_cores(inputs, SEQ=4096, TS=3, n_cores=8)
```
